# Optimizing a Trainium2 kernel written in Bass

```python
import math
import jax, jax.numpy as jnp
from jax import lax
import numpy as np

D_MODEL = 1024
BATCH = 2
SEQ = 16384
DEPTH = 4

N_META = 16
CHUNK = 64
CONV_K = 4
N_BRANCH = 4
BRANCH_W = D_MODEL // 2

GDN_DK = 128
GDN_DV = 128
GDN_HEADS = BRANCH_W // GDN_DV

M2_HEADDIM = 64
M2_HEADS = BRANCH_W // M2_HEADDIM
M2_GROUPS = 2
M2_HPG = M2_HEADS // M2_GROUPS
M2_DSTATE = 128

HG_DK = 128
HG_HEADS = BRANCH_W // HG_DK
HG_DV = BRANCH_W // HG_HEADS

S5_GROUP = 16
S5_NG = BRANCH_W // S5_GROUP
S5_P = 64

ALPHA = (2 * DEPTH) ** 0.25
BETA_INIT = (8 * DEPTH) ** -0.25
LN_EPS = 1e-5
RMS_EPS = 1e-6

IN_SIZES = (
    3 * BRANCH_W,
    BRANCH_W,
    GDN_HEADS,
    GDN_HEADS,
    BRANCH_W + 2 * M2_GROUPS * M2_DSTATE,
    BRANCH_W,
    M2_HEADS,
    BRANCH_W,
    BRANCH_W,
    BRANCH_W,
    BRANCH_W,
    BRANCH_W,
    BRANCH_W,
    N_BRANCH * D_MODEL,
)
N_IN = sum(IN_SIZES)
IN_OFFSETS = tuple(sum(IN_SIZES[:i + 1]) for i in range(len(IN_SIZES) - 1))

kernel_name = 'hybrid_gdn_ssd_hgrn2_s5_parallel_gated'


def layer_norm(x, g, b):
    xf = x.astype(jnp.float32)
    mu = jnp.mean(xf, axis=-1, keepdims=True)
    var = jnp.mean(jnp.square(xf - mu), axis=-1, keepdims=True)
    return ((xf - mu) * lax.rsqrt(var + LN_EPS) * g + b).astype(x.dtype)


def rms_norm(t):
    return t * lax.rsqrt(jnp.mean(jnp.square(t), axis=-1, keepdims=True) + RMS_EPS)


def l2_norm(t):
    return t * lax.rsqrt(jnp.sum(jnp.square(t), axis=-1, keepdims=True) + RMS_EPS)


def heads(t, n):
    return t.reshape(t.shape[:-1] + (n, t.shape[-1] // n))


def causal_conv(x, w):
    L = x.shape[1]
    xp = jnp.pad(x, ((0, 0), (CONV_K - 1, 0), (0, 0)))
    out = xp[:, CONV_K - 1:] * w[CONV_K - 1]
    for j in range(CONV_K - 1):
        out = out + xp[:, j:j + L] * w[j]
    return out


def front_pad_chunks(t, pad):
    t = jnp.pad(t, ((0, 0), (pad, 0)) + ((0, 0),) * (t.ndim - 2))
    return t.reshape((t.shape[0], -1, CHUNK) + t.shape[2:])


def causal_mask(strict=False):
    i = jnp.arange(CHUNK)
    return (i[:, None] > i[None, :]) if strict else (i[:, None] >= i[None, :])


def gated_delta_rule(q, k, v, g, beta):
    L = q.shape[1]
    pad = (-L) % CHUNK
    prep = lambda t: jnp.moveaxis(front_pad_chunks(t, pad), 3, 1)
    q, k, v, g, beta = prep(q), prep(k), prep(v), prep(g), prep(beta)
    gc = jnp.cumsum(g, axis=-1)
    decay = jnp.exp(jnp.where(causal_mask(), gc[..., :, None] - gc[..., None, :], -jnp.inf))
    kb = k * beta[..., None]
    a = jnp.where(causal_mask(True), jnp.einsum('bhnid,bhnjd->bhnij', kb, k) * decay, 0.0)
    system = a + jnp.eye(CHUNK, dtype=a.dtype)
    u = lax.linalg.triangular_solve(system, v * beta[..., None], left_side=True, lower=True, unit_diagonal=True)
    w = lax.linalg.triangular_solve(system, kb * jnp.exp(gc)[..., None], left_side=True, lower=True, unit_diagonal=True)
    qk = jnp.einsum('bhnid,bhnjd->bhnij', q, k) * decay
    q_dec = q * jnp.exp(gc)[..., None]
    k_dec = k * jnp.exp(gc[..., -1:] - gc)[..., None]
    g_tot = jnp.exp(gc[..., -1])

    def step(S, inp):
        u_n, w_n, qk_n, qd_n, kd_n, gt_n = inp
        v_new = u_n - jnp.einsum('bhck,bhkv->bhcv', w_n, S)
        o_n = jnp.einsum('bhck,bhkv->bhcv', qd_n, S) + jnp.einsum('bhij,bhjv->bhiv', qk_n, v_new)
        S = S * gt_n[..., None, None] + jnp.einsum('bhck,bhcv->bhkv', kd_n, v_new)
        return S, o_n

    xs = tuple(jnp.moveaxis(t, 2, 0) for t in (u, w, qk, q_dec, k_dec, g_tot))
    S0 = jnp.zeros(q.shape[:2] + (q.shape[-1], v.shape[-1]), jnp.float32)
    _, o = lax.scan(step, S0, xs)
    o = jnp.moveaxis(o, 0, 2)
    o = jnp.moveaxis(o, 1, 3).reshape(o.shape[0], -1, o.shape[1], o.shape[-1])
    return o[:, pad:]


def ssd_chunked(xdt, a, Bm, Cm):
    L = xdt.shape[1]
    pad = (-L) % CHUNK
    X = front_pad_chunks(xdt, pad)
    Bsz, Nc = X.shape[:2]
    X = X.reshape(Bsz, Nc, CHUNK, M2_GROUPS, M2_HPG, M2_HEADDIM)
    Bc = front_pad_chunks(Bm, pad)
    Cc = front_pad_chunks(Cm, pad)
    a_cum = jnp.cumsum(front_pad_chunks(a, pad).reshape(Bsz, Nc, CHUNK, M2_GROUPS, M2_HPG), axis=2)
    diff = a_cum[:, :, :, None] - a_cum[:, :, None, :]
    Lmat = jnp.exp(jnp.where(causal_mask()[:, :, None, None], diff, -jnp.inf))
    CB = jnp.einsum('bnlgs,bnmgs->bnlmg', Cc, Bc)
    y_diag = jnp.einsum('bnlmgk,bnmgkp->bnlgkp', CB[..., None] * Lmat, X)
    decay_states = jnp.exp(a_cum[:, :, -1:] - a_cum)
    states = jnp.einsum('bnlgs,bnlgkp->bngkps', Bc, X * decay_states[..., None])
    chunk_decay = jnp.exp(a_cum[:, :, -1])

    def step(S, inp):
        st, dc = inp
        return S * dc[..., None, None] + st, S

    S0 = jnp.zeros((Bsz, M2_GROUPS, M2_HPG, M2_HEADDIM, M2_DSTATE), jnp.float32)
    _, S_start = lax.scan(step, S0, (jnp.moveaxis(states, 1, 0), jnp.moveaxis(chunk_decay, 1, 0)))
    S_start = jnp.moveaxis(S_start, 0, 1)
    y_off = jnp.einsum('bnlgs,bngkps->bnlgkp', Cc, S_start) * jnp.exp(a_cum)[..., None]
    y = (y_diag + y_off).reshape(Bsz, Nc * CHUNK, M2_HEADS, M2_HEADDIM)
    return y[:, pad:]


def hgrn2_chunked(q, log_f, k, i):
    L = q.shape[1]
    pad = (-L) % CHUNK
    prep = lambda t: jnp.moveaxis(front_pad_chunks(t, pad), 3, 1)
    q, log_f, k, i = prep(q), prep(log_f), prep(k), prep(i)
    Bsz, H, Nc = q.shape[:3]

    def intra(S, inp):
        q_t, f_t, k_t, i_t = inp
        S = S * f_t[..., None] + k_t[..., None] * i_t[..., None, :]
        return S, jnp.einsum('bhnk,bhnkv->bhnv', q_t, S)

    xs = tuple(jnp.moveaxis(t, 3, 0) for t in (q, jnp.exp(log_f), k, i))
    S_loc, o_intra = lax.scan(intra, jnp.zeros((Bsz, H, Nc, HG_DK, HG_DV), jnp.float32), xs)
    o_intra = jnp.moveaxis(o_intra, 0, 3)
    g_cum = jnp.cumsum(log_f, axis=3)
    chunk_decay = jnp.exp(g_cum[:, :, :, -1])

    def inter(S, inp):
        s_loc, dc = inp
        return S * dc[..., None] + s_loc, S

    _, S_start = lax.scan(inter, jnp.zeros((Bsz, H, HG_DK, HG_DV), jnp.float32),
                          (jnp.moveaxis(S_loc, 2, 0), jnp.moveaxis(chunk_decay, 2, 0)))
    S_start = jnp.moveaxis(S_start, 0, 2)
    o = o_intra + jnp.einsum('bhnck,bhnkv->bhncv', q * jnp.exp(g_cum), S_start)
    o = jnp.moveaxis(o, 1, 3).reshape(Bsz, Nc * CHUNK, H, HG_DV)
    return o[:, pad:]


def s5_ssm(u, A_re, A_im, B_re, B_im, C_re, C_im, D, log_dt):
    f = lambda t: t.astype(jnp.float32)
    A_re, A_im, B_re, B_im, C_re, C_im, D = map(f, (A_re, A_im, B_re, B_im, C_re, C_im, D))
    Bsz, L, _ = u.shape
    ug = u.reshape(Bsz, L, S5_NG, S5_GROUP)
    dt = jnp.exp(f(log_dt))[:, None]
    mag = jnp.exp(A_re * dt)
    lam_re, lam_im = mag * jnp.cos(A_im * dt), mag * jnp.sin(A_im * dt)
    den = jnp.square(A_re) + jnp.square(A_im)
    nr, ni = lam_re - 1.0, lam_im
    z_re, z_im = (nr * A_re + ni * A_im) / den, (ni * A_re - nr * A_im) / den
    Bb_re = z_re[..., None] * B_re - z_im[..., None] * B_im
    Bb_im = z_re[..., None] * B_im + z_im[..., None] * B_re
    bu_re = jnp.einsum('gpc,blgc->lbgp', Bb_re, ug)
    bu_im = jnp.einsum('gpc,blgc->lbgp', Bb_im, ug)
    a_re = jnp.broadcast_to(lam_re, (L, 1, S5_NG, S5_P))
    a_im = jnp.broadcast_to(lam_im, (L, 1, S5_NG, S5_P))

    def combine(e1, e2):
        a1r, a1i, b1r, b1i = e1
        a2r, a2i, b2r, b2i = e2
        return (a1r * a2r - a1i * a2i, a1r * a2i + a1i * a2r,
                a2r * b1r - a2i * b1i + b2r, a2r * b1i + a2i * b1r + b2i)

    _, _, x_re, x_im = lax.associative_scan(combine, (a_re, a_im, bu_re, bu_im), axis=0)
    y = jnp.einsum('gcp,lbgp->blgc', C_re, x_re) - jnp.einsum('gcp,lbgp->blgc', C_im, x_im)
    return y.reshape(Bsz, L, BRANCH_W) + D * u


def hybrid_layer(h, w_in, gdn_conv_w, gdn_A_log, gdn_dt_bias, gdn_norm_w,
                 m2_conv_w, m2_conv_b, m2_dt_bias, m2_A_log, m2_D, m2_norm_w,
                 hg_lb, hg_norm_w,
                 s5_A_re, s5_A_im, s5_B_re, s5_B_im, s5_C_re, s5_C_im, s5_D, s5_log_dt,
                 s5_glu_w1, s5_glu_w2, w_branch, w_out, ln_g, ln_b):
    f32 = lambda t: t.astype(jnp.float32)
    Bsz, L, _ = h.shape
    proj = h @ w_in
    (gdn_qkv, gdn_z, gdn_b, gdn_a, m2_xbc, m2_z, m2_dt,
     hg_q, hg_f, hg_i, hg_z, s5_u, s5_z, gates) = jnp.split(proj, IN_OFFSETS, axis=-1)

    qkv = jax.nn.silu(causal_conv(f32(gdn_qkv), f32(gdn_conv_w)))
    q, k, v = jnp.split(qkv, 3, axis=-1)
    q = l2_norm(heads(q, GDN_HEADS)) * GDN_DK ** -0.5
    k = l2_norm(heads(k, GDN_HEADS))
    v = heads(v, GDN_HEADS)
    beta = jax.nn.sigmoid(f32(gdn_b))
    g = -jnp.exp(f32(gdn_A_log)) * jax.nn.softplus(f32(gdn_a) + f32(gdn_dt_bias))
    o_a = gated_delta_rule(q, k, v, g, beta)
    y_a = (rms_norm(o_a) * f32(gdn_norm_w)).reshape(Bsz, L, BRANCH_W) * jax.nn.silu(f32(gdn_z))

    xbc = jax.nn.silu(causal_conv(f32(m2_xbc), f32(m2_conv_w)) + f32(m2_conv_b))
    xs, Bm, Cm = jnp.split(xbc, [BRANCH_W, BRANCH_W + M2_GROUPS * M2_DSTATE], axis=-1)
    xs = heads(xs, M2_HEADS)
    Bm, Cm = heads(Bm, M2_GROUPS), heads(Cm, M2_GROUPS)
    dt = jax.nn.softplus(f32(m2_dt) + f32(m2_dt_bias))
    y_ssd = ssd_chunked(xs * dt[..., None], dt * (-jnp.exp(f32(m2_A_log))), Bm, Cm)
    y_ssd = (y_ssd + f32(m2_D)[:, None] * xs).reshape(Bsz, L, BRANCH_W)
    y_b = rms_norm(heads(y_ssd * jax.nn.silu(f32(m2_z)), M2_GROUPS)).reshape(Bsz, L, BRANCH_W) * f32(m2_norm_w)

    zf = f32(hg_f)
    log_f = jnp.logaddexp(jnp.log(hg_lb), jnp.log1p(-hg_lb) + jax.nn.log_sigmoid(zf))
    k_c = (1.0 - hg_lb) * jax.nn.sigmoid(-zf)
    o_c = hgrn2_chunked(heads(jax.nn.silu(f32(hg_q)), HG_HEADS), heads(log_f, HG_HEADS),
                        heads(k_c, HG_HEADS), heads(f32(hg_i), HG_HEADS))
    y_c = (rms_norm(o_c) * f32(hg_norm_w)).reshape(Bsz, L, BRANCH_W) * jax.nn.silu(f32(hg_z))

    y_s5 = jax.nn.gelu(s5_ssm(f32(s5_u), s5_A_re, s5_A_im, s5_B_re, s5_B_im,
                              s5_C_re, s5_C_im, s5_D, s5_log_dt))
    y_d = (y_s5 @ s5_glu_w1) * jax.nn.sigmoid(y_s5 @ s5_glu_w2) * jax.nn.silu(f32(s5_z))

    branches = jnp.stack([y_a, y_b, y_c, y_d], axis=2)
    branch_out = jnp.einsum('blkw,kwd->blkd', branches, w_branch)
    gate = jax.nn.sigmoid(f32(gates).reshape(Bsz, L, N_BRANCH, D_MODEL))
    mixed = jnp.sum(gate * branch_out, axis=2)
    out = mixed @ w_out
    return layer_norm(ALPHA * f32(h) + out, ln_g, ln_b).astype(h.dtype)


def _dt_bias(key, shape):
    dt = jnp.exp(jax.random.uniform(key, shape, jnp.float32, math.log(1e-3), math.log(1e-1)))
    return dt + jnp.log(-jnp.expm1(-dt))


def setup_inputs(seed: int = 0) -> dict:
    key = jax.random.key(seed)
    ks = iter(jax.random.split(key, 48))
    nrm = lambda shape, s: jax.random.normal(next(ks), shape, jnp.float32) * s
    s5_n = jnp.arange(S5_P, dtype=jnp.float32)
    return {
        'x': nrm((BATCH, SEQ, D_MODEL), 1.0),
        'meta_tokens': nrm((N_META, D_MODEL), 1.0),
        'ln_in_g': 1.0 + nrm((D_MODEL,), 0.02),
        'ln_in_b': nrm((D_MODEL,), 0.02),
        'w_in': nrm((DEPTH, D_MODEL, N_IN), D_MODEL ** -0.5),
        'gdn_conv_w': nrm((DEPTH, CONV_K, 3 * BRANCH_W), CONV_K ** -0.5),
        'gdn_A_log': jnp.log(jax.random.uniform(next(ks), (DEPTH, GDN_HEADS), jnp.float32, 1.0, 16.0)),
        'gdn_dt_bias': _dt_bias(next(ks), (DEPTH, GDN_HEADS)),
        'gdn_norm_w': 1.0 + nrm((DEPTH, GDN_DV), 0.02),
        'm2_conv_w': nrm((DEPTH, CONV_K, BRANCH_W + 2 * M2_GROUPS * M2_DSTATE), CONV_K ** -0.5),
        'm2_conv_b': nrm((DEPTH, BRANCH_W + 2 * M2_GROUPS * M2_DSTATE), 0.02),
        'm2_dt_bias': _dt_bias(next(ks), (DEPTH, M2_HEADS)),
        'm2_A_log': jnp.log(jax.random.uniform(next(ks), (DEPTH, M2_HEADS), jnp.float32, 1.0, 16.0)),
        'm2_D': 1.0 + nrm((DEPTH, M2_HEADS), 0.1),
        'm2_norm_w': 1.0 + nrm((DEPTH, BRANCH_W), 0.02),
        'hg_lb_logits': nrm((DEPTH, BRANCH_W), 0.1),
        'hg_norm_w': 1.0 + nrm((DEPTH, HG_DV), 0.02),
        's5_A_re': -0.5 + nrm((DEPTH, S5_NG, S5_P), 0.01),
        's5_A_im': math.pi * s5_n + nrm((DEPTH, S5_NG, S5_P), 0.01),
        's5_B_re': nrm((DEPTH, S5_NG, S5_P, S5_GROUP), (2 * S5_GROUP) ** -0.5),
        's5_B_im': nrm((DEPTH, S5_NG, S5_P, S5_GROUP), (2 * S5_GROUP) ** -0.5),
        's5_C_re': nrm((DEPTH, S5_NG, S5_GROUP, S5_P), S5_P ** -0.5),
        's5_C_im': nrm((DEPTH, S5_NG, S5_GROUP, S5_P), S5_P ** -0.5),
        's5_D': nrm((DEPTH, BRANCH_W), 1.0),
        's5_log_dt': jax.random.uniform(next(ks), (DEPTH, S5_NG), jnp.float32, math.log(1e-3), math.log(1e-1)),
        's5_glu_w1': nrm((DEPTH, BRANCH_W, BRANCH_W), BRANCH_W ** -0.5),
        's5_glu_w2': nrm((DEPTH, BRANCH_W, BRANCH_W), BRANCH_W ** -0.5),
        'w_branch': nrm((DEPTH, N_BRANCH, BRANCH_W, D_MODEL), BRANCH_W ** -0.5 * BETA_INIT),
        'w_out': nrm((DEPTH, D_MODEL, D_MODEL), D_MODEL ** -0.5 * BETA_INIT),
        'ln_g': 1.0 + nrm((DEPTH, D_MODEL), 0.02),
        'ln_b': nrm((DEPTH, D_MODEL), 0.02),
    }


def reference(x, meta_tokens, ln_in_g, ln_in_b, w_in, gdn_conv_w, gdn_A_log, gdn_dt_bias, gdn_norm_w,
              m2_conv_w, m2_conv_b, m2_dt_bias, m2_A_log, m2_D, m2_norm_w,
              hg_lb_logits, hg_norm_w,
              s5_A_re, s5_A_im, s5_B_re, s5_B_im, s5_C_re, s5_C_im, s5_D, s5_log_dt,
              s5_glu_w1, s5_glu_w2, w_branch, w_out, ln_g, ln_b):
    Bsz = x.shape[0]
    meta = jnp.broadcast_to(meta_tokens.astype(x.dtype)[None], (Bsz, N_META, D_MODEL))
    h = layer_norm(jnp.concatenate([meta, x], axis=1), ln_in_g, ln_in_b)
    cum = jnp.cumsum(jax.nn.softmax(hg_lb_logits.astype(jnp.float32), axis=0), axis=0)
    lower_bounds = cum - cum[0:1]
    for l in range(DEPTH):
        h = hybrid_layer(h, w_in[l], gdn_conv_w[l], gdn_A_log[l], gdn_dt_bias[l], gdn_norm_w[l],
                         m2_conv_w[l], m2_conv_b[l], m2_dt_bias[l], m2_A_log[l], m2_D[l], m2_norm_w[l],
                         lower_bounds[l], hg_norm_w[l],
                         s5_A_re[l], s5_A_im[l], s5_B_re[l], s5_B_im[l], s5_C_re[l], s5_C_im[l],
                         s5_D[l], s5_log_dt[l], s5_glu_w1[l], s5_glu_w2[l],
                         w_branch[l], w_out[l], ln_g[l], ln_b[l])
    return h[:, N_META:]
```

```python
import math
import os
import numpy as np
from contextlib import ExitStack
import concourse.bass as bass
import concourse.mybir as mybir
from concourse.bass_utils import run_bass_kernel_spmd

F32 = mybir.dt.float32
AF = mybir.ActivationFunctionType
ALU = mybir.AluOpType
AX = mybir.AxisListType

D = 1024
TT = 256
NB = TT // 128
NCH = TT // 64
KT = 8
NMETA = 16
BIG = 30000.0
ALPHA = 8.0 ** 0.25
LN_EPS = 1e-5
RMS_EPS = 1e-6
N_IN = 10768
S5S = 128

O_GQKV, O_GZ, O_GB, O_GA, O_MX, O_MZ, O_MDT = 0, 1536, 2048, 2052, 2056, 3080, 3592
O_HQ, O_HF, O_HI, O_HZ, O_SU, O_SZ, O_GATE = 3600, 4112, 4624, 5136, 5648, 6160, 6672

PP_CWG, PP_CWM, PP_CBM, PP_GNW, PP_MD, PP_MNW, PP_HNW, PP_SD, PP_SAR, PP_SAI, PP_SLDT = \
    0, 48, 80, 88, 89, 93, 97, 98, 102, 118, 134
NPP = 150
PR_A12, PR_B12, PR_LNG, PR_LNB = 0, 12, 24, 24 + 1024
NPR = 24 + 2048
CH = 128
PHASES = os.environ.get('KPHASES', 'gdn,ssd,hg,s5').split(',')


class V:
    __slots__ = ('ap', 'keys')

    def __init__(self, ap, keys):
        self.ap = ap
        self.keys = keys


class Buf:
    def __init__(self, h, name, ch=CH):
        self.h = h
        self.name = name
        self.ch = ch

    def keys(self, off, n):
        return [(self.name, c) for c in range(off // self.ch, (off + n - 1) // self.ch + 1)]

    def r(self, off, n, p0=0, p1=128):
        return V(self.h[p0:p1, off:off + n], self.keys(off, n))

    def r3(self, off, a, s, p0=0, p1=128):
        return V(self.h[p0:p1, off:off + a * s].rearrange("p (a s) -> p a s", s=s), self.keys(off, a * s))

    def rs(self, off, a, stride, lo, n, p0=0, p1=128):
        return V(self.h[p0:p1, off:off + a * stride].rearrange("p (a s) -> p a s", s=stride)[:, :, lo:lo + n],
                 self.keys(off, a * stride))


def bcast(v, shape):
    return V(v.ap.to_broadcast(shape), v.keys)


def usq(v, axis):
    return V(v.ap.unsqueeze(axis), v.keys)


class KB:
    ND = 8

    def __init__(self, nc, es):
        self.nc = nc
        self.E = {'pe': nc.tensor, 'dve': nc.vector, 'act': nc.scalar, 'pool': nc.gpsimd, 'sp': nc.sync}
        self.sem = {k: es.enter_context(nc.semaphore("s_" + k)) for k in self.E}
        self.cnt = {k: 0 for k in self.E}
        self.seen = {k: {} for k in self.E}
        self.dsem = [es.enter_context(nc.semaphore("d%d" % i)) for i in range(2 * self.ND)]
        self.dcnt = [0] * (2 * self.ND)
        self.dnext = {'sp': 0, 'pool': 0, 'act': 0}
        self.last_w = {}
        self.readers = {}
        self.nwait = 0
        self.ninst = 0
        self.per_eng = {k: 0 for k in self.E}
        self.base = {}
        self.loop = None
        self.dry = False
        self.tmpreg = {k: self.E[k].alloc_register("kbtmp_" + k) for k in self.E}

    def _wait(self, eng, tok):
        kind, a, v = tok
        if kind == 'c':
            if a == eng and eng == 'pe':
                return
            key = a
            sem = self.sem[a]
        else:
            key = ('d', a)
            sem = self.dsem[a]
        if self.seen[eng].get(key, 0) >= v:
            return
        if not self.dry:
            b = self.base.get(key, 0)
            nk = 0 if self.loop is None else self.loop[1].get(key, 0)
            if nk == 0:
                self.E[eng].wait_ge(sem, b + v)
            else:
                ti, n, start = self.loop
                r = self.tmpreg[eng]
                self.E[eng].reg_mul(r, ti, nk)
                self.E[eng].reg_add(r, r, b - start * nk + v)
                self.E[eng].wait_ge(sem, r)
            self.nwait += 1
        self.seen[eng][key] = v

    def _absval(self, key, v):
        b = self.base.get(key, 0)
        if self.loop is None:
            return b + v
        ti, n, start = self.loop
        nk = n.get(key, 0)
        if nk == 0:
            return b + v
        return ti * nk + (b - start * nk + v)

    def _deps(self, eng, reads, writes):
        for r in reads:
            t = self.last_w.get(r)
            if t is not None:
                self._wait(eng, t)
        for w in writes:
            t = self.last_w.get(w)
            if t is not None:
                self._wait(eng, t)
            for t in self.readers.get(w, ()):
                self._wait(eng, t)

    def _commit(self, tok, reads, writes):
        ws = set(writes)
        for w in writes:
            self.last_w[w] = tok
            self.readers[w] = []
        for r in reads:
            if r in ws:
                continue
            lst = self.readers.setdefault(r, [])
            lst.append(tok)
            if len(lst) > 8:
                d = {}
                for t in lst:
                    d[(t[0], t[1])] = t
                self.readers[r] = list(d.values())

    def op(self, eng, fn, reads=(), writes=()):
        self._deps(eng, reads, writes)
        self.cnt[eng] += 1
        if not self.dry:
            ins = fn(self.E[eng])
            ins.then_inc(self.sem[eng], 1)
            self.ninst += 1
            self.per_eng[eng] += 1
        self._commit(('c', eng, self.cnt[eng]), reads, writes)

    def dma(self, q, out, in_, reads=(), writes=()):
        self._deps(q, reads, writes)
        i = self.dnext[q] % self.ND + (self.ND if q == 'pool' else 0)
        self.dnext[q] += 1
        if self.dcnt[i] > 0:
            self._wait(q, ('d', i, self.dcnt[i]))
        if not self.dry:
            self.E[q].dma_start(out=out, in_=in_).then_inc(self.dsem[i], 16)
            self.ninst += 1
        self.dcnt[i] += 16
        self._commit(('d', i, self.dcnt[i]), reads, writes)

    def reset(self):
        for k in self.E:
            self.cnt[k] = 0
            self.seen[k] = {}
        for i in range(len(self.dcnt)):
            self.dcnt[i] = 0
        for q in self.dnext:
            self.dnext[q] = 0
        self.last_w = {}
        self.readers = {}

    def counts(self):
        n = {k: self.cnt[k] for k in self.E}
        for i in range(len(self.dcnt)):
            n[('d', i)] = self.dcnt[i]
        return n

    def barrier(self):
        self.finish('sp')
        self.cnt['sp'] += 1
        if not self.dry:
            self.E['sp'].sem_inc(self.sem['sp'], 1)
            self.ninst += 1
        for e in self.E:
            if e != 'sp':
                self._wait(e, ('c', 'sp', self.cnt['sp']))
        n = self.counts()
        if self.loop is None and not self.dry:
            for k, v in n.items():
                self.base[k] = self.base.get(k, 0) + v
        self.reset()
        return n

    def enter_loop(self, ti, n, start):
        self.loop = (ti, n, start)

    def exit_loop(self, niter):
        ti, n, start = self.loop
        self.loop = None
        for k, v in n.items():
            self.base[k] = self.base.get(k, 0) + niter * v

    def finish(self, q='sp'):
        for i in range(2 * self.ND):
            if self.dcnt[i] > 0:
                self._wait(q, ('d', i, self.dcnt[i]))
        for e in self.E:
            if e != q and self.cnt[e] > 0:
                self._wait(q, ('c', e, self.cnt[e]))


class G:
    def __init__(self, nc, es):
        self.nc = nc
        self.es = es
        self.kb = KB(nc, es)
        self.ident = None

    def sb(self, name, ncols, dt=F32):
        return Buf(self.es.enter_context(self.nc.sbuf_tensor(name, [128, ncols], dt)), name)

    def ps(self, name, ncols, dt=F32):
        return Buf(self.es.enter_context(self.nc.psum_tensor(name, [128, ncols], dt)), name, ch=512)

    def mm(self, out, lhsT, rhs, start=True, stop=True):
        self.kb.op('pe', lambda e: e.matmul(out.ap, lhsT=lhsT.ap, rhs=rhs.ap, start=start, stop=stop),
                   reads=lhsT.keys + rhs.keys, writes=out.keys)

    def tr(self, out, in_, n=128):
        idn = self.ident.r(0, n, 0, n)
        self.kb.op('pe', lambda e: e.transpose(out.ap, in_.ap, idn.ap),
                   reads=in_.keys + idn.keys, writes=out.keys)

    def act(self, out, in_, func, bias=0.0, scale=1.0):
        rd = list(in_.keys)
        b = bias
        if isinstance(bias, V):
            rd += bias.keys
            b = bias.ap
        self.kb.op('act', lambda e: e.activation(out=out.ap, in_=in_.ap, func=func, bias=b, scale=scale),
                   reads=rd, writes=out.keys)

    def tt(self, eng, out, a, b, op):
        self.kb.op(eng, lambda e: e.tensor_tensor(out=out.ap, in0=a.ap, in1=b.ap, op=op),
                   reads=a.keys + b.keys, writes=out.keys)

    def ts(self, eng, out, a, s1, op0, s2=None, op1=None):
        rd = list(a.keys)
        x1, x2 = s1, s2
        if isinstance(s1, V):
            rd += s1.keys
            x1 = s1.ap
        if isinstance(s2, V):
            rd += s2.keys
            x2 = s2.ap
        if op1 is None:
            self.kb.op(eng, lambda e: e.tensor_scalar(out=out.ap, in0=a.ap, scalar1=x1, scalar2=None, op0=op0),
                       reads=rd, writes=out.keys)
        else:
            self.kb.op(eng, lambda e: e.tensor_scalar(out=out.ap, in0=a.ap, scalar1=x1, scalar2=x2, op0=op0, op1=op1),
                       reads=rd, writes=out.keys)

    def stt(self, out, in0, scalar, in1, op0, op1):
        rd = in0.keys + in1.keys
        x = scalar
        if isinstance(scalar, V):
            rd = rd + scalar.keys
            x = scalar.ap
        self.kb.op('dve', lambda e: e.scalar_tensor_tensor(out=out.ap, in0=in0.ap, scalar=x, in1=in1.ap, op0=op0, op1=op1),
                   reads=rd, writes=out.keys)

    def cp(self, eng, out, in_):
        if eng == 'act':
            self.kb.op(eng, lambda e: e.copy(out=out.ap, in_=in_.ap), reads=in_.keys, writes=out.keys)
        else:
            self.kb.op(eng, lambda e: e.tensor_copy(out=out.ap, in_=in_.ap), reads=in_.keys, writes=out.keys)

    def scan(self, out, d0, d1, init):
        rd = d0.keys + d1.keys
        x = init
        if isinstance(init, V):
            rd = rd + init.keys
            x = init.ap
        self.kb.op('dve', lambda e: e.tensor_tensor_scan(out=out.ap, data0=d0.ap, data1=d1.ap, initial=x,
                                                          op0=ALU.mult, op1=ALU.add), reads=rd, writes=out.keys)

    def memset(self, eng, out, val):
        self.kb.op(eng, lambda e: e.memset(out.ap, val), writes=out.keys)

    def asel(self, out, in_, pattern, cmp, fill, base, cm):
        self.kb.op('pool', lambda e: e.affine_select(out=out.ap, in_=in_.ap, pattern=pattern, compare_op=cmp,
                                                     fill=fill, base=base, channel_multiplier=cm),
                   reads=in_.keys, writes=out.keys)

    def rsqrt(self, out, in_, addc):
        self.kb.op('act', lambda e: e.activation(out=out.ap, in_=in_.ap, func=AF.Sqrt, bias=addc, scale=1.0),
                   reads=in_.keys, writes=out.keys)
        self.recip(out, out)

    def recip(self, out, in_):
        self.kb.op('dve', lambda e: e.reciprocal(out=out.ap, in_=in_.ap), reads=in_.keys, writes=out.keys)

    def dma(self, q, out, in_):
        self.kb.dma(q, out.ap, in_.ap, reads=in_.keys, writes=out.keys)


def build(T, depth, seq_out):
    ntiles = (T + TT - 1) // TT
    Tp = ntiles * TT
    pad = Tp - T
    out_row0 = Tp - seq_out
    nc = bass.Bass("TRN2", target_bir_lowering=False)
    L = depth

    def dram(name, shape, kind="ExternalInput"):
        return nc.dram_tensor(name, shape, F32, kind=kind).ap()

    xin = dram("xin", [Tp, D])
    lnin = dram("lnin", [2, D])
    wfm = dram("wfm", [L, 80, 128, KT * 128])
    wsc = dram("wsc", [L, 128, KT * 16])
    witm = dram("witm", [L, 4, 128, KT * 128])
    wglu = dram("wglu", [L, 2, 4, 128, 4 * 128])
    wbr = dram("wbr", [L, 8, 4, 128, 4 * 128])
    wout = dram("wout", [L, KT, 128, D])
    pp_d = dram("pp", [L, 128, NPP])
    pr_d = dram("pr", [L, NPR])
    lbl_d = dram("lbl", [128, 4 * depth])
    s5b_d = dram("s5b", [L, 2, 128, 16 * 128])
    s5c_d = dram("s5c", [L, 2, 128, 16 * 32])
    yout = dram("y", [seq_out, D], kind="ExternalOutput")
    hb = [dram("hbuf%d" % i, [Tp, D], kind="Internal") for i in range(2)]

    def DV(ap, name, blk=0):
        return V(ap, [("dram_" + name, blk)])

    with ExitStack() as es:
        g = G(nc, es)
        kb = g.kb
        sb, ps = g.sb, g.ps
        M, A_, S_ = ALU.mult, ALU.add, ALU.subtract
        ident = sb("ident", 128)
        g.ident = ident
        ones = sb("ones", 128)
        zeros = sb("zeros", 128)
        NEGS = sb("NEGS", 128)
        NEGT = sb("NEGT", 128)
        TRIU = sb("TRIU", 128)
        M01 = sb("M01", 64)
        SEG = sb("SEG", TT)
        g.memset('pool', zeros.r(0, 128), 0.0)
        g.memset('pool', ones.r(0, 128), 1.0)
        g.asel(ident.r(0, 128), zeros.r(0, 128), [[-1, 128]], ALU.not_equal, 1.0, 0, 1)
        g.asel(NEGS.r(0, 128), zeros.r(0, 128), [[-1, 128]], ALU.is_gt, BIG, 0, 1)
        g.asel(NEGT.r(0, 128), zeros.r(0, 128), [[1, 128]], ALU.is_ge, -BIG, 0, -1)
        g.asel(TRIU.r(0, 128), ones.r(0, 128), [[1, 128]], ALU.is_ge, 0.0, 0, -1)
        g.asel(M01.r(0, 64, 0, 64), ones.r(0, 64, 0, 64), [[1, 64]], ALU.is_ge, 0.0, 0, -1)
        g.memset('pool', SEG.r(0, TT), 1.0)
        for c in range(NCH):
            g.memset('pool', SEG.r(c * 64, 1), 0.0)

        NW = 6
        wring = sb("wring", NW * 1024)
        htok = sb("htok", NB * D)
        hT = sb("hT", KT * TT)
        ppt = sb("ppt", NPP)
        prt = sb("prt", NPR)
        lbl = sb("lbl_s", 4 * depth)
        lbt = sb("lbt", depth * 4)
        omlb = sb("omlb", depth * 4)
        negA12 = sb("negA12", 12)
        wsm = sb("wsm", 8)
        halo_g = sb("halo_g", 12 * 3)
        halo_m = sb("halo_m", 8 * 3)
        S_g = sb("S_g", 4 * 128)
        S_m = sb("S_m", 8 * 128)
        S_h = sb("S_h", 4 * 128)
        xs5 = sb("xs5", 32)
        BbT = sb("BbT", 2 * 16 * 128)
        Cpd = sb("Cpd", 2 * 16 * 32)
        costab = sb("costab", 16 * S5S)
        sintab = sb("sintab", 16 * S5S)
        rho = sb("rho", 16)
        Y = [sb("y%d" % i, 4 * TT) for i in range(4)]
        padm = sb("padm", NB)
        scb = sb("scb", 16 * 16)
        bcx = sb("bcx", 8 * 128)
        ebx = sb("ebx", 8 * 128)
        wscb = sb("wscb", KT * 16)
        lnst = sb("lnst", 16)
        rbuf = sb("rbuf", D)
        NAR = 13440
        AR = sb("AR", NAR)
        PS = [ps("ps%d" % i, 512) for i in range(8)]
        for yb_ in Y:
            g.memset('pool', yb_.r(0, 4 * TT), 0.0)

        stream = []

        def layer_stream(l):
            it = []
            for ct in range(16):
                it.append(("gdn%d" % ct, wfm[l, ct, :, :], 1024))
            for ct in range(12):
                it.append(("m2_%d" % ct, wfm[l, 16 + ct, :, :], 1024))
            for ct in range(8):
                it.append(("hgqf%d" % ct, wfm[l, 28 + ct, :, :], 1024))
            for ct in range(4):
                it.append(("hgi%d" % ct, witm[l, ct, :, :], 1024))
            for ct in range(4):
                it.append(("hgz%d" % ct, wfm[l, 36 + ct, :, :], 1024))
            for ct in range(8):
                it.append(("s5_%d" % ct, wfm[l, 40 + ct, :, :], 1024))
            for ct in range(4):
                for j in range(2):
                    it.append(("glu%d_%d" % (j, ct), wglu[l, j, ct, :, :], 512))
            for dt in range(8):
                for b in range(4):
                    it.append(("gate%d_%d" % (b, dt), wfm[l, 48 + b * 8 + dt, :, :], 1024))
                    it.append(("wbr%d_%d" % (b, dt), wbr[l, dt, b, :, :], 512))
            for k in range(KT):
                it.append(("wout%d" % k, wout[l, k, :, :], 1024))
            pref = []
            if 'gdn' not in PHASES:
                pref.append('gdn')
            if 'ssd' not in PHASES:
                pref.append('m2_')
            if 'hg' not in PHASES:
                pref.append('hg')
            if 's5' not in PHASES:
                pref += ['s5_', 'glu']
            it = [x for x in it if not any(x[0].startswith(q) for q in pref)]
            return it

        wst = {'issued': 0, 'used': 0}
        PF = 4

        def w_begin(l):
            stream[:] = layer_stream(l)
            wst['issued'] = 0
            wst['used'] = 0

        def w_issue():
            i = wst['issued']
            key, src, ncols = stream[i]
            slot = i % NW
            g.dma('sp', wring.r(slot * 1024, ncols), DV(src, "w"))
            wst['issued'] += 1

        def w_get(key):
            i = wst['used']
            assert stream[i][0] == key, (stream[i][0], key)
            while wst['issued'] <= min(i + PF, len(stream) - 1):
                w_issue()
            wst['used'] += 1
            return (i % NW) * 1024

        def ln_rows(dst, src_sb, gam, bet):
            for c in range(2):
                kb.op('dve', lambda e: e.bn_stats(out=lnst.h[:, c * 6:(c + 1) * 6], in_=src_sb.ap[:, c * 512:(c + 1) * 512]),
                      reads=src_sb.keys, writes=lnst.keys(0, 16))
            kb.op('dve', lambda e: e.bn_aggr(out=lnst.h[:, 12:14], in_=lnst.h[:, 0:12].rearrange("p (c s) -> p c s", c=2)),
                  reads=lnst.keys(0, 16), writes=lnst.keys(0, 16))
            g.rsqrt(lnst.r(14, 1), lnst.r(13, 1), LN_EPS)
            g.ts('dve', dst, src_sb, lnst.r(12, 1), S_, lnst.r(14, 1), M)
            g.tt('pool', dst, dst, gam, M)
            g.tt('pool', dst, dst, bet, A_)

        def rmsnorm_fm(yv, ntile, wcol, scale_n, tmp_off, psb):
            pass

        if pad > 0:
            g.memset('dve', AR.r(0, D), 0.0)
            r = 0
            while r < pad:
                n = min(128, pad - r)
                for i in range(2):
                    g.dma('sp', DV(hb[i][r:r + n, :], "hb%d" % i), AR.r(0, D, 0, n))
                r += n

        g.dma('sp', prt.r(0, D), DV(lnin[0:1, :].to_broadcast([128, D]), "lnin"))
        g.dma('sp', prt.r(D, D), DV(lnin[1:2, :].to_broadcast([128, D]), "lnin"))
        for t in range(ntiles):
            for b in range(NB):
                r0 = t * TT + b * 128
                lo = max(0, pad - r0)
                if lo >= 128:
                    continue
                j = b % NB
                g.dma('sp', htok.r(j * D, D), DV(xin[r0:r0 + 128, :], "xin"))
                ln_rows(rbuf.r(0, D), htok.r(j * D, D), prt.r(0, D), prt.r(D, D))
                g.dma('pool', DV(hb[0][r0 + lo:r0 + 128, :], "hb0"), rbuf.r(0, D, lo, 128))

        g.dma('sp', lbl.r(0, 4 * depth), DV(lbl_d[:, :], "lbl"))
        g.act(lbl.r(0, 4 * depth), lbl.r(0, 4 * depth), AF.Exp)
        kb.op('dve', lambda e: e.tensor_reduce(out=AR.h[:, 0:4], in_=lbl.h[:, :].rearrange("p (c l) -> p c l", l=depth),
                                               axis=AX.X, op=ALU.add), reads=lbl.keys(0, 4 * depth), writes=AR.keys(0, 4))
        g.recip(AR.r(4, 4), AR.r(0, 4))
        g.memset('dve', lbt.r(0, 4), 0.0)
        for l in range(1, depth):
            ev = V(lbl.h[:, :].rearrange("p (c l) -> p c l", l=depth)[:, :, l], lbl.keys(0, 4 * depth))
            g.tt('dve', lbt.r(l * 4, 4), lbt.r((l - 1) * 4, 4), ev, A_)
        for l in range(1, depth):
            g.tt('dve', lbt.r(l * 4, 4), lbt.r(l * 4, 4), AR.r(4, 4), M)
        g.ts('dve', omlb.r(0, 4 * depth), lbt.r(0, 4 * depth), -1.0, M, 1.0, A_)

        for l in range(L):
            src = hb[l % 2]
            srcn = "hb%d" % (l % 2)
            dst = hb[(l + 1) % 2]
            dstn = "hb%d" % ((l + 1) % 2)
            last = (l == L - 1)
            g.dma('sp', ppt.r(0, NPP), DV(pp_d[l, :, :], "pp"))
            g.dma('sp', wscb.r(0, KT * 16), DV(wsc[l, :, :], "wsc"))
            g.dma('sp', prt.r(0, NPR), DV(pr_d[l:l + 1, :].to_broadcast([128, NPR]), "pr"))
            g.act(negA12.r(0, 12), prt.r(PR_A12, 12), AF.Exp)
            g.ts('dve', negA12.r(0, 12), negA12.r(0, 12), -1.0, M)
            g.ts('dve', wsm.r(0, 1), ppt.r(PP_GNW, 1), math.sqrt(128.0), M)
            g.ts('dve', wsm.r(1, 1), ppt.r(PP_HNW, 1), math.sqrt(128.0), M)
            g.ts('dve', wsm.r(2, 4), ppt.r(PP_MNW, 4), 16.0, M)
            for st_, n_ in ((halo_g, 36), (halo_m, 24), (S_g, 512), (S_m, 1024), (S_h, 512), (xs5, 32)):
                g.memset('pool', st_.r(0, n_), 0.0)
            a_re = ppt.r(PP_SAR, 16)
            a_im = ppt.r(PP_SAI, 16)

            def sl(i):
                return AR.r(8192 + i * 16, 16)
            DT_, ARD, ANG, T1, T2, COS, SIN, LRE, LIM, DEN, ZRE, ZIM = range(12)
            g.act(sl(DT_), ppt.r(PP_SLDT, 16), AF.Exp)
            g.tt('dve', sl(ARD), a_re, sl(DT_), M)
            g.act(rho.r(0, 16), sl(ARD), AF.Exp)
            g.tt('dve', sl(ANG), a_im, sl(DT_), M)
            g.act(sl(SIN), sl(ANG), AF.Sin, scale=1.0 / 32.0)
            g.ts('dve', sl(T2), sl(ANG), 1.0 / 32.0, M, 0.5 * math.pi, A_)
            g.act(sl(COS), sl(T2), AF.Sin)
            for _ in range(5):
                g.tt('dve', sl(T1), sl(COS), sl(COS), M)
                g.tt('dve', sl(T2), sl(SIN), sl(SIN), M)
                g.stt(sl(SIN), sl(SIN), 2.0, sl(COS), M, M)
                g.tt('dve', sl(COS), sl(T1), sl(T2), S_)
            g.tt('dve', sl(LRE), rho.r(0, 16), sl(COS), M)
            g.tt('dve', sl(LIM), rho.r(0, 16), sl(SIN), M)
            g.tt('dve', sl(DEN), a_re, a_re, M)
            g.tt('dve', sl(T1), a_im, a_im, M)
            g.tt('dve', sl(DEN), sl(DEN), sl(T1), A_)
            g.recip(sl(DEN), sl(DEN))
            g.ts('dve', sl(LRE), sl(LRE), -1.0, A_)
            g.tt('dve', sl(T1), sl(LRE), a_re, M)
            g.tt('dve', sl(T2), sl(LIM), a_im, M)
            g.tt('dve', sl(ZRE), sl(T1), sl(T2), A_)
            g.tt('dve', sl(ZRE), sl(ZRE), sl(DEN), M)
            g.tt('dve', sl(T1), sl(LIM), a_re, M)
            g.tt('dve', sl(T2), sl(LRE), a_im, M)
            g.tt('dve', sl(ZIM), sl(T1), sl(T2), S_)
            g.tt('dve', sl(ZIM), sl(ZIM), sl(DEN), M)
            g.cp('dve', costab.rs(0, 16, S5S, 0, 1), usq(sl(COS), 2))
            g.cp('dve', sintab.rs(0, 16, S5S, 0, 1), usq(sl(SIN), 2))
            n = 1
            while n < S5S:
                cn = bcast(costab.rs(0, 16, S5S, n - 1, 1), [128, 16, n])
                sn = bcast(sintab.rs(0, 16, S5S, n - 1, 1), [128, 16, n])
                t1 = AR.r3(8448, 16, n)
                t2 = AR.r3(9472, 16, n)
                c0 = costab.rs(0, 16, S5S, 0, n)
                s0 = sintab.rs(0, 16, S5S, 0, n)
                g.tt('dve', t1, c0, cn, M)
                g.tt('dve', t2, s0, sn, M)
                g.tt('dve', costab.rs(0, 16, S5S, n, n), t1, t2, S_)
                g.tt('dve', t1, s0, cn, M)
                g.tt('dve', t2, c0, sn, M)
                g.tt('dve', sintab.rs(0, 16, S5S, n, n), t1, t2, A_)
                n *= 2
            Bre = AR.r3(0, 16, 128)
            Bim = AR.r3(2048, 16, 128)
            X1 = AR.r3(4096, 16, 128)
            X2 = AR.r3(6144, 16, 128)
            g.dma('sp', AR.r(0, 2048), DV(s5b_d[l, 0, :, :], "s5b"))
            g.dma('sp', AR.r(2048, 2048), DV(s5b_d[l, 1, :, :], "s5b"))
            zre_b = bcast(usq(sl(ZRE), 2), [128, 16, 128])
            zim_b = bcast(usq(sl(ZIM), 2), [128, 16, 128])
            g.tt('dve', X1, Bre, zre_b, M)
            g.tt('dve', X2, Bim, zim_b, M)
            g.tt('dve', X1, X1, X2, S_)
            g.tt('dve', X2, Bim, zre_b, M)
            g.tt('dve', Bim, Bre, zim_b, M)
            g.tt('dve', X2, X2, Bim, A_)
            for ri, xo in enumerate((4096, 6144)):
                for pr4 in range(4):
                    pst = PS[pr4 % 2]
                    for q in range(4):
                        pr_ = pr4 * 4 + q
                        g.tr(pst.r(q * 128, 128), AR.r(xo + pr_ * 128, 128))
                    g.cp('act', BbT.r(ri * 2048 + pr4 * 512, 512), pst.r(0, 512))
            g.dma('sp', Cpd.r(0, 512), DV(s5c_d[l, 0, :, :], "s5c"))
            g.dma('sp', Cpd.r(512, 512), DV(s5c_d[l, 1, :, :], "s5c"))
            g.ts('pool', Cpd.r(512, 512), Cpd.r(512, 512), -1.0, M)

            def rows(ap, r0, n):
                if isinstance(r0, int):
                    return ap[r0:r0 + n, :]
                return ap[bass.ds(r0, n), :]

            def tile_body(t, dyn):
                tok0 = t * TT
                has_pad = (not dyn) and (tok0 < pad)
                w_begin(l)
                for b in range(NB):
                    r0 = tok0 + b * 128
                    g.dma('sp', htok.r(b * D, D), DV(rows(src, r0, 128), srcn))
                for k in range(KT):
                    pst = PS[k % 2]
                    for b in range(NB):
                        g.tr(pst.r(b * 128, 128), htok.r(b * D + k * 128, 128))
                    g.cp('act' if k % 2 == 0 else 'dve', hT.r(k * TT, TT), pst.r(0, TT))
                if has_pad:
                    for b in range(NB):
                        lo = pad - (tok0 + b * 128)
                        g.asel(padm.r(b, 1), ones.r(0, 1), [[0, 1]], ALU.is_ge, 0.0, -lo, 1)

                pp_state = {'i': 0}
                W3 = TT + 3

                def proj_fm(key):
                    wo = w_get(key)
                    pst = PS[pp_state['i'] % 2]
                    pp_state['i'] += 1
                    for k in range(KT):
                        g.mm(pst.r(0, TT), wring.r(wo + k * 128, 128), hT.r(k * TT, TT), start=(k == 0), stop=(k == KT - 1))
                    return pst.r(0, TT)

                def block_scalars(b, c0, nh):
                    R = lambda i, n=nh, o=0: scb.r(i * 16 + o, n)
                    pst = PS[2]
                    for k in range(KT):
                        g.mm(pst.r(0, 16), hT.r(k * TT + b * 128, 128), wscb.r(k * 16, 16),
                             start=(k == 0), stop=(k == KT - 1))
                    g.cp('dve', R(0, 16), pst.r(0, 16))
                    if c0 == 4:
                        g.act(R(1, 4), R(0, 4), AF.Sigmoid)
                    bo = PR_B12 + (0 if c0 == 4 else 4)
                    ao = 0 if c0 == 4 else 4
                    g.tt('dve', R(2), R(0, nh, c0), prt.r(bo, nh), A_)
                    g.act(R(3), R(2), AF.Abs)
                    g.act(R(3), R(3), AF.Exp, scale=-1.0)
                    g.act(R(3), R(3), AF.Ln, bias=1.0)
                    g.ts('dve', R(2), R(2), 0.0, ALU.max)
                    g.tt('dve', R(4), R(2), R(3), A_)
                    if has_pad and c0 == 8:
                        g.ts('dve', R(4), R(4), padm.r(b, 1), M)
                    g.tt('dve', R(5), R(4), negA12.r(ao, nh), M)
                    g.mm(pst.r(16, nh), TRIU.r(0, 128), R(5))
                    g.cp('dve', R(6), pst.r(16, nh))
                    Rv = AR.r3(11392, nh, 128)
                    g.tt('pool', Rv, bcast(usq(TRIU.r(0, 128), 1), [128, nh, 128]), bcast(usq(R(5), 2), [128, nh, 128]), M)
                    for q in range(nh // 4):
                        pq = PS[3]
                        g.mm(pq.r(0, 512), ones.r(0, 128), AR.r(11392 + q * 512, 512))
                        g.cp('act', bcx.r(q * 512, 512), pq.r(0, 512))
                    g.act(ebx.r(0, nh * 128), bcx.r(0, nh * 128), AF.Exp)
                    g.act(R(7), R(6), AF.Exp)
                    last = bcx.rs(0, nh, 128, 127, 1)
                    g.tt('dve', usq(R(8), 2), last, usq(R(6), 2), S_)
                    g.act(R(8), R(8), AF.Exp)
                    if c0 == 4:
                        g.tt('dve', R(9), R(1, 4), R(7), M)
                    else:
                        g.tt('dve', R(10), R(4), R(8), M)
                    return R

                def conv_fm(nct, xc_off, acc_off, halo, cw_off, cb_off):
                    W3 = TT + 3
                    g.cp('pool', AR.rs(xc_off, nct, W3, 0, 3), halo.r3(0, nct, 3))
                    for ct in range(nct):
                        xo = xc_off + ct * W3
                        acc = AR.r(acc_off + ct * TT, TT)
                        cw = lambda j: ppt.r(cw_off + ct * 4 + j, 1)
                        if cb_off is None:
                            g.ts('dve', acc, AR.r(xo + 3, TT), cw(3), M)
                        else:
                            g.ts('dve', acc, AR.r(xo + 3, TT), cw(3), M, ppt.r(cb_off + ct, 1), A_)
                        for j in range(3):
                            g.stt(acc, AR.r(xo + j, TT), cw(j), acc, M, A_)
                        g.act(acc, acc, AF.Silu)
                    g.cp('pool', halo.r3(0, nct, 3), AR.rs(xc_off, nct, W3, TT, 3))

                def rms_fm(ytiles, wcols, neps, zoffs, sq_off, rn_off):
                    pst = PS[3]
                    for i, yv in enumerate(ytiles):
                        sq = AR.r(sq_off, TT)
                        g.tt('pool', sq, yv, yv, M)
                        g.mm(pst.r(0, TT), ones.r(0, 128), sq, start=(i == 0), stop=(i == len(ytiles) - 1))
                    rn = AR.r(rn_off, TT)
                    g.rsqrt(rn, pst.r(0, TT), neps)
                    for yv, wc in zip(ytiles, wcols):
                        g.stt(yv, yv, wc, rn, M, M)

                if 'gdn' in PHASES:
                    XC, QKV, ZS = 0, 3200, 6272
                    W3 = TT + 3
                    for ct in range(12):
                        pv = proj_fm("gdn%d" % ct)
                        g.cp('act', AR.r(XC + ct * W3 + 3, TT), pv)
                    conv_fm(12, XC, QKV, halo_g, PP_CWG, None)
                    for ct in range(4):
                        pv = proj_fm("gdn%d" % (12 + ct))
                        g.act(AR.r(ZS + ct * TT, TT), pv, AF.Silu)
                    for ct in range(8):
                        x = AR.r(QKV + ct * TT, TT)
                        sq = AR.r(9600, TT)
                        g.tt('pool', sq, x, x, M)
                        pst = PS[3]
                        g.mm(pst.r(0, TT), ones.r(0, 128), sq)
                        rn = AR.r(9856, TT)
                        g.rsqrt(rn, pst.r(0, TT), RMS_EPS)
                        if ct < 4:
                            g.stt(x, rn, 128.0 ** -0.5, x, M, M)
                        else:
                            g.tt('dve', x, x, rn, M)
                    OSB = 10112
                    TA, Dm, Am, Bm, QKT, VB, KBG, KDEC, Um, WTm, VNEW, QDEC = [7296 + i * 128 for i in range(12)]
                    PQ = [8832 + 128 * i for i in range(4)]
                    RR = [9344, 9472]
                    for b in range(NB):
                        R = block_scalars(b, 4, 4)
                        for h in range(4):
                            qT = AR.r(QKV + h * TT + b * 128, 128)
                            kT = AR.r(QKV + (4 + h) * TT + b * 128, 128)
                            vT = AR.r(QKV + (8 + h) * TT + b * 128, 128)
                            gcb = bcx.r(h * 128, 128)
                            gcc = R(6, 1, h)
                            pa = PS[4]
                            g.mm(pa.r(0, 128), kT, kT)
                            g.mm(pa.r(128, 128), kT, qT)
                            g.stt(AR.r(TA, 128), gcb, gcc, NEGS.r(0, 128), S_, A_)
                            g.act(AR.r(Dm, 128), AR.r(TA, 128), AF.Exp, scale=-1.0)
                            g.stt(AR.r(Am, 128), AR.r(Dm, 128), R(1, 1, h), pa.r(0, 128), M, M)
                            g.stt(AR.r(TA, 128), gcb, gcc, NEGT.r(0, 128), S_, A_)
                            g.act(AR.r(Dm, 128), AR.r(TA, 128), AF.Exp)
                            g.tt('dve', AR.r(QKT, 128), AR.r(Dm, 128), pa.r(128, 128), M)
                            g.tr(pa.r(256, 128), AR.r(Am, 128))
                            g.cp('act', AR.r(Bm, 128), pa.r(256, 128))
                            g.tr(pa.r(384, 128), kT)
                            g.ts('dve', AR.r(KBG, 128), pa.r(384, 128), R(9, 1, h), M)
                            g.ts('dve', AR.r(KDEC, 128), pa.r(384, 128), R(8, 1, h), M)
                            pb = PS[5]
                            g.tr(pb.r(0, 128), vT)
                            g.ts('dve', AR.r(VB, 128), pb.r(0, 128), R(1, 1, h), M)
                            g.tt('pool', AR.r(RR[0], 128), ident.r(0, 128), AR.r(Bm, 128), S_)
                            Pc, Qc = Am, Bm
                            ri = 0
                            for lev in range(1, 7):
                                Pn, Qn = PQ[(lev % 2) * 1], PQ[2 + (lev % 2) * 1]
                                g.mm(pb.r(128, 128), AR.r(Qc, 128), AR.r(Pc, 128))
                                g.cp('act', AR.r(Pn, 128), pb.r(128, 128))
                                if lev < 6:
                                    g.mm(pb.r(256, 128), AR.r(Pc, 128), AR.r(Qc, 128))
                                    g.cp('dve', AR.r(Qn, 128), pb.r(256, 128))
                                g.mm(pb.r(384, 128), AR.r(Pn, 128), AR.r(RR[ri], 128))
                                g.tt('dve', AR.r(RR[1 - ri], 128), AR.r(RR[ri], 128), pb.r(384, 128), A_)
                                ri = 1 - ri
                                Pc, Qc = Pn, Qn
                            TTm = AR.r(RR[ri], 128)
                            pc = PS[6]
                            g.mm(pc.r(0, 128), TTm, AR.r(VB, 128))
                            g.cp('act', AR.r(Um, 128), pc.r(0, 128))
                            g.mm(pc.r(128, 128), AR.r(KBG, 128), TTm)
                            g.cp('act', AR.r(WTm, 128), pc.r(128, 128))
                            Sv = S_g.r(h * 128, 128)
                            g.mm(pc.r(256, 128), AR.r(WTm, 128), Sv)
                            g.tt('dve', AR.r(VNEW, 128), AR.r(Um, 128), pc.r(256, 128), S_)
                            g.tt('pool', AR.r(QDEC, 128), qT, ebx.r(h * 128, 128), M)
                            po = PS[7]
                            ov = po.r((h % 2) * 256 + b * 128, 128)
                            g.mm(ov, Sv, AR.r(QDEC, 128), start=True, stop=False)
                            g.mm(ov, AR.r(VNEW, 128), AR.r(QKT, 128), start=False, stop=True)
                            g.cp('act', AR.r(OSB + h * TT + b * 128, 128), ov)
                            g.mm(pc.r(384, 128), AR.r(KDEC, 128), AR.r(VNEW, 128))
                            elast = ebx.r(h * 128 + 127, 1)
                            g.stt(Sv, Sv, elast, pc.r(384, 128), M, A_)
                    for h in range(4):
                        ov = AR.r(OSB + h * TT, TT)
                        rms_fm([ov], [wsm.r(0, 1)], 128.0 * RMS_EPS, None, 9600, 9856)
                        g.tt('pool', Y[0].r(h * TT, TT), ov, AR.r(ZS + h * TT, TT), M)

                if 'ssd' in PHASES:
                    XC, XBC, ZS = 0, 3200, 6272
                    for ct in range(8):
                        pv = proj_fm("m2_%d" % ct)
                        g.cp('act', AR.r(XC + ct * W3 + 3, TT), pv)
                    conv_fm(8, XC, XBC, halo_m, PP_CWM, PP_CBM)
                    for ct in range(4):
                        pv = proj_fm("m2_%d" % (8 + ct))
                        g.act(AR.r(ZS + ct * TT, TT), pv, AF.Silu)
                    XDTP, XDEC, BTOK, CBT, DTm, CTD, TMPS = 7296, 8320, 8832, 9088, 9344, 10368, 12416
                    g.memset('pool', AR.r(XDTP, 1024), 0.0)
                    YPS = [PS[6], PS[7]]
                    for b in range(NB):
                        R = block_scalars(b, 8, 8)
                        px = PS[4]
                        for c in range(4):
                            g.tr(px.r(c * 128, 128), AR.r(XBC + c * TT + b * 128, 128))
                        pbk = PS[5]
                        for gi in range(2):
                            g.tr(pbk.r(gi * 128, 128), AR.r(XBC + (4 + gi) * TT + b * 128, 128))
                        g.cp('act', AR.r(BTOK, 256), pbk.r(0, 256))
                        for par in range(2):
                            outv = V(AR.h[:, XDTP:XDTP + 1024].rearrange("p (c q) -> p c q", q=256)[:, :, par * 192: par * 192 + 64],
                                     AR.keys(XDTP, 1024))
                            inv = V(px.h[:, 0:512].rearrange("p (c q) -> p c q", q=128)[:, :, par * 64:(par + 1) * 64], px.keys(0, 512))
                            sp_ = V(scb.h[:, 4 * 16:4 * 16 + 8].rearrange("p (c q) -> p c q", q=2)[:, :, par:par + 1].to_broadcast([128, 4, 64]),
                                    scb.keys(64, 8))
                            g.tt('dve', outv, inv, sp_, M)
                        g.tt('dve', AR.r3(XDEC, 8, 64), px.r3(0, 8, 64), bcast(usq(R(10), 2), [128, 8, 64]), M)
                        for gi in range(2):
                            g.mm(pbk.r(256 + gi * 128, 128), AR.r(XBC + (4 + gi) * TT + b * 128, 128),
                                 AR.r(XBC + (6 + gi) * TT + b * 128, 128))
                        g.cp('act', AR.r(CBT, 256), pbk.r(256, 256))
                        g.tt('dve', AR.r3(TMPS, 8, 128), bcx.r3(0, 8, 128), bcast(usq(R(6), 2), [128, 8, 128]), S_)
                        g.tt('pool', AR.r3(TMPS, 8, 128), AR.r3(TMPS, 8, 128), bcast(usq(NEGT.r(0, 128), 1), [128, 8, 128]), A_)
                        g.act(AR.r(DTm, 1024), AR.r(TMPS, 1024), AF.Exp)
                        for gi in range(2):
                            g.tt('pool', AR.r3(DTm + gi * 512, 4, 128), AR.r3(DTm + gi * 512, 4, 128),
                                 bcast(usq(AR.r(CBT + gi * 128, 128), 1), [128, 4, 128]), M)
                            g.tt('pool', AR.r3(CTD + gi * 512, 4, 128), ebx.r3(gi * 512, 4, 128),
                                 bcast(usq(AR.r(XBC + (6 + gi) * TT + b * 128, 128), 1), [128, 4, 128]), M)
                        for c in range(4):
                            yv = YPS[c // 2].r((c % 2) * 256 + b * 128, 128)
                            for hh in range(2):
                                h = 2 * c + hh
                                g.mm(yv, AR.r(XDTP + h * 128, 128), AR.r(DTm + h * 128, 128), start=(hh == 0), stop=False)
                                g.mm(yv, S_m.r(h * 128, 128), AR.r(CTD + h * 128, 128), start=False, stop=(hh == 1))
                        pu = PS[3]
                        for gi in range(2):
                            g.mm(pu.r(gi * 256, 256), AR.r(BTOK + gi * 128, 128), AR.r(XDEC + gi * 256, 256))
                        for par in range(2):
                            sv = V(S_m.h[:, :].rearrange("p (c q) -> p c q", q=256)[:, :, par * 192: par * 192 + 64], S_m.keys(0, 1024))
                            el = V(ebx.h[:, :].rearrange("p (c q) -> p c q", q=256)[:, :, par * 128 + 127: par * 128 + 128].to_broadcast([128, 4, 64]),
                                   ebx.keys(0, 1024))
                            uv = V(pu.h[:, 0:512].rearrange("p (c q) -> p c q", q=128)[:, :, par * 64:(par + 1) * 64], pu.keys(0, 512))
                            g.tt('pool', sv, sv, el, M)
                            g.tt('dve', sv, sv, uv, A_)
                    for c in range(4):
                        yv = AR.r(c * TT, TT)
                        g.stt(yv, AR.r(XBC + c * TT, TT), ppt.r(PP_MD + c, 1), YPS[c // 2].r((c % 2) * 256, TT), M, A_)
                        g.tt('pool', yv, yv, AR.r(ZS + c * TT, TT), M)
                    for gi in range(2):
                        tiles = [AR.r((2 * gi + i) * TT, TT) for i in range(2)]
                        rms_fm(tiles, [wsm.r(2 + 2 * gi + i, 1) for i in range(2)], 256.0 * RMS_EPS, None, 9600, 9856)
                        for i in range(2):
                            g.cp('pool', Y[1].r((2 * gi + i) * TT, TT), tiles[i])

                if 'hg' in PHASES:
                    QH, Fo, LOGF, GC, EG, QD, KD, KK, KDE, ZS, ITOK, ATM, KDT = \
                        0, 1024, 2048, 3072, 4096, 5120, 6144, 7168, 8192, 9216, 10240, 12288, 12544
                    for ct in range(4):
                        pv = proj_fm("hgqf%d" % ct)
                        g.act(AR.r(QH + ct * TT, TT), pv, AF.Silu)
                    for ct in range(4):
                        pv = proj_fm("hgqf%d" % (4 + ct))
                        f = AR.r(Fo + ct * TT, TT)
                        g.act(f, pv, AF.Sigmoid)
                        g.ts('dve', f, f, omlb.r(l * 4 + ct, 1), M, lbt.r(l * 4 + ct, 1), A_)
                        g.act(AR.r(LOGF + ct * TT, TT), f, AF.Ln)
                        g.ts('pool', AR.r(KK + ct * TT, TT), f, -1.0, M, 1.0, A_)
                        g.scan(AR.r(GC + ct * TT, TT), SEG.r(0, TT), AR.r(LOGF + ct * TT, TT), 0.0)
                    g.act(AR.r(EG, 1024), AR.r(GC, 1024), AF.Exp)
                    g.tt('pool', AR.r(QD, 1024), AR.r(QH, 1024), AR.r(EG, 1024), M)
                    g.ts('dve', AR.r(KD, 1024), AR.r(GC, 1024), -80.0, ALU.max)
                    g.act(AR.r(KD, 1024), AR.r(KD, 1024), AF.Exp, scale=-1.0)
                    g.tt('pool', AR.r(KD, 1024), AR.r(KD, 1024), AR.r(KK, 1024), M)
                    for ct in range(4):
                        for c in range(NCH):
                            o = ct * TT + c * 64
                            g.act(AR.r(KDE + o, 64), AR.r(GC + o, 64), AF.Exp, bias=AR.r(GC + o + 63, 1), scale=-1.0)
                    g.tt('pool', AR.r(KDE, 1024), AR.r(KDE, 1024), AR.r(KK, 1024), M)
                    for ct in range(4):
                        wo = w_get("hgi%d" % ct)
                        for c in range(NCH):
                            for k in range(KT):
                                g.mm(PS[4 + c].r(ct * 128, 128, 0, 64), hT.r(k * TT + c * 64, 64), wring.r(wo + k * 128, 128),
                                     start=(k == 0), stop=(k == KT - 1))
                    for c in range(NCH):
                        g.cp('act' if c % 2 == 0 else 'dve', AR.r(ITOK + c * 512, 512, 0, 64), PS[4 + c].r(0, 512, 0, 64))
                    for ct in range(4):
                        pv = proj_fm("hgz%d" % ct)
                        g.act(AR.r(ZS + ct * TT, TT), pv, AF.Silu)
                    OPS = [PS[6], PS[7]]
                    for c in range(NCH):
                        pa = PS[2]
                        for h in range(4):
                            o = h * TT + c * 64
                            g.mm(pa.r(h * 64, 64, 0, 64), AR.r(KD + o, 64), AR.r(QD + o, 64))
                        g.tt('dve', AR.r3(ATM, 4, 64, 0, 64), pa.r3(0, 4, 64, 0, 64), bcast(usq(M01.r(0, 64, 0, 64), 1), [64, 4, 64]), M)
                        pk = PS[3]
                        for h in range(4):
                            o = h * TT + c * 64
                            g.tr(pk.r(h * 128, 128, 0, 64), AR.r(KDE + o, 64))
                        g.cp('act', AR.r(KDT, 512, 0, 64), pk.r(0, 512, 0, 64))
                        for h in range(4):
                            o = h * TT + c * 64
                            iv = AR.r(ITOK + c * 512 + h * 128, 128, 0, 64)
                            Sv = S_h.r(h * 128, 128)
                            ov = OPS[h // 2].r((h % 2) * 256 + c * 64, 64)
                            g.mm(ov, iv, AR.r(ATM + h * 64, 64, 0, 64), start=True, stop=False)
                            g.mm(ov, Sv, AR.r(QD + o, 64), start=False, stop=True)
                            pu = PS[4 + h % 2]
                            g.mm(pu.r(256, 128), AR.r(KDT + h * 128, 128, 0, 64), iv)
                            g.stt(Sv, Sv, AR.r(EG + o + 63, 1), pu.r(256, 128), M, A_)
                    for h in range(4):
                        ov = AR.r(Fo + h * TT, TT)
                        g.cp('act', ov, OPS[h // 2].r((h % 2) * 256, TT))
                        rms_fm([ov], [wsm.r(1, 1)], 128.0 * RMS_EPS, None, LOGF, LOGF + TT)
                        g.tt('pool', Y[2].r(h * TT, TT), ov, AR.r(ZS + h * TT, TT), M)

                if 's5' in PHASES:
                    Uo, ZS, T1o, WRE, WIM, ZR, ZI, XRE, XIM, YTOK, YS, GEL = \
                        0, 1024, 2048, 4096, 4608, 5120, 5632, 6144, 6656, 7168, 7680, 8704
                    for ct in range(4):
                        pv = proj_fm("s5_%d" % ct)
                        g.cp('act', AR.r(Uo + ct * TT, TT), pv)
                    for ct in range(4):
                        pv = proj_fm("s5_%d" % (4 + ct))
                        g.act(AR.r(ZS + ct * TT, TT), pv, AF.Silu)
                    for b in range(NB):
                        py = PS[7]
                        for ft in range(4):
                            pre, pim = PS[4], PS[5]
                            uv = AR.r(Uo + ft * TT + b * 128, 128)
                            for q in range(4):
                                pr_ = ft * 4 + q
                                g.mm(pre.r(q * 128, 128), BbT.r(pr_ * 128, 128), uv)
                                g.mm(pim.r(q * 128, 128), BbT.r(2048 + pr_ * 128, 128), uv)
                            ct_ = costab.r(ft * 512, 512)
                            st_ = sintab.r(ft * 512, 512)
                            t = [AR.r(T1o + i * 512, 512) for i in range(4)]
                            g.tt('dve', t[0], pre.r(0, 512), ct_, M)
                            g.tt('dve', t[1], pim.r(0, 512), st_, M)
                            g.tt('dve', t[2], pim.r(0, 512), ct_, M)
                            g.tt('dve', t[3], pre.r(0, 512), st_, M)
                            g.tt('pool', AR.r(WRE, 512), t[0], t[1], A_)
                            g.tt('pool', AR.r(WIM, 512), t[2], t[3], S_)
                            for q in range(4):
                                pr_ = ft * 4 + q
                                rb = bcast(rho.r(pr_, 1), [128, 128])
                                g.scan(AR.r(ZR + q * 128, 128), rb, AR.r(WRE + q * 128, 128), xs5.r(pr_, 1))
                                g.scan(AR.r(ZI + q * 128, 128), rb, AR.r(WIM + q * 128, 128), xs5.r(16 + pr_, 1))
                            g.tt('pool', t[0], AR.r(ZR, 512), ct_, M)
                            g.tt('pool', t[1], AR.r(ZI, 512), st_, M)
                            g.tt('pool', t[2], AR.r(ZI, 512), ct_, M)
                            g.tt('pool', t[3], AR.r(ZR, 512), st_, M)
                            g.tt('pool', AR.r(XRE, 512), t[0], t[1], S_)
                            g.tt('pool', AR.r(XIM, 512), t[2], t[3], A_)
                            g.cp('pool', xs5.r3(ft * 4, 4, 1), AR.rs(XRE, 4, 128, 127, 1))
                            g.cp('pool', xs5.r3(16 + ft * 4, 4, 1), AR.rs(XIM, 4, 128, 127, 1))
                            for q in range(4):
                                pr_ = ft * 4 + q
                                yv = py.r(ft * 128 + q * 32, 32)
                                g.mm(yv, AR.r(XRE + q * 128, 128), Cpd.r(pr_ * 32, 32), start=True, stop=False)
                                g.mm(yv, AR.r(XIM + q * 128, 128), Cpd.r(512 + pr_ * 32, 32), start=False, stop=True)
                        g.cp('act', AR.r(YTOK, 512), py.r(0, 512))
                        pyt = PS[6]
                        for ft in range(4):
                            g.tr(pyt.r(ft * 128, 128), AR.r(YTOK + ft * 128, 128))
                        for ft in range(4):
                            g.stt(AR.r(YS + ft * TT + b * 128, 128), AR.r(Uo + ft * TT + b * 128, 128), ppt.r(PP_SD + ft, 1),
                                  pyt.r(ft * 128, 128), M, A_)
                    x = AR.r(YS, 1024)
                    g1 = AR.r(GEL, 1024)
                    g2 = AR.r(GEL + 1024, 1024)
                    g.tt('pool', g1, x, x, M)
                    g.ts('dve', g1, g1, 0.044715, M, 1.0, A_)
                    g.tt('pool', g1, g1, x, M)
                    g.act(g2, g1, AF.Sigmoid, scale=2.0 * math.sqrt(2.0 / math.pi))
                    g.tt('pool', x, x, g2, M)
                    for ct in range(4):
                        w1 = w_get("glu0_%d" % ct)
                        w2 = w_get("glu1_%d" % ct)
                        p1, p2 = PS[4], PS[5]
                        for k in range(4):
                            g.mm(p1.r(0, TT), wring.r(w1 + k * 128, 128), AR.r(YS + k * TT, TT), start=(k == 0), stop=(k == 3))
                        for k in range(4):
                            g.mm(p2.r(0, TT), wring.r(w2 + k * 128, 128), AR.r(YS + k * TT, TT), start=(k == 0), stop=(k == 3))
                        sg = AR.r(GEL + ct * TT, TT)
                        g.act(sg, p2.r(0, TT), AF.Sigmoid)
                        g.tt('dve', sg, sg, p1.r(0, TT), M)
                        g.tt('pool', Y[3].r(ct * TT, TT), sg, AR.r(ZS + ct * TT, TT), M)

                MIX, SGT, TMPM = 0, 2048, 2304
                for dt in range(8):
                    for bq in range(4):
                        pv = proj_fm("gate%d_%d" % (bq, dt))
                        sg = AR.r(SGT, TT)
                        g.act(sg, pv, AF.Sigmoid)
                        wo = w_get("wbr%d_%d" % (bq, dt))
                        pb_ = PS[2 + bq % 2]
                        for k in range(4):
                            g.mm(pb_.r(0, TT), wring.r(wo + k * 128, 128), Y[bq].r(k * TT, TT), start=(k == 0), stop=(k == 3))
                        if bq == 0:
                            g.tt('dve', AR.r(MIX + dt * TT, TT), sg, pb_.r(0, TT), M)
                        else:
                            g.tt('dve', AR.r(TMPM, TT), sg, pb_.r(0, TT), M)
                            g.tt('pool', AR.r(MIX + dt * TT, TT), AR.r(MIX + dt * TT, TT), AR.r(TMPM, TT), A_)
                for k in range(KT):
                    wo = w_get("wout%d" % k)
                    for b in range(NB):
                        for hf in range(2):
                            g.mm(PS[4 + b * 2 + hf].r(0, 512), AR.r(MIX + k * TT + b * 128, 128), wring.r(wo + hf * 512, 512),
                                 start=(k == 0), stop=(k == KT - 1))
                for b in range(NB):
                    r0 = tok0 + b * 128
                    for hf in range(2):
                        g.stt(rbuf.r(hf * 512, 512), htok.r(b * D + hf * 512, 512), ALPHA, PS[4 + b * 2 + hf].r(0, 512), M, A_)
                    ln_rows(rbuf.r(0, D), rbuf.r(0, D), prt.r(PR_LNG, D), prt.r(PR_LNB, D))
                    if dyn:
                        if not last:
                            g.dma('pool', DV(rows(dst, r0, 128), dstn), rbuf.r(0, D))
                        else:
                            g.dma('pool', DV(rows(yout, r0 - out_row0, 128), "y"), rbuf.r(0, D))
                        continue
                    lo = max(0, pad - r0)
                    if lo >= 128:
                        continue
                    if not last:
                        g.dma('pool', DV(dst[r0 + lo:r0 + 128, :], dstn), rbuf.r(0, D, lo, 128))
                    else:
                        lo2 = max(0, out_row0 - r0)
                        if lo2 < 128:
                            g.dma('pool', DV(yout[r0 + lo2 - out_row0:r0 + 128 - out_row0, :], "y"), rbuf.r(0, D, lo2, 128))
                assert wst['used'] == len(stream), (wst, len(stream))

            tile_body(0, False)
            kb.barrier()
            if ntiles > 1:
                assert out_row0 <= TT
                kb.dry = True
                tile_body(1, True)
                n_it = kb.barrier()
                kb.dry = False
                with nc.Fori(1, ntiles) as ti:
                    kb.enter_loop(ti, n_it, 1)
                    tile_body(ti, True)
                    kb.barrier()
                kb.exit_loop(ntiles - 1)
        kb.barrier()
        print("[build] ninst=%d nwait=%d per_eng=%s" % (kb.ninst, kb.nwait, kb.per_eng), flush=True)
    return nc


def _fm(w, c0, ncol):
    nt = ncol // 128
    x = w[:, c0:c0 + ncol].reshape(KT, 128, nt, 128)
    return np.ascontiguousarray(x.transpose(2, 1, 0, 3)).reshape(nt, 128, KT * 128)


def prep_weights(inp, depth):
    f = lambda a: np.asarray(a, dtype=np.float32)
    L = depth
    w_in = f(inp['w_in'])
    wfm = np.empty((L, 80, 128, KT * 128), np.float32)
    wsc = np.empty((L, 128, KT * 16), np.float32)
    witm = np.empty((L, 4, 128, KT * 128), np.float32)
    for l in range(L):
        w = w_in[l]
        parts = [_fm(w, O_GQKV, 1536), _fm(w, O_GZ, 512), _fm(w, O_MX, 1024), _fm(w, O_MZ, 512),
                 _fm(w, O_HQ, 512), _fm(w, O_HF, 512), _fm(w, O_HZ, 512), _fm(w, O_SU, 512), _fm(w, O_SZ, 512),
                 _fm(w, O_GATE, 4096)]
        wfm[l] = np.concatenate(parts, axis=0)
        sc = np.concatenate([w[:, O_GB:O_GB + 4], w[:, O_GA:O_GA + 4], w[:, O_MDT:O_MDT + 8]], axis=1)
        wsc[l] = sc.reshape(KT, 128, 16).transpose(1, 0, 2).reshape(128, KT * 16)
        witm[l] = _fm(w, O_HI, 512)
    glu = np.stack([f(inp['s5_glu_w1']), f(inp['s5_glu_w2'])], axis=1)
    wglu = np.ascontiguousarray(glu.reshape(L, 2, 4, 128, 4, 128).transpose(0, 1, 4, 3, 2, 5)).reshape(L, 2, 4, 128, 512)
    wb = f(inp['w_branch'])
    wbr = np.ascontiguousarray(wb.reshape(L, 4, 4, 128, 8, 128).transpose(0, 4, 1, 3, 2, 5)).reshape(L, 8, 4, 128, 512)
    wout = np.ascontiguousarray(f(inp['w_out']).reshape(L, KT, 128, D))
    pp = np.zeros((L, 128, NPP), np.float32)
    pr = np.zeros((L, NPR), np.float32)
    for l in range(L):
        pp[l, :, PP_CWG:PP_CWG + 48] = f(inp['gdn_conv_w'])[l].reshape(4, 12, 128).transpose(2, 1, 0).reshape(128, 48)
        pp[l, :, PP_CWM:PP_CWM + 32] = f(inp['m2_conv_w'])[l].reshape(4, 8, 128).transpose(2, 1, 0).reshape(128, 32)
        pp[l, :, PP_CBM:PP_CBM + 8] = f(inp['m2_conv_b'])[l].reshape(8, 128).T
        pp[l, :, PP_GNW] = f(inp['gdn_norm_w'])[l]
        pp[l, :, PP_MD:PP_MD + 4] = np.repeat(f(inp['m2_D'])[l], 64).reshape(4, 128).T
        pp[l, :, PP_MNW:PP_MNW + 4] = f(inp['m2_norm_w'])[l].reshape(4, 128).T
        pp[l, :, PP_HNW] = f(inp['hg_norm_w'])[l]
        pp[l, :, PP_SD:PP_SD + 4] = f(inp['s5_D'])[l].reshape(4, 128).T
        pp[l, :, PP_SAR:PP_SAR + 16] = f(inp['s5_A_re'])[l].reshape(16, 128).T
        pp[l, :, PP_SAI:PP_SAI + 16] = f(inp['s5_A_im'])[l].reshape(16, 128).T
        pp[l, :, PP_SLDT:PP_SLDT + 16] = np.repeat(f(inp['s5_log_dt'])[l], 64).reshape(16, 128).T
        pr[l, PR_A12:PR_A12 + 4] = f(inp['gdn_A_log'])[l]
        pr[l, PR_A12 + 4:PR_A12 + 12] = f(inp['m2_A_log'])[l]
        pr[l, PR_B12:PR_B12 + 4] = f(inp['gdn_dt_bias'])[l]
        pr[l, PR_B12 + 4:PR_B12 + 12] = f(inp['m2_dt_bias'])[l]
        pr[l, PR_LNG:PR_LNG + D] = f(inp['ln_g'])[l]
        pr[l, PR_LNB:PR_LNB + D] = f(inp['ln_b'])[l]
    lbl = np.ascontiguousarray(f(inp['hg_lb_logits']).reshape(L, 4, 128).transpose(2, 1, 0)).reshape(128, 4 * L)
    s5b = np.zeros((L, 2, 128, 16, 128), np.float32)
    s5c = np.zeros((L, 2, 128, 16, 32), np.float32)
    for j, (bn, cn) in enumerate((('s5_B_re', 's5_C_re'), ('s5_B_im', 's5_C_im'))):
        Bm = f(inp[bn])
        Cm = f(inp[cn])
        for pr_ in range(16):
            for g2 in range(2):
                gi = 2 * pr_ + g2
                col = (pr_ % 4) * 32 + g2 * 16
                s5b[:, j, g2 * 64:(g2 + 1) * 64, pr_, col:col + 16] = Bm[:, gi]
                s5c[:, j, g2 * 64:(g2 + 1) * 64, pr_, g2 * 16:(g2 + 1) * 16] = Cm[:, gi].transpose(0, 2, 1)
    lnin = np.stack([f(inp['ln_in_g']), f(inp['ln_in_b'])], axis=0)
    return dict(lnin=lnin, wfm=wfm, wsc=wsc, witm=witm, wglu=wglu, wbr=wbr, wout=wout, pp=pp, pr=pr, lbl=lbl,
                s5b=s5b.reshape(L, 2, 128, 2048), s5c=s5c.reshape(L, 2, 128, 512))


def make_xin(x_b, meta, Tp):
    T = x_b.shape[0] + meta.shape[0]
    xin = np.zeros((Tp, D), np.float32)
    xin[Tp - T:Tp - T + meta.shape[0]] = meta
    xin[Tp - x_b.shape[0]:] = x_b
    return xin


_NC_CACHE = {}


def kernel(**inputs):
    x = np.asarray(inputs['x'], dtype=np.float32)
    Bsz, SEQ, _ = x.shape
    depth = int(np.asarray(inputs['w_in']).shape[0])
    T = SEQ + NMETA
    ntiles = (T + TT - 1) // TT
    Tp = ntiles * TT
    ws = prep_weights(inputs, depth)
    meta = np.asarray(inputs['meta_tokens'], dtype=np.float32)
    key = (T, depth, SEQ)
    if key not in _NC_CACHE:
        _NC_CACHE[key] = build(T, depth, SEQ)
    nc = _NC_CACHE[key]
    in_maps = []
    for b in range(Bsz):
        m = dict(ws)
        m['xin'] = make_xin(x[b], meta, Tp)
        in_maps.append(m)
    res = run_bass_kernel_spmd(nc, in_maps, core_ids=list(range(Bsz)))
    return np.stack([np.asarray(res.results[b]['y'], dtype=np.float32) for b in range(Bsz)], axis=0)
```

```python
import math
import os
import numpy as np
from contextlib import ExitStack
import concourse.bass as bass
import concourse.mybir as mybir
from concourse.bass_utils import run_bass_kernel_spmd

F32 = mybir.dt.float32
F32R = mybir.dt.float32r
FAST_MM = True
AF = mybir.ActivationFunctionType
ALU = mybir.AluOpType
AX = mybir.AxisListType

D = 1024
TT = 256
NB = TT // 128
NCH = TT // 64
KT = 8
NMETA = 16
BIG = 30000.0
ALPHA = 8.0 ** 0.25
LN_EPS = 1e-5
RMS_EPS = 1e-6
N_IN = 10768
S5S = 128

O_GQKV, O_GZ, O_GB, O_GA, O_MX, O_MZ, O_MDT = 0, 1536, 2048, 2052, 2056, 3080, 3592
O_HQ, O_HF, O_HI, O_HZ, O_SU, O_SZ, O_GATE = 3600, 4112, 4624, 5136, 5648, 6160, 6672

PP_CWG, PP_CWM, PP_CBM, PP_GNW, PP_MD, PP_MNW, PP_HNW, PP_SD, PP_SAR, PP_SAI, PP_SLDT = \
    0, 48, 80, 88, 89, 93, 97, 98, 102, 118, 134
NPP = 150
PR_A12, PR_B12, PR_LNG, PR_LNB = 0, 12, 24, 24 + 1024
NPR = 24 + 2048
CH = 128
PHASES = os.environ.get('KPHASES', 'gdn,ssd,hg,s5').split(',')


class V:
    __slots__ = ('ap', 'keys')

    def __init__(self, ap, keys):
        self.ap = ap
        self.keys = keys


class Buf:
    def __init__(self, h, name, ch=CH):
        self.h = h
        self.name = name
        self.ch = ch

    def keys(self, off, n):
        return [(self.name, c) for c in range(off // self.ch, (off + n - 1) // self.ch + 1)]

    def r(self, off, n, p0=0, p1=128):
        return V(self.h[p0:p1, off:off + n], self.keys(off, n))

    def r3(self, off, a, s, p0=0, p1=128):
        return V(self.h[p0:p1, off:off + a * s].rearrange("p (a s) -> p a s", s=s), self.keys(off, a * s))

    def rs(self, off, a, stride, lo, n, p0=0, p1=128):
        return V(self.h[p0:p1, off:off + a * stride].rearrange("p (a s) -> p a s", s=stride)[:, :, lo:lo + n],
                 self.keys(off, a * stride))


def f32r(v):
    return V(v.ap.bitcast(F32R), v.keys) if FAST_MM else v


def bcast(v, shape):
    return V(v.ap.to_broadcast(shape), v.keys)


def usq(v, axis):
    return V(v.ap.unsqueeze(axis), v.keys)


class KB:
    ND = 8

    def __init__(self, nc, es):
        self.nc = nc
        self.E = {'pe': nc.tensor, 'dve': nc.vector, 'act': nc.scalar, 'pool': nc.gpsimd, 'sp': nc.sync}
        self.sem = {k: es.enter_context(nc.semaphore("s_" + k)) for k in self.E}
        self.cnt = {k: 0 for k in self.E}
        self.seen = {k: {} for k in self.E}
        self.dsem = [es.enter_context(nc.semaphore("d%d" % i)) for i in range(2 * self.ND)]
        self.dcnt = [0] * (2 * self.ND)
        self.dnext = {'sp': 0, 'pool': 0, 'act': 0}
        self.last_w = {}
        self.readers = {}
        self.nwait = 0
        self.ninst = 0
        self.per_eng = {k: 0 for k in self.E}
        self.base = {}
        self.loop = None
        self.dry = False
        self.tmpreg = {k: self.E[k].alloc_register("kbtmp_" + k) for k in self.E}

    def _wait(self, eng, tok):
        kind, a, v = tok
        if kind == 'c':
            if a == eng and eng == 'pe':
                return
            key = a
            sem = self.sem[a]
        else:
            key = ('d', a)
            sem = self.dsem[a]
        if self.seen[eng].get(key, 0) >= v:
            return
        if not self.dry:
            b = self.base.get(key, 0)
            nk = 0 if self.loop is None else self.loop[1].get(key, 0)
            if nk == 0:
                self.E[eng].wait_ge(sem, b + v)
            else:
                ti, n, start = self.loop
                r = self.tmpreg[eng]
                self.E[eng].reg_mul(r, ti, nk)
                self.E[eng].reg_add(r, r, b - start * nk + v)
                self.E[eng].wait_ge(sem, r)
            self.nwait += 1
        self.seen[eng][key] = v

    def _absval(self, key, v):
        b = self.base.get(key, 0)
        if self.loop is None:
            return b + v
        ti, n, start = self.loop
        nk = n.get(key, 0)
        if nk == 0:
            return b + v
        return ti * nk + (b - start * nk + v)

    def _deps(self, eng, reads, writes):
        for r in reads:
            t = self.last_w.get(r)
            if t is not None:
                self._wait(eng, t)
        for w in writes:
            t = self.last_w.get(w)
            if t is not None:
                self._wait(eng, t)
            for t in self.readers.get(w, ()):
                self._wait(eng, t)

    def _commit(self, tok, reads, writes):
        ws = set(writes)
        for w in writes:
            self.last_w[w] = tok
            self.readers[w] = []
        for r in reads:
            if r in ws:
                continue
            lst = self.readers.setdefault(r, [])
            lst.append(tok)
            if len(lst) > 8:
                d = {}
                for t in lst:
                    d[(t[0], t[1])] = t
                self.readers[r] = list(d.values())

    def op(self, eng, fn, reads=(), writes=()):
        self._deps(eng, reads, writes)
        self.cnt[eng] += 1
        if not self.dry:
            ins = fn(self.E[eng])
            ins.then_inc(self.sem[eng], 1)
            self.ninst += 1
            self.per_eng[eng] += 1
        self._commit(('c', eng, self.cnt[eng]), reads, writes)

    def dma(self, q, out, in_, reads=(), writes=()):
        self._deps(q, reads, writes)
        i = self.dnext[q] % self.ND + (self.ND if q == 'pool' else 0)
        self.dnext[q] += 1
        if self.dcnt[i] > 0:
            self._wait(q, ('d', i, self.dcnt[i]))
        if not self.dry:
            self.E[q].dma_start(out=out, in_=in_).then_inc(self.dsem[i], 16)
            self.ninst += 1
        self.dcnt[i] += 16
        self._commit(('d', i, self.dcnt[i]), reads, writes)

    def reset(self):
        for k in self.E:
            self.cnt[k] = 0
            self.seen[k] = {}
        for i in range(len(self.dcnt)):
            self.dcnt[i] = 0
        for q in self.dnext:
            self.dnext[q] = 0
        self.last_w = {}
        self.readers = {}

    def counts(self):
        n = {k: self.cnt[k] for k in self.E}
        for i in range(len(self.dcnt)):
            n[('d', i)] = self.dcnt[i]
        return n

    def barrier(self):
        self.finish('sp')
        self.cnt['sp'] += 1
        if not self.dry:
            self.E['sp'].sem_inc(self.sem['sp'], 1)
            self.ninst += 1
        for e in self.E:
            if e != 'sp':
                self._wait(e, ('c', 'sp', self.cnt['sp']))
        n = self.counts()
        if self.loop is None and not self.dry:
            for k, v in n.items():
                self.base[k] = self.base.get(k, 0) + v
        self.reset()
        return n

    def enter_loop(self, ti, n, start):
        self.loop = (ti, n, start)

    def exit_loop(self, niter):
        ti, n, start = self.loop
        self.loop = None
        for k, v in n.items():
            self.base[k] = self.base.get(k, 0) + niter * v

    def finish(self, q='sp'):
        for i in range(2 * self.ND):
            if self.dcnt[i] > 0:
                self._wait(q, ('d', i, self.dcnt[i]))
        for e in self.E:
            if e != q and self.cnt[e] > 0:
                self._wait(q, ('c', e, self.cnt[e]))


class G:
    def __init__(self, nc, es):
        self.nc = nc
        self.es = es
        self.kb = KB(nc, es)
        self.ident = None

    def sb(self, name, ncols, dt=F32):
        return Buf(self.es.enter_context(self.nc.sbuf_tensor(name, [128, ncols], dt)), name)

    def ps(self, name, ncols, dt=F32):
        return Buf(self.es.enter_context(self.nc.psum_tensor(name, [128, ncols], dt)), name, ch=512)

    def mm(self, out, lhsT, rhs, start=True, stop=True, r=False):
        la, ra = lhsT.ap, rhs.ap
        if r and FAST_MM:
            la = la.bitcast(F32R)
            ra = ra.bitcast(F32R)
        self.kb.op('pe', lambda e: e.matmul(out.ap, lhsT=la, rhs=ra, start=start, stop=stop),
                   reads=lhsT.keys + rhs.keys, writes=out.keys)

    def tr(self, out, in_, n=128):
        idn = self.ident.r(0, n, 0, n)
        self.kb.op('pe', lambda e: e.transpose(out.ap, in_.ap, idn.ap),
                   reads=in_.keys + idn.keys, writes=out.keys)

    def act(self, out, in_, func, bias=0.0, scale=1.0):
        rd = list(in_.keys)
        b = bias
        if isinstance(bias, V):
            rd += bias.keys
            b = bias.ap
        self.kb.op('act', lambda e: e.activation(out=out.ap, in_=in_.ap, func=func, bias=b, scale=scale),
                   reads=rd, writes=out.keys)

    def tt(self, eng, out, a, b, op):
        self.kb.op(eng, lambda e: e.tensor_tensor(out=out.ap, in0=a.ap, in1=b.ap, op=op),
                   reads=a.keys + b.keys, writes=out.keys)

    def ts(self, eng, out, a, s1, op0, s2=None, op1=None):
        rd = list(a.keys)
        x1, x2 = s1, s2
        if isinstance(s1, V):
            rd += s1.keys
            x1 = s1.ap
        if isinstance(s2, V):
            rd += s2.keys
            x2 = s2.ap
        if op1 is None:
            self.kb.op(eng, lambda e: e.tensor_scalar(out=out.ap, in0=a.ap, scalar1=x1, scalar2=None, op0=op0),
                       reads=rd, writes=out.keys)
        else:
            self.kb.op(eng, lambda e: e.tensor_scalar(out=out.ap, in0=a.ap, scalar1=x1, scalar2=x2, op0=op0, op1=op1),
                       reads=rd, writes=out.keys)

    def stt(self, out, in0, scalar, in1, op0, op1):
        rd = in0.keys + in1.keys
        x = scalar
        if isinstance(scalar, V):
            rd = rd + scalar.keys
            x = scalar.ap
        self.kb.op('dve', lambda e: e.scalar_tensor_tensor(out=out.ap, in0=in0.ap, scalar=x, in1=in1.ap, op0=op0, op1=op1),
                   reads=rd, writes=out.keys)

    def cp(self, eng, out, in_):
        if eng == 'act':
            self.kb.op(eng, lambda e: e.copy(out=out.ap, in_=in_.ap), reads=in_.keys, writes=out.keys)
        else:
            self.kb.op(eng, lambda e: e.tensor_copy(out=out.ap, in_=in_.ap), reads=in_.keys, writes=out.keys)

    def scan(self, out, d0, d1, init):
        rd = d0.keys + d1.keys
        x = init
        if isinstance(init, V):
            rd = rd + init.keys
            x = init.ap
        self.kb.op('dve', lambda e: e.tensor_tensor_scan(out=out.ap, data0=d0.ap, data1=d1.ap, initial=x,
                                                          op0=ALU.mult, op1=ALU.add), reads=rd, writes=out.keys)

    def memset(self, eng, out, val):
        self.kb.op(eng, lambda e: e.memset(out.ap, val), writes=out.keys)

    def asel(self, out, in_, pattern, cmp, fill, base, cm):
        self.kb.op('pool', lambda e: e.affine_select(out=out.ap, in_=in_.ap, pattern=pattern, compare_op=cmp,
                                                     fill=fill, base=base, channel_multiplier=cm),
                   reads=in_.keys, writes=out.keys)

    def rsqrt(self, out, in_, addc):
        self.kb.op('act', lambda e: e.activation(out=out.ap, in_=in_.ap, func=AF.Sqrt, bias=addc, scale=1.0),
                   reads=in_.keys, writes=out.keys)
        self.recip(out, out)

    def recip(self, out, in_):
        self.kb.op('dve', lambda e: e.reciprocal(out=out.ap, in_=in_.ap), reads=in_.keys, writes=out.keys)

    def dma(self, q, out, in_):
        self.kb.dma(q, out.ap, in_.ap, reads=in_.keys, writes=out.keys)


def build(T, depth, seq_out):
    ntiles = (T + TT - 1) // TT
    Tp = ntiles * TT
    pad = Tp - T
    out_row0 = Tp - seq_out
    nc = bass.Bass("TRN2", target_bir_lowering=False)
    L = depth

    def dram(name, shape, kind="ExternalInput"):
        return nc.dram_tensor(name, shape, F32, kind=kind).ap()

    xin = dram("xin", [Tp, D])
    lnin = dram("lnin", [2, D])
    wfm = dram("wfm", [L, 80, 128, KT * 128])
    wsc = dram("wsc", [L, 128, KT * 16])
    witm = dram("witm", [L, 4, 128, KT * 128])
    wglu = dram("wglu", [L, 2, 4, 128, 4 * 128])
    wbr = dram("wbr", [L, 8, 4, 128, 4 * 128])
    wout = dram("wout", [L, KT, 128, D])
    pp_d = dram("pp", [L, 128, NPP])
    pr_d = dram("pr", [L, NPR])
    lbl_d = dram("lbl", [128, 4 * depth])
    s5b_d = dram("s5b", [L, 2, 128, 16 * 128])
    s5c_d = dram("s5c", [L, 2, 128, 16 * 32])
    yout = dram("y", [seq_out, D], kind="ExternalOutput")
    hb = [dram("hbuf%d" % i, [Tp, D], kind="Internal") for i in range(2)]

    def DV(ap, name, blk=0):
        return V(ap, [("dram_" + name, blk)])

    with ExitStack() as es:
        g = G(nc, es)
        kb = g.kb
        sb, ps = g.sb, g.ps
        M, A_, S_ = ALU.mult, ALU.add, ALU.subtract
        ident = sb("ident", 128)
        g.ident = ident
        ones = sb("ones", 128)
        zeros = sb("zeros", 128)
        NEGS = sb("NEGS", 128)
        NEGT = sb("NEGT", 128)
        TRIU = sb("TRIU", 128)
        M01 = sb("M01", 64)
        SEG = sb("SEG", TT)
        g.memset('pool', zeros.r(0, 128), 0.0)
        g.memset('pool', ones.r(0, 128), 1.0)
        g.asel(ident.r(0, 128), zeros.r(0, 128), [[-1, 128]], ALU.not_equal, 1.0, 0, 1)
        g.asel(NEGS.r(0, 128), zeros.r(0, 128), [[-1, 128]], ALU.is_gt, BIG, 0, 1)
        g.asel(NEGT.r(0, 128), zeros.r(0, 128), [[1, 128]], ALU.is_ge, -BIG, 0, -1)
        g.asel(TRIU.r(0, 128), ones.r(0, 128), [[1, 128]], ALU.is_ge, 0.0, 0, -1)
        g.asel(M01.r(0, 64, 0, 64), ones.r(0, 64, 0, 64), [[1, 64]], ALU.is_ge, 0.0, 0, -1)
        g.memset('pool', SEG.r(0, TT), 1.0)
        for c in range(NCH):
            g.memset('pool', SEG.r(c * 64, 1), 0.0)

        NW = 6
        wring = sb("wring", NW * 1024)
        htok = sb("htok", NB * D)
        hT = sb("hT", KT * TT)
        ppt = sb("ppt", NPP)
        prt = sb("prt", NPR)
        lbl = sb("lbl_s", 4 * depth)
        lbt = sb("lbt", depth * 4)
        omlb = sb("omlb", depth * 4)
        negA12 = sb("negA12", 12)
        wsm = sb("wsm", 8)
        halo_g = sb("halo_g", 12 * 3)
        halo_m = sb("halo_m", 8 * 3)
        S_g = sb("S_g", 4 * 128)
        S_m = sb("S_m", 8 * 128)
        S_h = sb("S_h", 4 * 128)
        xs5 = sb("xs5", 32)
        BbT = sb("BbT", 2 * 16 * 128)
        Cpd = sb("Cpd", 2 * 16 * 32)
        costab = sb("costab", 16 * S5S)
        sintab = sb("sintab", 16 * S5S)
        rho = sb("rho", 16)
        Y = [sb("y%d" % i, 4 * TT) for i in range(4)]
        padm = sb("padm", NB)
        scb = sb("scb", 16 * 16)
        bcx = sb("bcx", 8 * 128)
        ebx = sb("ebx", 8 * 128)
        wscb = sb("wscb", KT * 16)
        lnst = sb("lnst", 16)
        rbuf = sb("rbuf", D)
        NAR = 13440
        AR = sb("AR", NAR)
        YSB = sb("ysb", 4 * TT)
        MIXB = sb("mixb", 8 * TT)
        PS = [ps("ps%d" % i, 512) for i in range(8)]
        if len(PHASES) < 4:
            for yb_ in Y:
                for q_ in range(8):
                    g.cp('pool', f32r(yb_.r(q_ * 128, 128)), zeros.r(0, 128))

        stream = []

        def layer_stream(l):
            it = []
            for ct in range(16):
                it.append(("gdn%d" % ct, wfm[l, ct, :, :], 1024))
            for ct in range(12):
                it.append(("m2_%d" % ct, wfm[l, 16 + ct, :, :], 1024))
            for ct in range(8):
                it.append(("hgqf%d" % ct, wfm[l, 28 + ct, :, :], 1024))
            for ct in range(4):
                it.append(("hgi%d" % ct, witm[l, ct, :, :], 1024))
            for ct in range(4):
                it.append(("hgz%d" % ct, wfm[l, 36 + ct, :, :], 1024))
            for ct in range(8):
                it.append(("s5_%d" % ct, wfm[l, 40 + ct, :, :], 1024))
            for ct in range(4):
                for j in range(2):
                    it.append(("glu%d_%d" % (j, ct), wglu[l, j, ct, :, :], 512))
            for dt in range(8):
                for b in range(4):
                    it.append(("gate%d_%d" % (b, dt), wfm[l, 48 + b * 8 + dt, :, :], 1024))
                    it.append(("wbr%d_%d" % (b, dt), wbr[l, dt, b, :, :], 512))
            for k in range(KT):
                it.append(("wout%d" % k, wout[l, k, :, :], 1024))
            pref = []
            if 'gdn' not in PHASES:
                pref.append('gdn')
            if 'ssd' not in PHASES:
                pref.append('m2_')
            if 'hg' not in PHASES:
                pref.append('hg')
            if 's5' not in PHASES:
                pref += ['s5_', 'glu']
            it = [x for x in it if not any(x[0].startswith(q) for q in pref)]
            return it

        wst = {'issued': 0, 'used': 0}
        PF = 4

        def w_begin(l):
            stream[:] = layer_stream(l)
            wst['issued'] = 0
            wst['used'] = 0

        def w_issue():
            i = wst['issued']
            key, src, ncols = stream[i]
            slot = i % NW
            g.dma('pool', f32r(wring.r(slot * 1024, ncols)), DV(src, "w"))
            wst['issued'] += 1

        def w_get(key):
            i = wst['used']
            assert stream[i][0] == key, (stream[i][0], key)
            while wst['issued'] <= min(i + PF, len(stream) - 1):
                w_issue()
            wst['used'] += 1
            return (i % NW) * 1024

        def ln_rows(dst, src_sb, gam, bet):
            for c in range(2):
                kb.op('dve', lambda e: e.bn_stats(out=lnst.h[:, c * 6:(c + 1) * 6], in_=src_sb.ap[:, c * 512:(c + 1) * 512]),
                      reads=src_sb.keys, writes=lnst.keys(0, 16))
            kb.op('dve', lambda e: e.bn_aggr(out=lnst.h[:, 12:14], in_=lnst.h[:, 0:12].rearrange("p (c s) -> p c s", c=2)),
                  reads=lnst.keys(0, 16), writes=lnst.keys(0, 16))
            g.rsqrt(lnst.r(14, 1), lnst.r(13, 1), LN_EPS)
            g.ts('dve', dst, src_sb, lnst.r(12, 1), S_, lnst.r(14, 1), M)
            g.tt('pool', dst, dst, gam, M)
            g.tt('pool', dst, dst, bet, A_)

        def rmsnorm_fm(yv, ntile, wcol, scale_n, tmp_off, psb):
            pass

        if pad > 0:
            g.memset('dve', AR.r(0, D), 0.0)
            r = 0
            while r < pad:
                n = min(128, pad - r)
                for i in range(2):
                    g.dma('sp', DV(hb[i][r:r + n, :], "hb%d" % i), AR.r(0, D, 0, n))
                r += n

        g.dma('sp', prt.r(0, D), DV(lnin[0:1, :].to_broadcast([128, D]), "lnin"))
        g.dma('sp', prt.r(D, D), DV(lnin[1:2, :].to_broadcast([128, D]), "lnin"))
        for t in range(ntiles):
            for b in range(NB):
                r0 = t * TT + b * 128
                lo = max(0, pad - r0)
                if lo >= 128:
                    continue
                j = b % NB
                g.dma('sp', htok.r(j * D, D), DV(xin[r0:r0 + 128, :], "xin"))
                ln_rows(rbuf.r(0, D), htok.r(j * D, D), prt.r(0, D), prt.r(D, D))
                g.dma('pool', DV(hb[0][r0 + lo:r0 + 128, :], "hb0"), rbuf.r(0, D, lo, 128))

        g.dma('sp', lbl.r(0, 4 * depth), DV(lbl_d[:, :], "lbl"))
        g.act(lbl.r(0, 4 * depth), lbl.r(0, 4 * depth), AF.Exp)
        kb.op('dve', lambda e: e.tensor_reduce(out=AR.h[:, 0:4], in_=lbl.h[:, :].rearrange("p (c l) -> p c l", l=depth),
                                               axis=AX.X, op=ALU.add), reads=lbl.keys(0, 4 * depth), writes=AR.keys(0, 4))
        g.recip(AR.r(4, 4), AR.r(0, 4))
        g.memset('dve', lbt.r(0, 4), 0.0)
        for l in range(1, depth):
            ev = V(lbl.h[:, :].rearrange("p (c l) -> p c l", l=depth)[:, :, l], lbl.keys(0, 4 * depth))
            g.tt('dve', lbt.r(l * 4, 4), lbt.r((l - 1) * 4, 4), ev, A_)
        for l in range(1, depth):
            g.tt('dve', lbt.r(l * 4, 4), lbt.r(l * 4, 4), AR.r(4, 4), M)
        g.ts('dve', omlb.r(0, 4 * depth), lbt.r(0, 4 * depth), -1.0, M, 1.0, A_)

        for l in range(L):
            src = hb[l % 2]
            srcn = "hb%d" % (l % 2)
            dst = hb[(l + 1) % 2]
            dstn = "hb%d" % ((l + 1) % 2)
            last = (l == L - 1)
            g.dma('sp', ppt.r(0, NPP), DV(pp_d[l, :, :], "pp"))
            g.dma('sp', wscb.r(0, KT * 16), DV(wsc[l, :, :], "wsc"))
            g.dma('sp', prt.r(0, NPR), DV(pr_d[l:l + 1, :].to_broadcast([128, NPR]), "pr"))
            g.act(negA12.r(0, 12), prt.r(PR_A12, 12), AF.Exp)
            g.ts('dve', negA12.r(0, 12), negA12.r(0, 12), -1.0, M)
            g.ts('dve', wsm.r(0, 1), ppt.r(PP_GNW, 1), math.sqrt(128.0), M)
            g.ts('dve', wsm.r(1, 1), ppt.r(PP_HNW, 1), math.sqrt(128.0), M)
            g.ts('dve', wsm.r(2, 4), ppt.r(PP_MNW, 4), 16.0, M)
            for st_, n_ in ((halo_g, 36), (halo_m, 24), (S_g, 512), (S_m, 1024), (S_h, 512), (xs5, 32)):
                g.memset('pool', st_.r(0, n_), 0.0)
            a_re = ppt.r(PP_SAR, 16)
            a_im = ppt.r(PP_SAI, 16)

            def sl(i):
                return AR.r(8192 + i * 16, 16)
            DT_, ARD, ANG, T1, T2, COS, SIN, LRE, LIM, DEN, ZRE, ZIM = range(12)
            g.act(sl(DT_), ppt.r(PP_SLDT, 16), AF.Exp)
            g.tt('dve', sl(ARD), a_re, sl(DT_), M)
            g.act(rho.r(0, 16), sl(ARD), AF.Exp)
            g.tt('dve', sl(ANG), a_im, sl(DT_), M)
            g.act(sl(SIN), sl(ANG), AF.Sin, scale=1.0 / 32.0)
            g.ts('dve', sl(T2), sl(ANG), 1.0 / 32.0, M, 0.5 * math.pi, A_)
            g.act(sl(COS), sl(T2), AF.Sin)
            for _ in range(5):
                g.tt('dve', sl(T1), sl(COS), sl(COS), M)
                g.tt('dve', sl(T2), sl(SIN), sl(SIN), M)
                g.stt(sl(SIN), sl(SIN), 2.0, sl(COS), M, M)
                g.tt('dve', sl(COS), sl(T1), sl(T2), S_)
            g.tt('dve', sl(LRE), rho.r(0, 16), sl(COS), M)
            g.tt('dve', sl(LIM), rho.r(0, 16), sl(SIN), M)
            g.tt('dve', sl(DEN), a_re, a_re, M)
            g.tt('dve', sl(T1), a_im, a_im, M)
            g.tt('dve', sl(DEN), sl(DEN), sl(T1), A_)
            g.recip(sl(DEN), sl(DEN))
            g.ts('dve', sl(LRE), sl(LRE), -1.0, A_)
            g.tt('dve', sl(T1), sl(LRE), a_re, M)
            g.tt('dve', sl(T2), sl(LIM), a_im, M)
            g.tt('dve', sl(ZRE), sl(T1), sl(T2), A_)
            g.tt('dve', sl(ZRE), sl(ZRE), sl(DEN), M)
            g.tt('dve', sl(T1), sl(LIM), a_re, M)
            g.tt('dve', sl(T2), sl(LRE), a_im, M)
            g.tt('dve', sl(ZIM), sl(T1), sl(T2), S_)
            g.tt('dve', sl(ZIM), sl(ZIM), sl(DEN), M)
            g.cp('dve', costab.rs(0, 16, S5S, 0, 1), usq(sl(COS), 2))
            g.cp('dve', sintab.rs(0, 16, S5S, 0, 1), usq(sl(SIN), 2))
            n = 1
            while n < S5S:
                cn = bcast(costab.rs(0, 16, S5S, n - 1, 1), [128, 16, n])
                sn = bcast(sintab.rs(0, 16, S5S, n - 1, 1), [128, 16, n])
                t1 = AR.r3(8448, 16, n)
                t2 = AR.r3(9472, 16, n)
                c0 = costab.rs(0, 16, S5S, 0, n)
                s0 = sintab.rs(0, 16, S5S, 0, n)
                g.tt('dve', t1, c0, cn, M)
                g.tt('dve', t2, s0, sn, M)
                g.tt('dve', costab.rs(0, 16, S5S, n, n), t1, t2, S_)
                g.tt('dve', t1, s0, cn, M)
                g.tt('dve', t2, c0, sn, M)
                g.tt('dve', sintab.rs(0, 16, S5S, n, n), t1, t2, A_)
                n *= 2
            Bre = AR.r3(0, 16, 128)
            Bim = AR.r3(2048, 16, 128)
            X1 = AR.r3(4096, 16, 128)
            X2 = AR.r3(6144, 16, 128)
            g.dma('sp', AR.r(0, 2048), DV(s5b_d[l, 0, :, :], "s5b"))
            g.dma('sp', AR.r(2048, 2048), DV(s5b_d[l, 1, :, :], "s5b"))
            zre_b = bcast(usq(sl(ZRE), 2), [128, 16, 128])
            zim_b = bcast(usq(sl(ZIM), 2), [128, 16, 128])
            g.tt('dve', X1, Bre, zre_b, M)
            g.tt('dve', X2, Bim, zim_b, M)
            g.tt('dve', X1, X1, X2, S_)
            g.tt('dve', X2, Bim, zre_b, M)
            g.tt('dve', Bim, Bre, zim_b, M)
            g.tt('dve', X2, X2, Bim, A_)
            for ri, xo in enumerate((4096, 6144)):
                for pr4 in range(4):
                    pst = PS[pr4 % 2]
                    for q in range(4):
                        pr_ = pr4 * 4 + q
                        g.tr(pst.r(q * 128, 128), AR.r(xo + pr_ * 128, 128))
                    g.cp('act', BbT.r(ri * 2048 + pr4 * 512, 512), pst.r(0, 512))
            g.dma('sp', Cpd.r(0, 512), DV(s5c_d[l, 0, :, :], "s5c"))
            g.dma('sp', Cpd.r(512, 512), DV(s5c_d[l, 1, :, :], "s5c"))
            g.ts('pool', Cpd.r(512, 512), Cpd.r(512, 512), -1.0, M)

            def rows(ap, r0, n):
                if isinstance(r0, int):
                    return ap[r0:r0 + n, :]
                return ap[bass.ds(r0, n), :]

            def tile_body(t, dyn):
                tok0 = t * TT
                has_pad = (not dyn) and (tok0 < pad)
                w_begin(l)
                for b in range(NB):
                    r0 = tok0 + b * 128
                    g.dma('sp', htok.r(b * D, D), DV(rows(src, r0, 128), srcn))
                for k in range(KT):
                    pst = PS[k % 2]
                    for b in range(NB):
                        g.tr(pst.r(b * 128, 128), htok.r(b * D + k * 128, 128))
                    g.cp('act' if k % 2 == 0 else 'dve', f32r(hT.r(k * TT, TT)), pst.r(0, TT))
                if has_pad:
                    for b in range(NB):
                        lo = pad - (tok0 + b * 128)
                        g.asel(padm.r(b, 1), ones.r(0, 1), [[0, 1]], ALU.is_ge, 0.0, -lo, 1)

                pp_state = {'i': 0}
                W3 = TT + 3

                def proj_fm(key):
                    wo = w_get(key)
                    pst = PS[pp_state['i'] % 2]
                    pp_state['i'] += 1
                    for k in range(KT):
                        g.mm(pst.r(0, TT), wring.r(wo + k * 128, 128), hT.r(k * TT, TT), start=(k == 0), stop=(k == KT - 1), r=True)
                    return pst.r(0, TT)

                def block_scalars(b, c0, nh):
                    R = lambda i, n=nh, o=0: scb.r(i * 16 + o, n)
                    pst = PS[2]
                    for k in range(KT):
                        g.mm(pst.r(0, 16), hT.r(k * TT + b * 128, 128), wscb.r(k * 16, 16),
                             start=(k == 0), stop=(k == KT - 1))
                    g.cp('dve', R(0, 16), pst.r(0, 16))
                    if c0 == 4:
                        g.act(R(1, 4), R(0, 4), AF.Sigmoid)
                    bo = PR_B12 + (0 if c0 == 4 else 4)
                    ao = 0 if c0 == 4 else 4
                    g.tt('dve', R(2), R(0, nh, c0), prt.r(bo, nh), A_)
                    g.act(R(3), R(2), AF.Abs)
                    g.act(R(3), R(3), AF.Exp, scale=-1.0)
                    g.act(R(3), R(3), AF.Ln, bias=1.0)
                    g.ts('dve', R(2), R(2), 0.0, ALU.max)
                    g.tt('dve', R(4), R(2), R(3), A_)
                    if has_pad and c0 == 8:
                        g.ts('dve', R(4), R(4), padm.r(b, 1), M)
                    g.tt('dve', R(5), R(4), negA12.r(ao, nh), M)
                    g.mm(pst.r(16, nh), TRIU.r(0, 128), R(5))
                    g.cp('dve', R(6), pst.r(16, nh))
                    Rv = AR.r3(11392, nh, 128)
                    g.tt('pool', Rv, bcast(usq(TRIU.r(0, 128), 1), [128, nh, 128]), bcast(usq(R(5), 2), [128, nh, 128]), M)
                    for q in range(nh // 4):
                        pq = PS[3]
                        g.mm(pq.r(0, 512), ones.r(0, 128), AR.r(11392 + q * 512, 512))
                        g.cp('act', bcx.r(q * 512, 512), pq.r(0, 512))
                    g.act(ebx.r(0, nh * 128), bcx.r(0, nh * 128), AF.Exp)
                    g.act(R(7), R(6), AF.Exp)
                    last = bcx.rs(0, nh, 128, 127, 1)
                    g.tt('dve', usq(R(8), 2), last, usq(R(6), 2), S_)
                    g.act(R(8), R(8), AF.Exp)
                    if c0 == 4:
                        g.tt('dve', R(9), R(1, 4), R(7), M)
                    else:
                        g.tt('dve', R(10), R(4), R(8), M)
                    return R

                def conv_fm(nct, xc_off, acc_off, halo, cw_off, cb_off):
                    W3 = TT + 3
                    g.cp('pool', AR.rs(xc_off, nct, W3, 0, 3), halo.r3(0, nct, 3))
                    for ct in range(nct):
                        xo = xc_off + ct * W3
                        acc = AR.r(acc_off + ct * TT, TT)
                        cw = lambda j: ppt.r(cw_off + ct * 4 + j, 1)
                        if cb_off is None:
                            g.ts('dve', acc, AR.r(xo + 3, TT), cw(3), M)
                        else:
                            g.ts('dve', acc, AR.r(xo + 3, TT), cw(3), M, ppt.r(cb_off + ct, 1), A_)
                        for j in range(3):
                            g.stt(acc, AR.r(xo + j, TT), cw(j), acc, M, A_)
                        g.act(acc, acc, AF.Silu)
                    g.cp('pool', halo.r3(0, nct, 3), AR.rs(xc_off, nct, W3, TT, 3))

                def rms_fm(ytiles, wcols, neps, zoffs, sq_off, rn_off):
                    pst = PS[3]
                    for i, yv in enumerate(ytiles):
                        sq = AR.r(sq_off, TT)
                        g.tt('pool', sq, yv, yv, M)
                        g.mm(pst.r(0, TT), ones.r(0, 128), sq, start=(i == 0), stop=(i == len(ytiles) - 1))
                    rn = AR.r(rn_off, TT)
                    g.rsqrt(rn, pst.r(0, TT), neps)
                    for yv, wc in zip(ytiles, wcols):
                        g.stt(yv, yv, wc, rn, M, M)

                if 'gdn' in PHASES:
                    XC, QKV, ZS = 0, 3200, 6272
                    W3 = TT + 3
                    for ct in range(12):
                        pv = proj_fm("gdn%d" % ct)
                        g.cp('act', AR.r(XC + ct * W3 + 3, TT), pv)
                    conv_fm(12, XC, QKV, halo_g, PP_CWG, None)
                    for ct in range(4):
                        pv = proj_fm("gdn%d" % (12 + ct))
                        g.act(AR.r(ZS + ct * TT, TT), pv, AF.Silu)
                    for ct in range(8):
                        x = AR.r(QKV + ct * TT, TT)
                        sq = AR.r(9600, TT)
                        g.tt('pool', sq, x, x, M)
                        pst = PS[3]
                        g.mm(pst.r(0, TT), ones.r(0, 128), sq)
                        rn = AR.r(9856, TT)
                        g.rsqrt(rn, pst.r(0, TT), RMS_EPS)
                        if ct < 4:
                            g.stt(x, rn, 128.0 ** -0.5, x, M, M)
                        else:
                            g.tt('dve', x, x, rn, M)
                    OSB = 10112
                    TA, Dm, Am, Bm, QKT, VB, KBG, KDEC, Um, WTm, VNEW, QDEC = [7296 + i * 128 for i in range(12)]
                    PQ = [8832 + 128 * i for i in range(4)]
                    RR = [9344, 9472]
                    for b in range(NB):
                        R = block_scalars(b, 4, 4)
                        for h in range(4):
                            qT = AR.r(QKV + h * TT + b * 128, 128)
                            kT = AR.r(QKV + (4 + h) * TT + b * 128, 128)
                            vT = AR.r(QKV + (8 + h) * TT + b * 128, 128)
                            gcb = bcx.r(h * 128, 128)
                            gcc = R(6, 1, h)
                            pa = PS[4]
                            g.mm(pa.r(0, 128), kT, kT)
                            g.mm(pa.r(128, 128), kT, qT)
                            g.stt(AR.r(TA, 128), gcb, gcc, NEGS.r(0, 128), S_, A_)
                            g.act(AR.r(Dm, 128), AR.r(TA, 128), AF.Exp, scale=-1.0)
                            g.stt(AR.r(Am, 128), AR.r(Dm, 128), R(1, 1, h), pa.r(0, 128), M, M)
                            g.stt(AR.r(TA, 128), gcb, gcc, NEGT.r(0, 128), S_, A_)
                            g.act(AR.r(Dm, 128), AR.r(TA, 128), AF.Exp)
                            g.tt('dve', AR.r(QKT, 128), AR.r(Dm, 128), pa.r(128, 128), M)
                            g.tr(pa.r(256, 128), AR.r(Am, 128))
                            g.cp('act', AR.r(Bm, 128), pa.r(256, 128))
                            g.tr(pa.r(384, 128), kT)
                            g.ts('dve', AR.r(KBG, 128), pa.r(384, 128), R(9, 1, h), M)
                            g.ts('dve', AR.r(KDEC, 128), pa.r(384, 128), R(8, 1, h), M)
                            pb = PS[5]
                            g.tr(pb.r(0, 128), vT)
                            g.ts('dve', AR.r(VB, 128), pb.r(0, 128), R(1, 1, h), M)
                            g.tt('pool', AR.r(RR[0], 128), ident.r(0, 128), AR.r(Bm, 128), S_)
                            Pc, Qc = Am, Bm
                            ri = 0
                            for lev in range(1, 7):
                                Pn, Qn = PQ[(lev % 2) * 1], PQ[2 + (lev % 2) * 1]
                                g.mm(pb.r(128, 128), AR.r(Qc, 128), AR.r(Pc, 128))
                                g.cp('act', AR.r(Pn, 128), pb.r(128, 128))
                                if lev < 6:
                                    g.mm(pb.r(256, 128), AR.r(Pc, 128), AR.r(Qc, 128))
                                    g.cp('dve', AR.r(Qn, 128), pb.r(256, 128))
                                g.mm(pb.r(384, 128), AR.r(Pn, 128), AR.r(RR[ri], 128))
                                g.tt('dve', AR.r(RR[1 - ri], 128), AR.r(RR[ri], 128), pb.r(384, 128), A_)
                                ri = 1 - ri
                                Pc, Qc = Pn, Qn
                            TTm = AR.r(RR[ri], 128)
                            pc = PS[6]
                            g.mm(pc.r(0, 128), TTm, AR.r(VB, 128))
                            g.cp('act', AR.r(Um, 128), pc.r(0, 128))
                            g.mm(pc.r(128, 128), AR.r(KBG, 128), TTm)
                            g.cp('act', AR.r(WTm, 128), pc.r(128, 128))
                            Sv = S_g.r(h * 128, 128)
                            g.mm(pc.r(256, 128), AR.r(WTm, 128), Sv)
                            g.tt('dve', AR.r(VNEW, 128), AR.r(Um, 128), pc.r(256, 128), S_)
                            g.tt('pool', AR.r(QDEC, 128), qT, ebx.r(h * 128, 128), M)
                            po = PS[7]
                            ov = po.r((h % 2) * 256 + b * 128, 128)
                            g.mm(ov, Sv, AR.r(QDEC, 128), start=True, stop=False)
                            g.mm(ov, AR.r(VNEW, 128), AR.r(QKT, 128), start=False, stop=True)
                            g.cp('act', AR.r(OSB + h * TT + b * 128, 128), ov)
                            g.mm(pc.r(384, 128), AR.r(KDEC, 128), AR.r(VNEW, 128))
                            elast = ebx.r(h * 128 + 127, 1)
                            g.stt(Sv, Sv, elast, pc.r(384, 128), M, A_)
                    for h in range(4):
                        ov = AR.r(OSB + h * TT, TT)
                        rms_fm([ov], [wsm.r(0, 1)], 128.0 * RMS_EPS, None, 9600, 9856)
                        g.tt('pool', f32r(Y[0].r(h * TT, TT)), ov, AR.r(ZS + h * TT, TT), M)

                if 'ssd' in PHASES:
                    XC, XBC, ZS = 0, 3200, 6272
                    for ct in range(8):
                        pv = proj_fm("m2_%d" % ct)
                        g.cp('act', AR.r(XC + ct * W3 + 3, TT), pv)
                    conv_fm(8, XC, XBC, halo_m, PP_CWM, PP_CBM)
                    for ct in range(4):
                        pv = proj_fm("m2_%d" % (8 + ct))
                        g.act(AR.r(ZS + ct * TT, TT), pv, AF.Silu)
                    XDTP, XDEC, BTOK, CBT, DTm, CTD, TMPS = 7296, 8320, 8832, 9088, 9344, 10368, 12416
                    g.memset('pool', AR.r(XDTP, 1024), 0.0)
                    YPS = [PS[6], PS[7]]
                    for b in range(NB):
                        R = block_scalars(b, 8, 8)
                        px = PS[4]
                        for c in range(4):
                            g.tr(px.r(c * 128, 128), AR.r(XBC + c * TT + b * 128, 128))
                        pbk = PS[5]
                        for gi in range(2):
                            g.tr(pbk.r(gi * 128, 128), AR.r(XBC + (4 + gi) * TT + b * 128, 128))
                        g.cp('act', AR.r(BTOK, 256), pbk.r(0, 256))
                        for par in range(2):
                            outv = V(AR.h[:, XDTP:XDTP + 1024].rearrange("p (c q) -> p c q", q=256)[:, :, par * 192: par * 192 + 64],
                                     AR.keys(XDTP, 1024))
                            inv = V(px.h[:, 0:512].rearrange("p (c q) -> p c q", q=128)[:, :, par * 64:(par + 1) * 64], px.keys(0, 512))
                            sp_ = V(scb.h[:, 4 * 16:4 * 16 + 8].rearrange("p (c q) -> p c q", q=2)[:, :, par:par + 1].to_broadcast([128, 4, 64]),
                                    scb.keys(64, 8))
                            g.tt('dve', outv, inv, sp_, M)
                        g.tt('dve', AR.r3(XDEC, 8, 64), px.r3(0, 8, 64), bcast(usq(R(10), 2), [128, 8, 64]), M)
                        for gi in range(2):
                            g.mm(pbk.r(256 + gi * 128, 128), AR.r(XBC + (4 + gi) * TT + b * 128, 128),
                                 AR.r(XBC + (6 + gi) * TT + b * 128, 128))
                        g.cp('act', AR.r(CBT, 256), pbk.r(256, 256))
                        g.tt('dve', AR.r3(TMPS, 8, 128), bcx.r3(0, 8, 128), bcast(usq(R(6), 2), [128, 8, 128]), S_)
                        g.tt('pool', AR.r3(TMPS, 8, 128), AR.r3(TMPS, 8, 128), bcast(usq(NEGT.r(0, 128), 1), [128, 8, 128]), A_)
                        g.act(AR.r(DTm, 1024), AR.r(TMPS, 1024), AF.Exp)
                        for gi in range(2):
                            g.tt('pool', AR.r3(DTm + gi * 512, 4, 128), AR.r3(DTm + gi * 512, 4, 128),
                                 bcast(usq(AR.r(CBT + gi * 128, 128), 1), [128, 4, 128]), M)
                            g.tt('pool', AR.r3(CTD + gi * 512, 4, 128), ebx.r3(gi * 512, 4, 128),
                                 bcast(usq(AR.r(XBC + (6 + gi) * TT + b * 128, 128), 1), [128, 4, 128]), M)
                        for c in range(4):
                            yv = YPS[c // 2].r((c % 2) * 256 + b * 128, 128)
                            for hh in range(2):
                                h = 2 * c + hh
                                g.mm(yv, AR.r(XDTP + h * 128, 128), AR.r(DTm + h * 128, 128), start=(hh == 0), stop=False)
                                g.mm(yv, S_m.r(h * 128, 128), AR.r(CTD + h * 128, 128), start=False, stop=(hh == 1))
                        pu = PS[3]
                        for gi in range(2):
                            g.mm(pu.r(gi * 256, 256), AR.r(BTOK + gi * 128, 128), AR.r(XDEC + gi * 256, 256))
                        for par in range(2):
                            sv = V(S_m.h[:, :].rearrange("p (c q) -> p c q", q=256)[:, :, par * 192: par * 192 + 64], S_m.keys(0, 1024))
                            el = V(ebx.h[:, :].rearrange("p (c q) -> p c q", q=256)[:, :, par * 128 + 127: par * 128 + 128].to_broadcast([128, 4, 64]),
                                   ebx.keys(0, 1024))
                            uv = V(pu.h[:, 0:512].rearrange("p (c q) -> p c q", q=128)[:, :, par * 64:(par + 1) * 64], pu.keys(0, 512))
                            g.tt('pool', sv, sv, el, M)
                            g.tt('dve', sv, sv, uv, A_)
                    for c in range(4):
                        yv = AR.r(c * TT, TT)
                        g.stt(yv, AR.r(XBC + c * TT, TT), ppt.r(PP_MD + c, 1), YPS[c // 2].r((c % 2) * 256, TT), M, A_)
                        g.tt('pool', yv, yv, AR.r(ZS + c * TT, TT), M)
                    for gi in range(2):
                        tiles = [AR.r((2 * gi + i) * TT, TT) for i in range(2)]
                        rms_fm(tiles, [wsm.r(2 + 2 * gi + i, 1) for i in range(2)], 256.0 * RMS_EPS, None, 9600, 9856)
                        for i in range(2):
                            g.cp('pool', f32r(Y[1].r((2 * gi + i) * TT, TT)), tiles[i])

                if 'hg' in PHASES:
                    QH, Fo, LOGF, GC, EG, QD, KD, KK, KDE, ZS, ITOK, ATM, KDT = \
                        0, 1024, 2048, 3072, 4096, 5120, 6144, 7168, 8192, 9216, 10240, 12288, 12544
                    for ct in range(4):
                        pv = proj_fm("hgqf%d" % ct)
                        g.act(AR.r(QH + ct * TT, TT), pv, AF.Silu)
                    for ct in range(4):
                        pv = proj_fm("hgqf%d" % (4 + ct))
                        f = AR.r(Fo + ct * TT, TT)
                        g.act(f, pv, AF.Sigmoid)
                        g.ts('dve', f, f, omlb.r(l * 4 + ct, 1), M, lbt.r(l * 4 + ct, 1), A_)
                        g.act(AR.r(LOGF + ct * TT, TT), f, AF.Ln)
                        g.ts('pool', AR.r(KK + ct * TT, TT), f, -1.0, M, 1.0, A_)
                        g.scan(AR.r(GC + ct * TT, TT), SEG.r(0, TT), AR.r(LOGF + ct * TT, TT), 0.0)
                    g.act(AR.r(EG, 1024), AR.r(GC, 1024), AF.Exp)
                    g.tt('pool', AR.r(QD, 1024), AR.r(QH, 1024), AR.r(EG, 1024), M)
                    g.ts('dve', AR.r(KD, 1024), AR.r(GC, 1024), -80.0, ALU.max)
                    g.act(AR.r(KD, 1024), AR.r(KD, 1024), AF.Exp, scale=-1.0)
                    g.tt('pool', AR.r(KD, 1024), AR.r(KD, 1024), AR.r(KK, 1024), M)
                    for ct in range(4):
                        for c in range(NCH):
                            o = ct * TT + c * 64
                            g.act(AR.r(KDE + o, 64), AR.r(GC + o, 64), AF.Exp, bias=AR.r(GC + o + 63, 1), scale=-1.0)
                    g.tt('pool', AR.r(KDE, 1024), AR.r(KDE, 1024), AR.r(KK, 1024), M)
                    for ct in range(4):
                        wo = w_get("hgi%d" % ct)
                        for c in range(NCH):
                            for k in range(KT):
                                g.mm(PS[4 + c].r(ct * 128, 128, 0, 64), hT.r(k * TT + c * 64, 64), wring.r(wo + k * 128, 128),
                                     start=(k == 0), stop=(k == KT - 1))
                    for c in range(NCH):
                        g.cp('act' if c % 2 == 0 else 'dve', AR.r(ITOK + c * 512, 512, 0, 64), PS[4 + c].r(0, 512, 0, 64))
                    for ct in range(4):
                        pv = proj_fm("hgz%d" % ct)
                        g.act(AR.r(ZS + ct * TT, TT), pv, AF.Silu)
                    OPS = [PS[6], PS[7]]
                    for c in range(NCH):
                        pa = PS[2]
                        for h in range(4):
                            o = h * TT + c * 64
                            g.mm(pa.r(h * 64, 64, 0, 64), AR.r(KD + o, 64), AR.r(QD + o, 64))
                        g.tt('dve', AR.r3(ATM, 4, 64, 0, 64), pa.r3(0, 4, 64, 0, 64), bcast(usq(M01.r(0, 64, 0, 64), 1), [64, 4, 64]), M)
                        pk = PS[3]
                        for h in range(4):
                            o = h * TT + c * 64
                            g.tr(pk.r(h * 128, 128, 0, 64), AR.r(KDE + o, 64))
                        g.cp('act', AR.r(KDT, 512, 0, 64), pk.r(0, 512, 0, 64))
                        for h in range(4):
                            o = h * TT + c * 64
                            iv = AR.r(ITOK + c * 512 + h * 128, 128, 0, 64)
                            Sv = S_h.r(h * 128, 128)
                            ov = OPS[h // 2].r((h % 2) * 256 + c * 64, 64)
                            g.mm(ov, iv, AR.r(ATM + h * 64, 64, 0, 64), start=True, stop=False)
                            g.mm(ov, Sv, AR.r(QD + o, 64), start=False, stop=True)
                            pu = PS[4 + h % 2]
                            g.mm(pu.r(256, 128), AR.r(KDT + h * 128, 128, 0, 64), iv)
                            g.stt(Sv, Sv, AR.r(EG + o + 63, 1), pu.r(256, 128), M, A_)
                    for h in range(4):
                        ov = AR.r(Fo + h * TT, TT)
                        g.cp('act', ov, OPS[h // 2].r((h % 2) * 256, TT))
                        rms_fm([ov], [wsm.r(1, 1)], 128.0 * RMS_EPS, None, LOGF, LOGF + TT)
                        g.tt('pool', f32r(Y[2].r(h * TT, TT)), ov, AR.r(ZS + h * TT, TT), M)

                if 's5' in PHASES:
                    Uo, ZS, T1o, WRE, WIM, ZR, ZI, XRE, XIM, YTOK, YS, GEL = \
                        0, 1024, 2048, 4096, 4608, 5120, 5632, 6144, 6656, 7168, 7680, 8704
                    for ct in range(4):
                        pv = proj_fm("s5_%d" % ct)
                        g.cp('act', AR.r(Uo + ct * TT, TT), pv)
                    for ct in range(4):
                        pv = proj_fm("s5_%d" % (4 + ct))
                        g.act(AR.r(ZS + ct * TT, TT), pv, AF.Silu)
                    for b in range(NB):
                        py = PS[7]
                        for ft in range(4):
                            pre, pim = PS[4], PS[5]
                            uv = AR.r(Uo + ft * TT + b * 128, 128)
                            for q in range(4):
                                pr_ = ft * 4 + q
                                g.mm(pre.r(q * 128, 128), BbT.r(pr_ * 128, 128), uv)
                                g.mm(pim.r(q * 128, 128), BbT.r(2048 + pr_ * 128, 128), uv)
                            ct_ = costab.r(ft * 512, 512)
                            st_ = sintab.r(ft * 512, 512)
                            t = [AR.r(T1o + i * 512, 512) for i in range(4)]
                            g.tt('dve', t[0], pre.r(0, 512), ct_, M)
                            g.tt('dve', t[1], pim.r(0, 512), st_, M)
                            g.tt('dve', t[2], pim.r(0, 512), ct_, M)
                            g.tt('dve', t[3], pre.r(0, 512), st_, M)
                            g.tt('pool', AR.r(WRE, 512), t[0], t[1], A_)
                            g.tt('pool', AR.r(WIM, 512), t[2], t[3], S_)
                            for q in range(4):
                                pr_ = ft * 4 + q
                                rb = bcast(rho.r(pr_, 1), [128, 128])
                                g.scan(AR.r(ZR + q * 128, 128), rb, AR.r(WRE + q * 128, 128), xs5.r(pr_, 1))
                                g.scan(AR.r(ZI + q * 128, 128), rb, AR.r(WIM + q * 128, 128), xs5.r(16 + pr_, 1))
                            g.tt('pool', t[0], AR.r(ZR, 512), ct_, M)
                            g.tt('pool', t[1], AR.r(ZI, 512), st_, M)
                            g.tt('pool', t[2], AR.r(ZI, 512), ct_, M)
                            g.tt('pool', t[3], AR.r(ZR, 512), st_, M)
                            g.tt('pool', AR.r(XRE, 512), t[0], t[1], S_)
                            g.tt('pool', AR.r(XIM, 512), t[2], t[3], A_)
                            g.cp('pool', xs5.r3(ft * 4, 4, 1), AR.rs(XRE, 4, 128, 127, 1))
                            g.cp('pool', xs5.r3(16 + ft * 4, 4, 1), AR.rs(XIM, 4, 128, 127, 1))
                            for q in range(4):
                                pr_ = ft * 4 + q
                                yv = py.r(ft * 128 + q * 32, 32)
                                g.mm(yv, AR.r(XRE + q * 128, 128), Cpd.r(pr_ * 32, 32), start=True, stop=False)
                                g.mm(yv, AR.r(XIM + q * 128, 128), Cpd.r(512 + pr_ * 32, 32), start=False, stop=True)
                        g.cp('act', AR.r(YTOK, 512), py.r(0, 512))
                        pyt = PS[6]
                        for ft in range(4):
                            g.tr(pyt.r(ft * 128, 128), AR.r(YTOK + ft * 128, 128))
                        for ft in range(4):
                            g.stt(AR.r(YS + ft * TT + b * 128, 128), AR.r(Uo + ft * TT + b * 128, 128), ppt.r(PP_SD + ft, 1),
                                  pyt.r(ft * 128, 128), M, A_)
                    x = AR.r(YS, 1024)
                    g1 = AR.r(GEL, 1024)
                    g2 = AR.r(GEL + 1024, 1024)
                    g.tt('pool', g1, x, x, M)
                    g.ts('dve', g1, g1, 0.044715, M, 1.0, A_)
                    g.tt('pool', g1, g1, x, M)
                    g.act(g2, g1, AF.Sigmoid, scale=2.0 * math.sqrt(2.0 / math.pi))
                    g.tt('pool', f32r(YSB.r(0, 4 * TT)), x, g2, M)
                    for ct in range(4):
                        w1 = w_get("glu0_%d" % ct)
                        w2 = w_get("glu1_%d" % ct)
                        p1, p2 = PS[4], PS[5]
                        for k in range(4):
                            g.mm(p1.r(0, TT), wring.r(w1 + k * 128, 128), YSB.r(k * TT, TT), start=(k == 0), stop=(k == 3), r=True)
                        for k in range(4):
                            g.mm(p2.r(0, TT), wring.r(w2 + k * 128, 128), YSB.r(k * TT, TT), start=(k == 0), stop=(k == 3), r=True)
                        sg = AR.r(GEL + ct * TT, TT)
                        g.act(sg, p2.r(0, TT), AF.Sigmoid)
                        g.tt('dve', sg, sg, p1.r(0, TT), M)
                        g.tt('pool', f32r(Y[3].r(ct * TT, TT)), sg, AR.r(ZS + ct * TT, TT), M)

                MIX, SGT, TMPM = 0, 2048, 2304
                for dt in range(8):
                    for bq in range(4):
                        pv = proj_fm("gate%d_%d" % (bq, dt))
                        sg = AR.r(SGT, TT)
                        g.act(sg, pv, AF.Sigmoid)
                        wo = w_get("wbr%d_%d" % (bq, dt))
                        pb_ = PS[2 + bq % 2]
                        for k in range(4):
                            g.mm(pb_.r(0, TT), wring.r(wo + k * 128, 128), Y[bq].r(k * TT, TT), start=(k == 0), stop=(k == 3), r=True)
                        if bq == 0:
                            g.tt('dve', f32r(MIXB.r(dt * TT, TT)), sg, pb_.r(0, TT), M)
                        else:
                            g.tt('dve', AR.r(TMPM, TT), sg, pb_.r(0, TT), M)
                            g.tt('pool', f32r(MIXB.r(dt * TT, TT)), MIXB.r(dt * TT, TT), AR.r(TMPM, TT), A_)
                for k in range(KT):
                    wo = w_get("wout%d" % k)
                    for b in range(NB):
                        for hf in range(2):
                            g.mm(PS[4 + b * 2 + hf].r(0, 512), MIXB.r(k * TT + b * 128, 128), wring.r(wo + hf * 512, 512),
                                 start=(k == 0), stop=(k == KT - 1), r=True)
                for b in range(NB):
                    r0 = tok0 + b * 128
                    for hf in range(2):
                        g.stt(rbuf.r(hf * 512, 512), htok.r(b * D + hf * 512, 512), ALPHA, PS[4 + b * 2 + hf].r(0, 512), M, A_)
                    ln_rows(rbuf.r(0, D), rbuf.r(0, D), prt.r(PR_LNG, D), prt.r(PR_LNB, D))
                    if dyn:
                        if not last:
                            g.dma('pool', DV(rows(dst, r0, 128), dstn), rbuf.r(0, D))
                        else:
                            g.dma('pool', DV(rows(yout, r0 - out_row0, 128), "y"), rbuf.r(0, D))
                        continue
                    lo = max(0, pad - r0)
                    if lo >= 128:
                        continue
                    if not last:
                        g.dma('pool', DV(dst[r0 + lo:r0 + 128, :], dstn), rbuf.r(0, D, lo, 128))
                    else:
                        lo2 = max(0, out_row0 - r0)
                        if lo2 < 128:
                            g.dma('pool', DV(yout[r0 + lo2 - out_row0:r0 + 128 - out_row0, :], "y"), rbuf.r(0, D, lo2, 128))
                assert wst['used'] == len(stream), (wst, len(stream))

            tile_body(0, False)
            kb.barrier()
            if ntiles == 2:
                tile_body(1, False)
                kb.barrier()
            elif ntiles > 2:
                assert out_row0 <= TT
                kb.dry = True
                tile_body(1, True)
                n_it = kb.barrier()
                kb.dry = False
                with nc.Fori(1, ntiles) as ti:
                    kb.enter_loop(ti, n_it, 1)
                    tile_body(ti, True)
                    kb.barrier()
                kb.exit_loop(ntiles - 1)
        kb.barrier()
        print("[build] ninst=%d nwait=%d per_eng=%s" % (kb.ninst, kb.nwait, kb.per_eng), flush=True)
    return nc


def _fm(w, c0, ncol):
    nt = ncol // 128
    x = w[:, c0:c0 + ncol].reshape(KT, 128, nt, 128)
    return np.ascontiguousarray(x.transpose(2, 1, 0, 3)).reshape(nt, 128, KT * 128)


def prep_weights(inp, depth):
    f = lambda a: np.asarray(a, dtype=np.float32)
    L = depth
    w_in = f(inp['w_in'])
    wfm = np.empty((L, 80, 128, KT * 128), np.float32)
    wsc = np.empty((L, 128, KT * 16), np.float32)
    witm = np.empty((L, 4, 128, KT * 128), np.float32)
    for l in range(L):
        w = w_in[l]
        parts = [_fm(w, O_GQKV, 1536), _fm(w, O_GZ, 512), _fm(w, O_MX, 1024), _fm(w, O_MZ, 512),
                 _fm(w, O_HQ, 512), _fm(w, O_HF, 512), _fm(w, O_HZ, 512), _fm(w, O_SU, 512), _fm(w, O_SZ, 512),
                 _fm(w, O_GATE, 4096)]
        wfm[l] = np.concatenate(parts, axis=0)
        sc = np.concatenate([w[:, O_GB:O_GB + 4], w[:, O_GA:O_GA + 4], w[:, O_MDT:O_MDT + 8]], axis=1)
        wsc[l] = sc.reshape(KT, 128, 16).transpose(1, 0, 2).reshape(128, KT * 16)
        witm[l] = _fm(w, O_HI, 512)
    glu = np.stack([f(inp['s5_glu_w1']), f(inp['s5_glu_w2'])], axis=1)
    wglu = np.ascontiguousarray(glu.reshape(L, 2, 4, 128, 4, 128).transpose(0, 1, 4, 3, 2, 5)).reshape(L, 2, 4, 128, 512)
    wb = f(inp['w_branch'])
    wbr = np.ascontiguousarray(wb.reshape(L, 4, 4, 128, 8, 128).transpose(0, 4, 1, 3, 2, 5)).reshape(L, 8, 4, 128, 512)
    wout = np.ascontiguousarray(f(inp['w_out']).reshape(L, KT, 128, D))
    pp = np.zeros((L, 128, NPP), np.float32)
    pr = np.zeros((L, NPR), np.float32)
    for l in range(L):
        pp[l, :, PP_CWG:PP_CWG + 48] = f(inp['gdn_conv_w'])[l].reshape(4, 12, 128).transpose(2, 1, 0).reshape(128, 48)
        pp[l, :, PP_CWM:PP_CWM + 32] = f(inp['m2_conv_w'])[l].reshape(4, 8, 128).transpose(2, 1, 0).reshape(128, 32)
        pp[l, :, PP_CBM:PP_CBM + 8] = f(inp['m2_conv_b'])[l].reshape(8, 128).T
        pp[l, :, PP_GNW] = f(inp['gdn_norm_w'])[l]
        pp[l, :, PP_MD:PP_MD + 4] = np.repeat(f(inp['m2_D'])[l], 64).reshape(4, 128).T
        pp[l, :, PP_MNW:PP_MNW + 4] = f(inp['m2_norm_w'])[l].reshape(4, 128).T
        pp[l, :, PP_HNW] = f(inp['hg_norm_w'])[l]
        pp[l, :, PP_SD:PP_SD + 4] = f(inp['s5_D'])[l].reshape(4, 128).T
        pp[l, :, PP_SAR:PP_SAR + 16] = f(inp['s5_A_re'])[l].reshape(16, 128).T
        pp[l, :, PP_SAI:PP_SAI + 16] = f(inp['s5_A_im'])[l].reshape(16, 128).T
        pp[l, :, PP_SLDT:PP_SLDT + 16] = np.repeat(f(inp['s5_log_dt'])[l], 64).reshape(16, 128).T
        pr[l, PR_A12:PR_A12 + 4] = f(inp['gdn_A_log'])[l]
        pr[l, PR_A12 + 4:PR_A12 + 12] = f(inp['m2_A_log'])[l]
        pr[l, PR_B12:PR_B12 + 4] = f(inp['gdn_dt_bias'])[l]
        pr[l, PR_B12 + 4:PR_B12 + 12] = f(inp['m2_dt_bias'])[l]
        pr[l, PR_LNG:PR_LNG + D] = f(inp['ln_g'])[l]
        pr[l, PR_LNB:PR_LNB + D] = f(inp['ln_b'])[l]
    lbl = np.ascontiguousarray(f(inp['hg_lb_logits']).reshape(L, 4, 128).transpose(2, 1, 0)).reshape(128, 4 * L)
    s5b = np.zeros((L, 2, 128, 16, 128), np.float32)
    s5c = np.zeros((L, 2, 128, 16, 32), np.float32)
    for j, (bn, cn) in enumerate((('s5_B_re', 's5_C_re'), ('s5_B_im', 's5_C_im'))):
        Bm = f(inp[bn])
        Cm = f(inp[cn])
        for pr_ in range(16):
            for g2 in range(2):
                gi = 2 * pr_ + g2
                col = (pr_ % 4) * 32 + g2 * 16
                s5b[:, j, g2 * 64:(g2 + 1) * 64, pr_, col:col + 16] = Bm[:, gi]
                s5c[:, j, g2 * 64:(g2 + 1) * 64, pr_, g2 * 16:(g2 + 1) * 16] = Cm[:, gi].transpose(0, 2, 1)
    lnin = np.stack([f(inp['ln_in_g']), f(inp['ln_in_b'])], axis=0)
    return dict(lnin=lnin, wfm=wfm, wsc=wsc, witm=witm, wglu=wglu, wbr=wbr, wout=wout, pp=pp, pr=pr, lbl=lbl,
                s5b=s5b.reshape(L, 2, 128, 2048), s5c=s5c.reshape(L, 2, 128, 512))


def make_xin(x_b, meta, Tp):
    T = x_b.shape[0] + meta.shape[0]
    xin = np.zeros((Tp, D), np.float32)
    xin[Tp - T:Tp - T + meta.shape[0]] = meta
    xin[Tp - x_b.shape[0]:] = x_b
    return xin


_NC_CACHE = {}


def kernel(**inputs):
    x = np.asarray(inputs['x'], dtype=np.float32)
    Bsz, SEQ, _ = x.shape
    depth = int(np.asarray(inputs['w_in']).shape[0])
    T = SEQ + NMETA
    ntiles = (T + TT - 1) // TT
    Tp = ntiles * TT
    ws = prep_weights(inputs, depth)
    meta = np.asarray(inputs['meta_tokens'], dtype=np.float32)
    key = (T, depth, SEQ)
    if key not in _NC_CACHE:
        _NC_CACHE[key] = build(T, depth, SEQ)
    nc = _NC_CACHE[key]
    in_maps = []
    for b in range(Bsz):
        m = dict(ws)
        m['xin'] = make_xin(x[b], meta, Tp)
        in_maps.append(m)
    res = run_bass_kernel_spmd(nc, in_maps, core_ids=list(range(Bsz)))
    return np.stack([np.asarray(res.results[b]['y'], dtype=np.float32) for b in range(Bsz)], axis=0)
```

```python
import math
import os
import numpy as np
from contextlib import ExitStack
import concourse.bass as bass
import concourse.mybir as mybir
from concourse.bass_utils import run_bass_kernel_spmd

F32 = mybir.dt.float32
F32R = mybir.dt.float32r
FAST_MM = True
POOL_TO = os.environ.get('KPOOLTO', 'dve')
AF = mybir.ActivationFunctionType
ALU = mybir.AluOpType
AX = mybir.AxisListType

D = 1024
TT = 256
NB = TT // 128
NCH = TT // 64
KT = 8
NMETA = 16
BIG = 30000.0
ALPHA = 8.0 ** 0.25
LN_EPS = 1e-5
RMS_EPS = 1e-6
N_IN = 10768
S5S = 128

O_GQKV, O_GZ, O_GB, O_GA, O_MX, O_MZ, O_MDT = 0, 1536, 2048, 2052, 2056, 3080, 3592
O_HQ, O_HF, O_HI, O_HZ, O_SU, O_SZ, O_GATE = 3600, 4112, 4624, 5136, 5648, 6160, 6672

PP_CWG, PP_CWM, PP_CBM, PP_GNW, PP_MD, PP_MNW, PP_HNW, PP_SD, PP_SAR, PP_SAI, PP_SLDT = \
    0, 48, 80, 88, 89, 93, 97, 98, 102, 118, 134
NPP = 150
PR_A12, PR_B12, PR_LNG, PR_LNB = 0, 12, 24, 24 + 1024
NPR = 24 + 2048
CH = 128
PHASES = os.environ.get('KPHASES', 'gdn,ssd,hg,s5').split(',')


class V:
    __slots__ = ('ap', 'keys')

    def __init__(self, ap, keys):
        self.ap = ap
        self.keys = keys


class Buf:
    def __init__(self, h, name, ch=CH):
        self.h = h
        self.name = name
        self.ch = ch

    def keys(self, off, n):
        return [(self.name, c) for c in range(off // self.ch, (off + n - 1) // self.ch + 1)]

    def r(self, off, n, p0=0, p1=128):
        return V(self.h[p0:p1, off:off + n], self.keys(off, n))

    def r3(self, off, a, s, p0=0, p1=128):
        return V(self.h[p0:p1, off:off + a * s].rearrange("p (a s) -> p a s", s=s), self.keys(off, a * s))

    def rs(self, off, a, stride, lo, n, p0=0, p1=128):
        return V(self.h[p0:p1, off:off + a * stride].rearrange("p (a s) -> p a s", s=stride)[:, :, lo:lo + n],
                 self.keys(off, a * stride))


def f32r(v):
    return V(v.ap.bitcast(F32R), v.keys) if FAST_MM else v


def bcast(v, shape):
    return V(v.ap.to_broadcast(shape), v.keys)


def usq(v, axis):
    return V(v.ap.unsqueeze(axis), v.keys)


class KB:
    ND = 8

    def __init__(self, nc, es):
        self.nc = nc
        self.E = {'pe': nc.tensor, 'dve': nc.vector, 'act': nc.scalar, 'pool': nc.gpsimd, 'sp': nc.sync}
        self.sem = {k: es.enter_context(nc.semaphore("s_" + k)) for k in self.E}
        self.cnt = {k: 0 for k in self.E}
        self.seen = {k: {} for k in self.E}
        self.dsem = [es.enter_context(nc.semaphore("d%d" % i)) for i in range(2 * self.ND)]
        self.dcnt = [0] * (2 * self.ND)
        self.dnext = {'sp': 0, 'pool': 0, 'act': 0}
        self.last_w = {}
        self.readers = {}
        self.nwait = 0
        self.ninst = 0
        self.per_eng = {k: 0 for k in self.E}
        self.base = {}
        self.loop = None
        self.dry = False
        self.tmpreg = {k: self.E[k].alloc_register("kbtmp_" + k) for k in self.E}

    def _wait(self, eng, tok):
        kind, a, v = tok
        if kind == 'c':
            if a == eng and eng == 'pe':
                return
            key = a
            sem = self.sem[a]
        else:
            key = ('d', a)
            sem = self.dsem[a]
        if self.seen[eng].get(key, 0) >= v:
            return
        if not self.dry:
            b = self.base.get(key, 0)
            nk = 0 if self.loop is None else self.loop[1].get(key, 0)
            if nk == 0:
                self.E[eng].wait_ge(sem, b + v)
            else:
                ti, n, start = self.loop
                r = self.tmpreg[eng]
                self.E[eng].reg_mul(r, ti, nk)
                self.E[eng].reg_add(r, r, b - start * nk + v)
                self.E[eng].wait_ge(sem, r)
            self.nwait += 1
        self.seen[eng][key] = v

    def _absval(self, key, v):
        b = self.base.get(key, 0)
        if self.loop is None:
            return b + v
        ti, n, start = self.loop
        nk = n.get(key, 0)
        if nk == 0:
            return b + v
        return ti * nk + (b - start * nk + v)

    def _deps(self, eng, reads, writes):
        for r in reads:
            t = self.last_w.get(r)
            if t is not None:
                self._wait(eng, t)
        for w in writes:
            t = self.last_w.get(w)
            if t is not None:
                self._wait(eng, t)
            for t in self.readers.get(w, ()):
                self._wait(eng, t)

    def _commit(self, tok, reads, writes):
        ws = set(writes)
        for w in writes:
            self.last_w[w] = tok
            self.readers[w] = []
        for r in reads:
            if r in ws:
                continue
            lst = self.readers.setdefault(r, [])
            lst.append(tok)
            if len(lst) > 8:
                d = {}
                for t in lst:
                    d[(t[0], t[1])] = t
                self.readers[r] = list(d.values())

    def op(self, eng, fn, reads=(), writes=()):
        self._deps(eng, reads, writes)
        self.cnt[eng] += 1
        if not self.dry:
            ins = fn(self.E[eng])
            ins.then_inc(self.sem[eng], 1)
            self.ninst += 1
            self.per_eng[eng] += 1
        self._commit(('c', eng, self.cnt[eng]), reads, writes)

    def dma(self, q, out, in_, reads=(), writes=()):
        self._deps(q, reads, writes)
        i = self.dnext[q] % self.ND + (self.ND if q == 'pool' else 0)
        self.dnext[q] += 1
        if self.dcnt[i] > 0:
            self._wait(q, ('d', i, self.dcnt[i]))
        if not self.dry:
            self.E[q].dma_start(out=out, in_=in_).then_inc(self.dsem[i], 16)
            self.ninst += 1
        self.dcnt[i] += 16
        self._commit(('d', i, self.dcnt[i]), reads, writes)

    def reset(self):
        for k in self.E:
            self.cnt[k] = 0
            self.seen[k] = {}
        for i in range(len(self.dcnt)):
            self.dcnt[i] = 0
        for q in self.dnext:
            self.dnext[q] = 0
        self.last_w = {}
        self.readers = {}

    def counts(self):
        n = {k: self.cnt[k] for k in self.E}
        for i in range(len(self.dcnt)):
            n[('d', i)] = self.dcnt[i]
        return n

    def barrier(self):
        self.finish('sp')
        self.cnt['sp'] += 1
        if not self.dry:
            self.E['sp'].sem_inc(self.sem['sp'], 1)
            self.ninst += 1
        for e in self.E:
            if e != 'sp':
                self._wait(e, ('c', 'sp', self.cnt['sp']))
        n = self.counts()
        if self.loop is None and not self.dry:
            for k, v in n.items():
                self.base[k] = self.base.get(k, 0) + v
        self.reset()
        return n

    def enter_loop(self, ti, n, start):
        self.loop = (ti, n, start)

    def exit_loop(self, niter):
        ti, n, start = self.loop
        self.loop = None
        for k, v in n.items():
            self.base[k] = self.base.get(k, 0) + niter * v

    def finish(self, q='sp'):
        for i in range(2 * self.ND):
            if self.dcnt[i] > 0:
                self._wait(q, ('d', i, self.dcnt[i]))
        for e in self.E:
            if e != q and self.cnt[e] > 0:
                self._wait(q, ('c', e, self.cnt[e]))


class G:
    def __init__(self, nc, es):
        self.nc = nc
        self.es = es
        self.kb = KB(nc, es)
        self.ident = None

    def sb(self, name, ncols, dt=F32):
        return Buf(self.es.enter_context(self.nc.sbuf_tensor(name, [128, ncols], dt)), name)

    def ps(self, name, ncols, dt=F32):
        return Buf(self.es.enter_context(self.nc.psum_tensor(name, [128, ncols], dt)), name, ch=512)

    def mm(self, out, lhsT, rhs, start=True, stop=True, r=False):
        la, ra = lhsT.ap, rhs.ap
        if r and FAST_MM:
            la = la.bitcast(F32R)
            ra = ra.bitcast(F32R)
        self.kb.op('pe', lambda e: e.matmul(out.ap, lhsT=la, rhs=ra, start=start, stop=stop),
                   reads=lhsT.keys + rhs.keys, writes=out.keys)

    def tr(self, out, in_, n=128):
        idn = self.ident.r(0, n, 0, n)
        self.kb.op('pe', lambda e: e.transpose(out.ap, in_.ap, idn.ap),
                   reads=in_.keys + idn.keys, writes=out.keys)

    def act(self, out, in_, func, bias=0.0, scale=1.0):
        rd = list(in_.keys)
        b = bias
        if isinstance(bias, V):
            rd += bias.keys
            b = bias.ap
        self.kb.op('act', lambda e: e.activation(out=out.ap, in_=in_.ap, func=func, bias=b, scale=scale),
                   reads=rd, writes=out.keys)

    def tt(self, eng, out, a, b, op):
        if eng == 'pool':
            eng = POOL_TO
        self.kb.op(eng, lambda e: e.tensor_tensor(out=out.ap, in0=a.ap, in1=b.ap, op=op),
                   reads=a.keys + b.keys, writes=out.keys)

    def ts(self, eng, out, a, s1, op0, s2=None, op1=None):
        if eng == 'pool':
            eng = POOL_TO
        rd = list(a.keys)
        x1, x2 = s1, s2
        if isinstance(s1, V):
            rd += s1.keys
            x1 = s1.ap
        if isinstance(s2, V):
            rd += s2.keys
            x2 = s2.ap
        if op1 is None:
            self.kb.op(eng, lambda e: e.tensor_scalar(out=out.ap, in0=a.ap, scalar1=x1, scalar2=None, op0=op0),
                       reads=rd, writes=out.keys)
        else:
            self.kb.op(eng, lambda e: e.tensor_scalar(out=out.ap, in0=a.ap, scalar1=x1, scalar2=x2, op0=op0, op1=op1),
                       reads=rd, writes=out.keys)

    def stt(self, out, in0, scalar, in1, op0, op1):
        rd = in0.keys + in1.keys
        x = scalar
        if isinstance(scalar, V):
            rd = rd + scalar.keys
            x = scalar.ap
        self.kb.op('dve', lambda e: e.scalar_tensor_tensor(out=out.ap, in0=in0.ap, scalar=x, in1=in1.ap, op0=op0, op1=op1),
                   reads=rd, writes=out.keys)

    def cp(self, eng, out, in_):
        if eng == 'pool':
            eng = POOL_TO
        if eng == 'act':
            self.kb.op(eng, lambda e: e.copy(out=out.ap, in_=in_.ap), reads=in_.keys, writes=out.keys)
        else:
            self.kb.op(eng, lambda e: e.tensor_copy(out=out.ap, in_=in_.ap), reads=in_.keys, writes=out.keys)

    def scan(self, out, d0, d1, init):
        rd = d0.keys + d1.keys
        x = init
        if isinstance(init, V):
            rd = rd + init.keys
            x = init.ap
        self.kb.op('dve', lambda e: e.tensor_tensor_scan(out=out.ap, data0=d0.ap, data1=d1.ap, initial=x,
                                                          op0=ALU.mult, op1=ALU.add), reads=rd, writes=out.keys)

    def memset(self, eng, out, val):
        self.kb.op(eng, lambda e: e.memset(out.ap, val), writes=out.keys)

    def asel(self, out, in_, pattern, cmp, fill, base, cm):
        self.kb.op('pool', lambda e: e.affine_select(out=out.ap, in_=in_.ap, pattern=pattern, compare_op=cmp,
                                                     fill=fill, base=base, channel_multiplier=cm),
                   reads=in_.keys, writes=out.keys)

    def rsqrt(self, out, in_, addc):
        self.kb.op('act', lambda e: e.activation(out=out.ap, in_=in_.ap, func=AF.Sqrt, bias=addc, scale=1.0),
                   reads=in_.keys, writes=out.keys)
        self.recip(out, out)

    def recip(self, out, in_):
        self.kb.op('dve', lambda e: e.reciprocal(out=out.ap, in_=in_.ap), reads=in_.keys, writes=out.keys)

    def dma(self, q, out, in_):
        self.kb.dma(q, out.ap, in_.ap, reads=in_.keys, writes=out.keys)


def build(T, depth, seq_out):
    ntiles = (T + TT - 1) // TT
    Tp = ntiles * TT
    pad = Tp - T
    out_row0 = Tp - seq_out
    nc = bass.Bass("TRN2", target_bir_lowering=False)
    L = depth

    def dram(name, shape, kind="ExternalInput"):
        return nc.dram_tensor(name, shape, F32, kind=kind).ap()

    xin = dram("xin", [Tp, D])
    lnin = dram("lnin", [2, D])
    wfm = dram("wfm", [L, 80, 128, KT * 128])
    wsc = dram("wsc", [L, 128, KT * 16])
    witm = dram("witm", [L, 4, 128, KT * 128])
    wglu = dram("wglu", [L, 2, 4, 128, 4 * 128])
    wbr = dram("wbr", [L, 8, 4, 128, 4 * 128])
    wout = dram("wout", [L, KT, 128, D])
    pp_d = dram("pp", [L, 128, NPP])
    pr_d = dram("pr", [L, NPR])
    lbl_d = dram("lbl", [128, 4 * depth])
    s5b_d = dram("s5b", [L, 2, 128, 16 * 128])
    s5c_d = dram("s5c", [L, 2, 128, 16 * 32])
    yout = dram("y", [seq_out, D], kind="ExternalOutput")
    hb = [dram("hbuf%d" % i, [Tp, D], kind="Internal") for i in range(2)]

    def DV(ap, name, blk=0):
        return V(ap, [("dram_" + name, blk)])

    with ExitStack() as es:
        g = G(nc, es)
        kb = g.kb
        sb, ps = g.sb, g.ps
        M, A_, S_ = ALU.mult, ALU.add, ALU.subtract
        ident = sb("ident", 128)
        g.ident = ident
        ones = sb("ones", 128)
        zeros = sb("zeros", 128)
        NEGS = sb("NEGS", 128)
        NEGT = sb("NEGT", 128)
        TRIU = sb("TRIU", 128)
        M01 = sb("M01", 64)
        SEG = sb("SEG", TT)
        g.memset('pool', zeros.r(0, 128), 0.0)
        g.memset('pool', ones.r(0, 128), 1.0)
        g.asel(ident.r(0, 128), zeros.r(0, 128), [[-1, 128]], ALU.not_equal, 1.0, 0, 1)
        g.asel(NEGS.r(0, 128), zeros.r(0, 128), [[-1, 128]], ALU.is_gt, BIG, 0, 1)
        g.asel(NEGT.r(0, 128), zeros.r(0, 128), [[1, 128]], ALU.is_ge, -BIG, 0, -1)
        g.asel(TRIU.r(0, 128), ones.r(0, 128), [[1, 128]], ALU.is_ge, 0.0, 0, -1)
        g.asel(M01.r(0, 64, 0, 64), ones.r(0, 64, 0, 64), [[1, 64]], ALU.is_ge, 0.0, 0, -1)
        g.memset('pool', SEG.r(0, TT), 1.0)
        for c in range(NCH):
            g.memset('pool', SEG.r(c * 64, 1), 0.0)

        NW = 6
        wring = sb("wring", NW * 1024)
        htok = sb("htok", NB * D)
        hT = sb("hT", KT * TT)
        ppt = sb("ppt", NPP)
        prt = sb("prt", NPR)
        lbl = sb("lbl_s", 4 * depth)
        lbt = sb("lbt", depth * 4)
        omlb = sb("omlb", depth * 4)
        negA12 = sb("negA12", 12)
        wsm = sb("wsm", 8)
        halo_g = sb("halo_g", 12 * 3)
        halo_m = sb("halo_m", 8 * 3)
        S_g = sb("S_g", 4 * 128)
        S_m = sb("S_m", 8 * 128)
        S_h = sb("S_h", 4 * 128)
        xs5 = sb("xs5", 32)
        BbT = sb("BbT", 2 * 16 * 128)
        Cpd = sb("Cpd", 2 * 16 * 32)
        costab = sb("costab", 16 * S5S)
        sintab = sb("sintab", 16 * S5S)
        rho = sb("rho", 16)
        Y = [sb("y%d" % i, 4 * TT) for i in range(4)]
        padm = sb("padm", NB)
        scb = sb("scb", 16 * 16)
        bcx = sb("bcx", 8 * 128)
        ebx = sb("ebx", 8 * 128)
        wscb = sb("wscb", KT * 16)
        lnst = sb("lnst", 16)
        rbuf = sb("rbuf", D)
        NAR = 13440
        AR = sb("AR", NAR)
        YSB = sb("ysb", 4 * TT)
        MIXB = sb("mixb", 8 * TT)
        PS = [ps("ps%d" % i, 512) for i in range(8)]
        if len(PHASES) < 4:
            for yb_ in Y:
                for q_ in range(8):
                    g.cp('pool', f32r(yb_.r(q_ * 128, 128)), zeros.r(0, 128))

        stream = []

        def layer_stream(l):
            it = []
            for ct in range(16):
                it.append(("gdn%d" % ct, wfm[l, ct, :, :], 1024))
            for ct in range(12):
                it.append(("m2_%d" % ct, wfm[l, 16 + ct, :, :], 1024))
            for ct in range(8):
                it.append(("hgqf%d" % ct, wfm[l, 28 + ct, :, :], 1024))
            for ct in range(4):
                it.append(("hgi%d" % ct, witm[l, ct, :, :], 1024))
            for ct in range(4):
                it.append(("hgz%d" % ct, wfm[l, 36 + ct, :, :], 1024))
            for ct in range(8):
                it.append(("s5_%d" % ct, wfm[l, 40 + ct, :, :], 1024))
            for ct in range(4):
                for j in range(2):
                    it.append(("glu%d_%d" % (j, ct), wglu[l, j, ct, :, :], 512))
            for dt in range(8):
                for b in range(4):
                    it.append(("gate%d_%d" % (b, dt), wfm[l, 48 + b * 8 + dt, :, :], 1024))
                    it.append(("wbr%d_%d" % (b, dt), wbr[l, dt, b, :, :], 512))
            for k in range(KT):
                it.append(("wout%d" % k, wout[l, k, :, :], 1024))
            pref = []
            if 'gdn' not in PHASES:
                pref.append('gdn')
            if 'ssd' not in PHASES:
                pref.append('m2_')
            if 'hg' not in PHASES:
                pref.append('hg')
            if 's5' not in PHASES:
                pref += ['s5_', 'glu']
            it = [x for x in it if not any(x[0].startswith(q) for q in pref)]
            return it

        wst = {'issued': 0, 'used': 0}
        PF = 4

        def w_begin(l):
            stream[:] = layer_stream(l)
            wst['issued'] = 0
            wst['used'] = 0

        def w_issue():
            i = wst['issued']
            key, src, ncols = stream[i]
            slot = i % NW
            g.dma('pool', f32r(wring.r(slot * 1024, ncols)), DV(src, "w"))
            wst['issued'] += 1

        def w_get(key):
            i = wst['used']
            assert stream[i][0] == key, (stream[i][0], key)
            while wst['issued'] <= min(i + PF, len(stream) - 1):
                w_issue()
            wst['used'] += 1
            return (i % NW) * 1024

        def ln_rows(dst, src_sb, gam, bet):
            for c in range(2):
                kb.op('dve', lambda e: e.bn_stats(out=lnst.h[:, c * 6:(c + 1) * 6], in_=src_sb.ap[:, c * 512:(c + 1) * 512]),
                      reads=src_sb.keys, writes=lnst.keys(0, 16))
            kb.op('dve', lambda e: e.bn_aggr(out=lnst.h[:, 12:14], in_=lnst.h[:, 0:12].rearrange("p (c s) -> p c s", c=2)),
                  reads=lnst.keys(0, 16), writes=lnst.keys(0, 16))
            g.rsqrt(lnst.r(14, 1), lnst.r(13, 1), LN_EPS)
            g.ts('dve', dst, src_sb, lnst.r(12, 1), S_, lnst.r(14, 1), M)
            g.tt('pool', dst, dst, gam, M)
            g.tt('pool', dst, dst, bet, A_)

        def rmsnorm_fm(yv, ntile, wcol, scale_n, tmp_off, psb):
            pass

        if pad > 0:
            g.memset('dve', AR.r(0, D), 0.0)
            r = 0
            while r < pad:
                n = min(128, pad - r)
                for i in range(2):
                    g.dma('sp', DV(hb[i][r:r + n, :], "hb%d" % i), AR.r(0, D, 0, n))
                r += n

        g.dma('sp', prt.r(0, D), DV(lnin[0:1, :].to_broadcast([128, D]), "lnin"))
        g.dma('sp', prt.r(D, D), DV(lnin[1:2, :].to_broadcast([128, D]), "lnin"))
        for t in range(ntiles):
            for b in range(NB):
                r0 = t * TT + b * 128
                lo = max(0, pad - r0)
                if lo >= 128:
                    continue
                j = b % NB
                g.dma('sp', htok.r(j * D, D), DV(xin[r0:r0 + 128, :], "xin"))
                ln_rows(rbuf.r(0, D), htok.r(j * D, D), prt.r(0, D), prt.r(D, D))
                g.dma('pool', DV(hb[0][r0 + lo:r0 + 128, :], "hb0"), rbuf.r(0, D, lo, 128))

        g.dma('sp', lbl.r(0, 4 * depth), DV(lbl_d[:, :], "lbl"))
        g.act(lbl.r(0, 4 * depth), lbl.r(0, 4 * depth), AF.Exp)
        kb.op('dve', lambda e: e.tensor_reduce(out=AR.h[:, 0:4], in_=lbl.h[:, :].rearrange("p (c l) -> p c l", l=depth),
                                               axis=AX.X, op=ALU.add), reads=lbl.keys(0, 4 * depth), writes=AR.keys(0, 4))
        g.recip(AR.r(4, 4), AR.r(0, 4))
        g.memset('dve', lbt.r(0, 4), 0.0)
        for l in range(1, depth):
            ev = V(lbl.h[:, :].rearrange("p (c l) -> p c l", l=depth)[:, :, l], lbl.keys(0, 4 * depth))
            g.tt('dve', lbt.r(l * 4, 4), lbt.r((l - 1) * 4, 4), ev, A_)
        for l in range(1, depth):
            g.tt('dve', lbt.r(l * 4, 4), lbt.r(l * 4, 4), AR.r(4, 4), M)
        g.ts('dve', omlb.r(0, 4 * depth), lbt.r(0, 4 * depth), -1.0, M, 1.0, A_)

        for l in range(L):
            src = hb[l % 2]
            srcn = "hb%d" % (l % 2)
            dst = hb[(l + 1) % 2]
            dstn = "hb%d" % ((l + 1) % 2)
            last = (l == L - 1)
            g.dma('sp', ppt.r(0, NPP), DV(pp_d[l, :, :], "pp"))
            g.dma('sp', wscb.r(0, KT * 16), DV(wsc[l, :, :], "wsc"))
            g.dma('sp', prt.r(0, NPR), DV(pr_d[l:l + 1, :].to_broadcast([128, NPR]), "pr"))
            g.act(negA12.r(0, 12), prt.r(PR_A12, 12), AF.Exp)
            g.ts('dve', negA12.r(0, 12), negA12.r(0, 12), -1.0, M)
            g.ts('dve', wsm.r(0, 1), ppt.r(PP_GNW, 1), math.sqrt(128.0), M)
            g.ts('dve', wsm.r(1, 1), ppt.r(PP_HNW, 1), math.sqrt(128.0), M)
            g.ts('dve', wsm.r(2, 4), ppt.r(PP_MNW, 4), 16.0, M)
            for st_, n_ in ((halo_g, 36), (halo_m, 24), (S_g, 512), (S_m, 1024), (S_h, 512), (xs5, 32)):
                g.memset('pool', st_.r(0, n_), 0.0)
            a_re = ppt.r(PP_SAR, 16)
            a_im = ppt.r(PP_SAI, 16)

            def sl(i):
                return AR.r(8192 + i * 16, 16)
            DT_, ARD, ANG, T1, T2, COS, SIN, LRE, LIM, DEN, ZRE, ZIM = range(12)
            g.act(sl(DT_), ppt.r(PP_SLDT, 16), AF.Exp)
            g.tt('dve', sl(ARD), a_re, sl(DT_), M)
            g.act(rho.r(0, 16), sl(ARD), AF.Exp)
            g.tt('dve', sl(ANG), a_im, sl(DT_), M)
            g.act(sl(SIN), sl(ANG), AF.Sin, scale=1.0 / 32.0)
            g.ts('dve', sl(T2), sl(ANG), 1.0 / 32.0, M, 0.5 * math.pi, A_)
            g.act(sl(COS), sl(T2), AF.Sin)
            for _ in range(5):
                g.tt('dve', sl(T1), sl(COS), sl(COS), M)
                g.tt('dve', sl(T2), sl(SIN), sl(SIN), M)
                g.stt(sl(SIN), sl(SIN), 2.0, sl(COS), M, M)
                g.tt('dve', sl(COS), sl(T1), sl(T2), S_)
            g.tt('dve', sl(LRE), rho.r(0, 16), sl(COS), M)
            g.tt('dve', sl(LIM), rho.r(0, 16), sl(SIN), M)
            g.tt('dve', sl(DEN), a_re, a_re, M)
            g.tt('dve', sl(T1), a_im, a_im, M)
            g.tt('dve', sl(DEN), sl(DEN), sl(T1), A_)
            g.recip(sl(DEN), sl(DEN))
            g.ts('dve', sl(LRE), sl(LRE), -1.0, A_)
            g.tt('dve', sl(T1), sl(LRE), a_re, M)
            g.tt('dve', sl(T2), sl(LIM), a_im, M)
            g.tt('dve', sl(ZRE), sl(T1), sl(T2), A_)
            g.tt('dve', sl(ZRE), sl(ZRE), sl(DEN), M)
            g.tt('dve', sl(T1), sl(LIM), a_re, M)
            g.tt('dve', sl(T2), sl(LRE), a_im, M)
            g.tt('dve', sl(ZIM), sl(T1), sl(T2), S_)
            g.tt('dve', sl(ZIM), sl(ZIM), sl(DEN), M)
            g.cp('dve', costab.rs(0, 16, S5S, 0, 1), usq(sl(COS), 2))
            g.cp('dve', sintab.rs(0, 16, S5S, 0, 1), usq(sl(SIN), 2))
            n = 1
            while n < S5S:
                cn = bcast(costab.rs(0, 16, S5S, n - 1, 1), [128, 16, n])
                sn = bcast(sintab.rs(0, 16, S5S, n - 1, 1), [128, 16, n])
                t1 = AR.r3(8448, 16, n)
                t2 = AR.r3(9472, 16, n)
                c0 = costab.rs(0, 16, S5S, 0, n)
                s0 = sintab.rs(0, 16, S5S, 0, n)
                g.tt('dve', t1, c0, cn, M)
                g.tt('dve', t2, s0, sn, M)
                g.tt('dve', costab.rs(0, 16, S5S, n, n), t1, t2, S_)
                g.tt('dve', t1, s0, cn, M)
                g.tt('dve', t2, c0, sn, M)
                g.tt('dve', sintab.rs(0, 16, S5S, n, n), t1, t2, A_)
                n *= 2
            Bre = AR.r3(0, 16, 128)
            Bim = AR.r3(2048, 16, 128)
            X1 = AR.r3(4096, 16, 128)
            X2 = AR.r3(6144, 16, 128)
            g.dma('sp', AR.r(0, 2048), DV(s5b_d[l, 0, :, :], "s5b"))
            g.dma('sp', AR.r(2048, 2048), DV(s5b_d[l, 1, :, :], "s5b"))
            zre_b = bcast(usq(sl(ZRE), 2), [128, 16, 128])
            zim_b = bcast(usq(sl(ZIM), 2), [128, 16, 128])
            g.tt('dve', X1, Bre, zre_b, M)
            g.tt('dve', X2, Bim, zim_b, M)
            g.tt('dve', X1, X1, X2, S_)
            g.tt('dve', X2, Bim, zre_b, M)
            g.tt('dve', Bim, Bre, zim_b, M)
            g.tt('dve', X2, X2, Bim, A_)
            for ri, xo in enumerate((4096, 6144)):
                for pr4 in range(4):
                    pst = PS[pr4 % 2]
                    for q in range(4):
                        pr_ = pr4 * 4 + q
                        g.tr(pst.r(q * 128, 128), AR.r(xo + pr_ * 128, 128))
                    g.cp('act', BbT.r(ri * 2048 + pr4 * 512, 512), pst.r(0, 512))
            g.dma('sp', Cpd.r(0, 512), DV(s5c_d[l, 0, :, :], "s5c"))
            g.dma('sp', Cpd.r(512, 512), DV(s5c_d[l, 1, :, :], "s5c"))
            g.ts('pool', Cpd.r(512, 512), Cpd.r(512, 512), -1.0, M)

            def rows(ap, r0, n):
                if isinstance(r0, int):
                    return ap[r0:r0 + n, :]
                return ap[bass.ds(r0, n), :]

            def tile_body(t, dyn):
                tok0 = t * TT
                has_pad = (not dyn) and (tok0 < pad)
                w_begin(l)
                for b in range(NB):
                    r0 = tok0 + b * 128
                    g.dma('sp', htok.r(b * D, D), DV(rows(src, r0, 128), srcn))
                for k in range(KT):
                    pst = PS[k % 2]
                    for b in range(NB):
                        g.tr(pst.r(b * 128, 128), htok.r(b * D + k * 128, 128))
                    g.cp('act' if k % 2 == 0 else 'dve', f32r(hT.r(k * TT, TT)), pst.r(0, TT))
                if has_pad:
                    for b in range(NB):
                        lo = pad - (tok0 + b * 128)
                        g.asel(padm.r(b, 1), ones.r(0, 1), [[0, 1]], ALU.is_ge, 0.0, -lo, 1)

                pp_state = {'i': 0}
                W3 = TT + 3

                def proj_fm(key):
                    wo = w_get(key)
                    pst = PS[pp_state['i'] % 2]
                    pp_state['i'] += 1
                    for k in range(KT):
                        g.mm(pst.r(0, TT), wring.r(wo + k * 128, 128), hT.r(k * TT, TT), start=(k == 0), stop=(k == KT - 1), r=True)
                    return pst.r(0, TT)

                def block_scalars(b, c0, nh):
                    R = lambda i, n=nh, o=0: scb.r(i * 16 + o, n)
                    pst = PS[2]
                    for k in range(KT):
                        g.mm(pst.r(0, 16), hT.r(k * TT + b * 128, 128), wscb.r(k * 16, 16),
                             start=(k == 0), stop=(k == KT - 1))
                    g.cp('dve', R(0, 16), pst.r(0, 16))
                    if c0 == 4:
                        g.act(R(1, 4), R(0, 4), AF.Sigmoid)
                    bo = PR_B12 + (0 if c0 == 4 else 4)
                    ao = 0 if c0 == 4 else 4
                    g.tt('dve', R(2), R(0, nh, c0), prt.r(bo, nh), A_)
                    g.act(R(3), R(2), AF.Abs)
                    g.act(R(3), R(3), AF.Exp, scale=-1.0)
                    g.act(R(3), R(3), AF.Ln, bias=1.0)
                    g.ts('dve', R(2), R(2), 0.0, ALU.max)
                    g.tt('dve', R(4), R(2), R(3), A_)
                    if has_pad and c0 == 8:
                        g.ts('dve', R(4), R(4), padm.r(b, 1), M)
                    g.tt('dve', R(5), R(4), negA12.r(ao, nh), M)
                    g.mm(pst.r(16, nh), TRIU.r(0, 128), R(5))
                    g.cp('dve', R(6), pst.r(16, nh))
                    Rv = AR.r3(11392, nh, 128)
                    g.tt('pool', Rv, bcast(usq(TRIU.r(0, 128), 1), [128, nh, 128]), bcast(usq(R(5), 2), [128, nh, 128]), M)
                    for q in range(nh // 4):
                        pq = PS[3]
                        g.mm(pq.r(0, 512), ones.r(0, 128), AR.r(11392 + q * 512, 512))
                        g.cp('act', bcx.r(q * 512, 512), pq.r(0, 512))
                    g.act(ebx.r(0, nh * 128), bcx.r(0, nh * 128), AF.Exp)
                    g.act(R(7), R(6), AF.Exp)
                    last = bcx.rs(0, nh, 128, 127, 1)
                    g.tt('dve', usq(R(8), 2), last, usq(R(6), 2), S_)
                    g.act(R(8), R(8), AF.Exp)
                    if c0 == 4:
                        g.tt('dve', R(9), R(1, 4), R(7), M)
                    else:
                        g.tt('dve', R(10), R(4), R(8), M)
                    return R

                def conv_fm(nct, xc_off, acc_off, halo, cw_off, cb_off):
                    W3 = TT + 3
                    g.cp('pool', AR.rs(xc_off, nct, W3, 0, 3), halo.r3(0, nct, 3))
                    for ct in range(nct):
                        xo = xc_off + ct * W3
                        acc = AR.r(acc_off + ct * TT, TT)
                        cw = lambda j: ppt.r(cw_off + ct * 4 + j, 1)
                        if cb_off is None:
                            g.ts('dve', acc, AR.r(xo + 3, TT), cw(3), M)
                        else:
                            g.ts('dve', acc, AR.r(xo + 3, TT), cw(3), M, ppt.r(cb_off + ct, 1), A_)
                        for j in range(3):
                            g.stt(acc, AR.r(xo + j, TT), cw(j), acc, M, A_)
                        g.act(acc, acc, AF.Silu)
                    g.cp('pool', halo.r3(0, nct, 3), AR.rs(xc_off, nct, W3, TT, 3))

                def rms_fm(ytiles, wcols, neps, zoffs, sq_off, rn_off):
                    pst = PS[3]
                    for i, yv in enumerate(ytiles):
                        sq = AR.r(sq_off, TT)
                        g.tt('pool', sq, yv, yv, M)
                        g.mm(pst.r(0, TT), ones.r(0, 128), sq, start=(i == 0), stop=(i == len(ytiles) - 1))
                    rn = AR.r(rn_off, TT)
                    g.rsqrt(rn, pst.r(0, TT), neps)
                    for yv, wc in zip(ytiles, wcols):
                        g.stt(yv, yv, wc, rn, M, M)

                if 'gdn' in PHASES:
                    XC, QKV, ZS = 0, 3200, 6272
                    W3 = TT + 3
                    for ct in range(12):
                        pv = proj_fm("gdn%d" % ct)
                        g.cp('act', AR.r(XC + ct * W3 + 3, TT), pv)
                    conv_fm(12, XC, QKV, halo_g, PP_CWG, None)
                    for ct in range(4):
                        pv = proj_fm("gdn%d" % (12 + ct))
                        g.act(AR.r(ZS + ct * TT, TT), pv, AF.Silu)
                    for ct in range(8):
                        x = AR.r(QKV + ct * TT, TT)
                        sq = AR.r(9600, TT)
                        g.tt('pool', sq, x, x, M)
                        pst = PS[3]
                        g.mm(pst.r(0, TT), ones.r(0, 128), sq)
                        rn = AR.r(9856, TT)
                        g.rsqrt(rn, pst.r(0, TT), RMS_EPS)
                        if ct < 4:
                            g.stt(x, rn, 128.0 ** -0.5, x, M, M)
                        else:
                            g.tt('dve', x, x, rn, M)
                    OSB = 10112
                    TA, Dm, Am, Bm, QKT, VB, KBG, KDEC, Um, WTm, VNEW, QDEC = [7296 + i * 128 for i in range(12)]
                    PQ = [8832 + 128 * i for i in range(4)]
                    RR = [9344, 9472]
                    for b in range(NB):
                        R = block_scalars(b, 4, 4)
                        for h in range(4):
                            qT = AR.r(QKV + h * TT + b * 128, 128)
                            kT = AR.r(QKV + (4 + h) * TT + b * 128, 128)
                            vT = AR.r(QKV + (8 + h) * TT + b * 128, 128)
                            gcb = bcx.r(h * 128, 128)
                            gcc = R(6, 1, h)
                            pa = PS[4]
                            g.mm(pa.r(0, 128), kT, kT)
                            g.mm(pa.r(128, 128), kT, qT)
                            g.stt(AR.r(TA, 128), gcb, gcc, NEGS.r(0, 128), S_, A_)
                            g.act(AR.r(Dm, 128), AR.r(TA, 128), AF.Exp, scale=-1.0)
                            g.stt(AR.r(Am, 128), AR.r(Dm, 128), R(1, 1, h), pa.r(0, 128), M, M)
                            g.stt(AR.r(TA, 128), gcb, gcc, NEGT.r(0, 128), S_, A_)
                            g.act(AR.r(Dm, 128), AR.r(TA, 128), AF.Exp)
                            g.tt('dve', AR.r(QKT, 128), AR.r(Dm, 128), pa.r(128, 128), M)
                            g.tr(pa.r(256, 128), AR.r(Am, 128))
                            g.cp('act', AR.r(Bm, 128), pa.r(256, 128))
                            g.tr(pa.r(384, 128), kT)
                            g.ts('dve', AR.r(KBG, 128), pa.r(384, 128), R(9, 1, h), M)
                            g.ts('dve', AR.r(KDEC, 128), pa.r(384, 128), R(8, 1, h), M)
                            pb = PS[5]
                            g.tr(pb.r(0, 128), vT)
                            g.ts('dve', AR.r(VB, 128), pb.r(0, 128), R(1, 1, h), M)
                            g.tt('pool', AR.r(RR[0], 128), ident.r(0, 128), AR.r(Bm, 128), S_)
                            Pc, Qc = Am, Bm
                            ri = 0
                            for lev in range(1, 7):
                                Pn, Qn = PQ[(lev % 2) * 1], PQ[2 + (lev % 2) * 1]
                                g.mm(pb.r(128, 128), AR.r(Qc, 128), AR.r(Pc, 128))
                                g.cp('act', AR.r(Pn, 128), pb.r(128, 128))
                                if lev < 6:
                                    g.mm(pb.r(256, 128), AR.r(Pc, 128), AR.r(Qc, 128))
                                    g.cp('dve', AR.r(Qn, 128), pb.r(256, 128))
                                g.mm(pb.r(384, 128), AR.r(Pn, 128), AR.r(RR[ri], 128))
                                g.tt('dve', AR.r(RR[1 - ri], 128), AR.r(RR[ri], 128), pb.r(384, 128), A_)
                                ri = 1 - ri
                                Pc, Qc = Pn, Qn
                            TTm = AR.r(RR[ri], 128)
                            pc = PS[6]
                            g.mm(pc.r(0, 128), TTm, AR.r(VB, 128))
                            g.cp('act', AR.r(Um, 128), pc.r(0, 128))
                            g.mm(pc.r(128, 128), AR.r(KBG, 128), TTm)
                            g.cp('act', AR.r(WTm, 128), pc.r(128, 128))
                            Sv = S_g.r(h * 128, 128)
                            g.mm(pc.r(256, 128), AR.r(WTm, 128), Sv)
                            g.tt('dve', AR.r(VNEW, 128), AR.r(Um, 128), pc.r(256, 128), S_)
                            g.tt('pool', AR.r(QDEC, 128), qT, ebx.r(h * 128, 128), M)
                            po = PS[7]
                            ov = po.r((h % 2) * 256 + b * 128, 128)
                            g.mm(ov, Sv, AR.r(QDEC, 128), start=True, stop=False)
                            g.mm(ov, AR.r(VNEW, 128), AR.r(QKT, 128), start=False, stop=True)
                            g.cp('act', AR.r(OSB + h * TT + b * 128, 128), ov)
                            g.mm(pc.r(384, 128), AR.r(KDEC, 128), AR.r(VNEW, 128))
                            elast = ebx.r(h * 128 + 127, 1)
                            g.stt(Sv, Sv, elast, pc.r(384, 128), M, A_)
                    for h in range(4):
                        ov = AR.r(OSB + h * TT, TT)
                        rms_fm([ov], [wsm.r(0, 1)], 128.0 * RMS_EPS, None, 9600, 9856)
                        g.tt('pool', f32r(Y[0].r(h * TT, TT)), ov, AR.r(ZS + h * TT, TT), M)

                if 'ssd' in PHASES:
                    XC, XBC, ZS = 0, 3200, 6272
                    for ct in range(8):
                        pv = proj_fm("m2_%d" % ct)
                        g.cp('act', AR.r(XC + ct * W3 + 3, TT), pv)
                    conv_fm(8, XC, XBC, halo_m, PP_CWM, PP_CBM)
                    for ct in range(4):
                        pv = proj_fm("m2_%d" % (8 + ct))
                        g.act(AR.r(ZS + ct * TT, TT), pv, AF.Silu)
                    XDTP, XDEC, BTOK, CBT, DTm, CTD, TMPS = 7296, 8320, 8832, 9088, 9344, 10368, 12416
                    g.memset('pool', AR.r(XDTP, 1024), 0.0)
                    YPS = [PS[6], PS[7]]
                    for b in range(NB):
                        R = block_scalars(b, 8, 8)
                        px = PS[4]
                        for c in range(4):
                            g.tr(px.r(c * 128, 128), AR.r(XBC + c * TT + b * 128, 128))
                        pbk = PS[5]
                        for gi in range(2):
                            g.tr(pbk.r(gi * 128, 128), AR.r(XBC + (4 + gi) * TT + b * 128, 128))
                        g.cp('act', AR.r(BTOK, 256), pbk.r(0, 256))
                        for par in range(2):
                            outv = V(AR.h[:, XDTP:XDTP + 1024].rearrange("p (c q) -> p c q", q=256)[:, :, par * 192: par * 192 + 64],
                                     AR.keys(XDTP, 1024))
                            inv = V(px.h[:, 0:512].rearrange("p (c q) -> p c q", q=128)[:, :, par * 64:(par + 1) * 64], px.keys(0, 512))
                            sp_ = V(scb.h[:, 4 * 16:4 * 16 + 8].rearrange("p (c q) -> p c q", q=2)[:, :, par:par + 1].to_broadcast([128, 4, 64]),
                                    scb.keys(64, 8))
                            g.tt('dve', outv, inv, sp_, M)
                        g.tt('dve', AR.r3(XDEC, 8, 64), px.r3(0, 8, 64), bcast(usq(R(10), 2), [128, 8, 64]), M)
                        for gi in range(2):
                            g.mm(pbk.r(256 + gi * 128, 128), AR.r(XBC + (4 + gi) * TT + b * 128, 128),
                                 AR.r(XBC + (6 + gi) * TT + b * 128, 128))
                        g.cp('act', AR.r(CBT, 256), pbk.r(256, 256))
                        g.tt('dve', AR.r3(TMPS, 8, 128), bcx.r3(0, 8, 128), bcast(usq(R(6), 2), [128, 8, 128]), S_)
                        g.tt('pool', AR.r3(TMPS, 8, 128), AR.r3(TMPS, 8, 128), bcast(usq(NEGT.r(0, 128), 1), [128, 8, 128]), A_)
                        g.act(AR.r(DTm, 1024), AR.r(TMPS, 1024), AF.Exp)
                        for gi in range(2):
                            g.tt('pool', AR.r3(DTm + gi * 512, 4, 128), AR.r3(DTm + gi * 512, 4, 128),
                                 bcast(usq(AR.r(CBT + gi * 128, 128), 1), [128, 4, 128]), M)
                            g.tt('pool', AR.r3(CTD + gi * 512, 4, 128), ebx.r3(gi * 512, 4, 128),
                                 bcast(usq(AR.r(XBC + (6 + gi) * TT + b * 128, 128), 1), [128, 4, 128]), M)
                        for c in range(4):
                            yv = YPS[c // 2].r((c % 2) * 256 + b * 128, 128)
                            for hh in range(2):
                                h = 2 * c + hh
                                g.mm(yv, AR.r(XDTP + h * 128, 128), AR.r(DTm + h * 128, 128), start=(hh == 0), stop=False)
                                g.mm(yv, S_m.r(h * 128, 128), AR.r(CTD + h * 128, 128), start=False, stop=(hh == 1))
                        pu = PS[3]
                        for gi in range(2):
                            g.mm(pu.r(gi * 256, 256), AR.r(BTOK + gi * 128, 128), AR.r(XDEC + gi * 256, 256))
                        for par in range(2):
                            sv = V(S_m.h[:, :].rearrange("p (c q) -> p c q", q=256)[:, :, par * 192: par * 192 + 64], S_m.keys(0, 1024))
                            el = V(ebx.h[:, :].rearrange("p (c q) -> p c q", q=256)[:, :, par * 128 + 127: par * 128 + 128].to_broadcast([128, 4, 64]),
                                   ebx.keys(0, 1024))
                            uv = V(pu.h[:, 0:512].rearrange("p (c q) -> p c q", q=128)[:, :, par * 64:(par + 1) * 64], pu.keys(0, 512))
                            g.tt('pool', sv, sv, el, M)
                            g.tt('dve', sv, sv, uv, A_)
                    for c in range(4):
                        yv = AR.r(c * TT, TT)
                        g.stt(yv, AR.r(XBC + c * TT, TT), ppt.r(PP_MD + c, 1), YPS[c // 2].r((c % 2) * 256, TT), M, A_)
                        g.tt('pool', yv, yv, AR.r(ZS + c * TT, TT), M)
                    for gi in range(2):
                        tiles = [AR.r((2 * gi + i) * TT, TT) for i in range(2)]
                        rms_fm(tiles, [wsm.r(2 + 2 * gi + i, 1) for i in range(2)], 256.0 * RMS_EPS, None, 9600, 9856)
                        for i in range(2):
                            g.cp('pool', f32r(Y[1].r((2 * gi + i) * TT, TT)), tiles[i])

                if 'hg' in PHASES:
                    QH, Fo, LOGF, GC, EG, QD, KD, KK, KDE, ZS, ITOK, ATM, KDT = \
                        0, 1024, 2048, 3072, 4096, 5120, 6144, 7168, 8192, 9216, 10240, 12288, 12544
                    for ct in range(4):
                        pv = proj_fm("hgqf%d" % ct)
                        g.act(AR.r(QH + ct * TT, TT), pv, AF.Silu)
                    for ct in range(4):
                        pv = proj_fm("hgqf%d" % (4 + ct))
                        f = AR.r(Fo + ct * TT, TT)
                        g.act(f, pv, AF.Sigmoid)
                        g.ts('dve', f, f, omlb.r(l * 4 + ct, 1), M, lbt.r(l * 4 + ct, 1), A_)
                        g.act(AR.r(LOGF + ct * TT, TT), f, AF.Ln)
                        g.ts('pool', AR.r(KK + ct * TT, TT), f, -1.0, M, 1.0, A_)
                        g.scan(AR.r(GC + ct * TT, TT), SEG.r(0, TT), AR.r(LOGF + ct * TT, TT), 0.0)
                    g.act(AR.r(EG, 1024), AR.r(GC, 1024), AF.Exp)
                    g.tt('pool', AR.r(QD, 1024), AR.r(QH, 1024), AR.r(EG, 1024), M)
                    g.ts('dve', AR.r(KD, 1024), AR.r(GC, 1024), -80.0, ALU.max)
                    g.act(AR.r(KD, 1024), AR.r(KD, 1024), AF.Exp, scale=-1.0)
                    g.tt('pool', AR.r(KD, 1024), AR.r(KD, 1024), AR.r(KK, 1024), M)
                    for ct in range(4):
                        for c in range(NCH):
                            o = ct * TT + c * 64
                            g.act(AR.r(KDE + o, 64), AR.r(GC + o, 64), AF.Exp, bias=AR.r(GC + o + 63, 1), scale=-1.0)
                    g.tt('pool', AR.r(KDE, 1024), AR.r(KDE, 1024), AR.r(KK, 1024), M)
                    for ct in range(4):
                        wo = w_get("hgi%d" % ct)
                        for c in range(NCH):
                            for k in range(KT):
                                g.mm(PS[4 + c].r(ct * 128, 128, 0, 64), hT.r(k * TT + c * 64, 64), wring.r(wo + k * 128, 128),
                                     start=(k == 0), stop=(k == KT - 1))
                    for c in range(NCH):
                        g.cp('act' if c % 2 == 0 else 'dve', AR.r(ITOK + c * 512, 512, 0, 64), PS[4 + c].r(0, 512, 0, 64))
                    for ct in range(4):
                        pv = proj_fm("hgz%d" % ct)
                        g.act(AR.r(ZS + ct * TT, TT), pv, AF.Silu)
                    OPS = [PS[6], PS[7]]
                    for c in range(NCH):
                        pa = PS[2]
                        for h in range(4):
                            o = h * TT + c * 64
                            g.mm(pa.r(h * 64, 64, 0, 64), AR.r(KD + o, 64), AR.r(QD + o, 64))
                        g.tt('dve', AR.r3(ATM, 4, 64, 0, 64), pa.r3(0, 4, 64, 0, 64), bcast(usq(M01.r(0, 64, 0, 64), 1), [64, 4, 64]), M)
                        pk = PS[3]
                        for h in range(4):
                            o = h * TT + c * 64
                            g.tr(pk.r(h * 128, 128, 0, 64), AR.r(KDE + o, 64))
                        g.cp('act', AR.r(KDT, 512, 0, 64), pk.r(0, 512, 0, 64))
                        for h in range(4):
                            o = h * TT + c * 64
                            iv = AR.r(ITOK + c * 512 + h * 128, 128, 0, 64)
                            Sv = S_h.r(h * 128, 128)
                            ov = OPS[h // 2].r((h % 2) * 256 + c * 64, 64)
                            g.mm(ov, iv, AR.r(ATM + h * 64, 64, 0, 64), start=True, stop=False)
                            g.mm(ov, Sv, AR.r(QD + o, 64), start=False, stop=True)
                            pu = PS[4 + h % 2]
                            g.mm(pu.r(256, 128), AR.r(KDT + h * 128, 128, 0, 64), iv)
                            g.stt(Sv, Sv, AR.r(EG + o + 63, 1), pu.r(256, 128), M, A_)
                    for h in range(4):
                        ov = AR.r(Fo + h * TT, TT)
                        g.cp('act', ov, OPS[h // 2].r((h % 2) * 256, TT))
                        rms_fm([ov], [wsm.r(1, 1)], 128.0 * RMS_EPS, None, LOGF, LOGF + TT)
                        g.tt('pool', f32r(Y[2].r(h * TT, TT)), ov, AR.r(ZS + h * TT, TT), M)

                if 's5' in PHASES:
                    Uo, ZS, T1o, WRE, WIM, ZR, ZI, XRE, XIM, YTOK, YS, GEL = \
                        0, 1024, 2048, 4096, 4608, 5120, 5632, 6144, 6656, 7168, 7680, 8704
                    for ct in range(4):
                        pv = proj_fm("s5_%d" % ct)
                        g.cp('act', AR.r(Uo + ct * TT, TT), pv)
                    for ct in range(4):
                        pv = proj_fm("s5_%d" % (4 + ct))
                        g.act(AR.r(ZS + ct * TT, TT), pv, AF.Silu)
                    for b in range(NB):
                        py = PS[7]
                        for ft in range(4):
                            pre, pim = PS[4], PS[5]
                            uv = AR.r(Uo + ft * TT + b * 128, 128)
                            for q in range(4):
                                pr_ = ft * 4 + q
                                g.mm(pre.r(q * 128, 128), BbT.r(pr_ * 128, 128), uv)
                                g.mm(pim.r(q * 128, 128), BbT.r(2048 + pr_ * 128, 128), uv)
                            ct_ = costab.r(ft * 512, 512)
                            st_ = sintab.r(ft * 512, 512)
                            t = [AR.r(T1o + i * 512, 512) for i in range(4)]
                            g.tt('dve', t[0], pre.r(0, 512), ct_, M)
                            g.tt('dve', t[1], pim.r(0, 512), st_, M)
                            g.tt('dve', t[2], pim.r(0, 512), ct_, M)
                            g.tt('dve', t[3], pre.r(0, 512), st_, M)
                            g.tt('pool', AR.r(WRE, 512), t[0], t[1], A_)
                            g.tt('pool', AR.r(WIM, 512), t[2], t[3], S_)
                            for q in range(4):
                                pr_ = ft * 4 + q
                                rb = bcast(rho.r(pr_, 1), [128, 128])
                                g.scan(AR.r(ZR + q * 128, 128), rb, AR.r(WRE + q * 128, 128), xs5.r(pr_, 1))
                                g.scan(AR.r(ZI + q * 128, 128), rb, AR.r(WIM + q * 128, 128), xs5.r(16 + pr_, 1))
                            g.tt('pool', t[0], AR.r(ZR, 512), ct_, M)
                            g.tt('pool', t[1], AR.r(ZI, 512), st_, M)
                            g.tt('pool', t[2], AR.r(ZI, 512), ct_, M)
                            g.tt('pool', t[3], AR.r(ZR, 512), st_, M)
                            g.tt('pool', AR.r(XRE, 512), t[0], t[1], S_)
                            g.tt('pool', AR.r(XIM, 512), t[2], t[3], A_)
                            g.cp('pool', xs5.r3(ft * 4, 4, 1), AR.rs(XRE, 4, 128, 127, 1))
                            g.cp('pool', xs5.r3(16 + ft * 4, 4, 1), AR.rs(XIM, 4, 128, 127, 1))
                            for q in range(4):
                                pr_ = ft * 4 + q
                                yv = py.r(ft * 128 + q * 32, 32)
                                g.mm(yv, AR.r(XRE + q * 128, 128), Cpd.r(pr_ * 32, 32), start=True, stop=False)
                                g.mm(yv, AR.r(XIM + q * 128, 128), Cpd.r(512 + pr_ * 32, 32), start=False, stop=True)
                        g.cp('act', AR.r(YTOK, 512), py.r(0, 512))
                        pyt = PS[6]
                        for ft in range(4):
                            g.tr(pyt.r(ft * 128, 128), AR.r(YTOK + ft * 128, 128))
                        for ft in range(4):
                            g.stt(AR.r(YS + ft * TT + b * 128, 128), AR.r(Uo + ft * TT + b * 128, 128), ppt.r(PP_SD + ft, 1),
                                  pyt.r(ft * 128, 128), M, A_)
                    x = AR.r(YS, 1024)
                    g1 = AR.r(GEL, 1024)
                    g2 = AR.r(GEL + 1024, 1024)
                    g.tt('pool', g1, x, x, M)
                    g.ts('dve', g1, g1, 0.044715, M, 1.0, A_)
                    g.tt('pool', g1, g1, x, M)
                    g.act(g2, g1, AF.Sigmoid, scale=2.0 * math.sqrt(2.0 / math.pi))
                    g.tt('pool', f32r(YSB.r(0, 4 * TT)), x, g2, M)
                    for ct in range(4):
                        w1 = w_get("glu0_%d" % ct)
                        w2 = w_get("glu1_%d" % ct)
                        p1, p2 = PS[4], PS[5]
                        for k in range(4):
                            g.mm(p1.r(0, TT), wring.r(w1 + k * 128, 128), YSB.r(k * TT, TT), start=(k == 0), stop=(k == 3), r=True)
                        for k in range(4):
                            g.mm(p2.r(0, TT), wring.r(w2 + k * 128, 128), YSB.r(k * TT, TT), start=(k == 0), stop=(k == 3), r=True)
                        sg = AR.r(GEL + ct * TT, TT)
                        g.act(sg, p2.r(0, TT), AF.Sigmoid)
                        g.tt('dve', sg, sg, p1.r(0, TT), M)
                        g.tt('pool', f32r(Y[3].r(ct * TT, TT)), sg, AR.r(ZS + ct * TT, TT), M)

                MIX, SGT, TMPM = 0, 2048, 2304
                for dt in range(8):
                    for bq in range(4):
                        pv = proj_fm("gate%d_%d" % (bq, dt))
                        sg = AR.r(SGT, TT)
                        g.act(sg, pv, AF.Sigmoid)
                        wo = w_get("wbr%d_%d" % (bq, dt))
                        pb_ = PS[2 + bq % 2]
                        for k in range(4):
                            g.mm(pb_.r(0, TT), wring.r(wo + k * 128, 128), Y[bq].r(k * TT, TT), start=(k == 0), stop=(k == 3), r=True)
                        if bq == 0:
                            g.tt('dve', f32r(MIXB.r(dt * TT, TT)), sg, pb_.r(0, TT), M)
                        else:
                            g.tt('dve', AR.r(TMPM, TT), sg, pb_.r(0, TT), M)
                            g.tt('pool', f32r(MIXB.r(dt * TT, TT)), MIXB.r(dt * TT, TT), AR.r(TMPM, TT), A_)
                for k in range(KT):
                    wo = w_get("wout%d" % k)
                    for b in range(NB):
                        for hf in range(2):
                            g.mm(PS[4 + b * 2 + hf].r(0, 512), MIXB.r(k * TT + b * 128, 128), wring.r(wo + hf * 512, 512),
                                 start=(k == 0), stop=(k == KT - 1), r=True)
                for b in range(NB):
                    r0 = tok0 + b * 128
                    for hf in range(2):
                        g.stt(rbuf.r(hf * 512, 512), htok.r(b * D + hf * 512, 512), ALPHA, PS[4 + b * 2 + hf].r(0, 512), M, A_)
                    ln_rows(rbuf.r(0, D), rbuf.r(0, D), prt.r(PR_LNG, D), prt.r(PR_LNB, D))
                    if dyn:
                        if not last:
                            g.dma('pool', DV(rows(dst, r0, 128), dstn), rbuf.r(0, D))
                        else:
                            g.dma('pool', DV(rows(yout, r0 - out_row0, 128), "y"), rbuf.r(0, D))
                        continue
                    lo = max(0, pad - r0)
                    if lo >= 128:
                        continue
                    if not last:
                        g.dma('pool', DV(dst[r0 + lo:r0 + 128, :], dstn), rbuf.r(0, D, lo, 128))
                    else:
                        lo2 = max(0, out_row0 - r0)
                        if lo2 < 128:
                            g.dma('pool', DV(yout[r0 + lo2 - out_row0:r0 + 128 - out_row0, :], "y"), rbuf.r(0, D, lo2, 128))
                assert wst['used'] == len(stream), (wst, len(stream))

            tile_body(0, False)
            kb.barrier()
            if ntiles == 2:
                tile_body(1, False)
                kb.barrier()
            elif ntiles > 2:
                assert out_row0 <= TT
                kb.dry = True
                tile_body(1, True)
                n_it = kb.barrier()
                kb.dry = False
                with nc.Fori(1, ntiles) as ti:
                    kb.enter_loop(ti, n_it, 1)
                    tile_body(ti, True)
                    kb.barrier()
                kb.exit_loop(ntiles - 1)
        kb.barrier()
        print("[build] ninst=%d nwait=%d per_eng=%s" % (kb.ninst, kb.nwait, kb.per_eng), flush=True)
    return nc


def _fm(w, c0, ncol):
    nt = ncol // 128
    x = w[:, c0:c0 + ncol].reshape(KT, 128, nt, 128)
    return np.ascontiguousarray(x.transpose(2, 1, 0, 3)).reshape(nt, 128, KT * 128)


def prep_weights(inp, depth):
    f = lambda a: np.asarray(a, dtype=np.float32)
    L = depth
    w_in = f(inp['w_in'])
    wfm = np.empty((L, 80, 128, KT * 128), np.float32)
    wsc = np.empty((L, 128, KT * 16), np.float32)
    witm = np.empty((L, 4, 128, KT * 128), np.float32)
    for l in range(L):
        w = w_in[l]
        parts = [_fm(w, O_GQKV, 1536), _fm(w, O_GZ, 512), _fm(w, O_MX, 1024), _fm(w, O_MZ, 512),
                 _fm(w, O_HQ, 512), _fm(w, O_HF, 512), _fm(w, O_HZ, 512), _fm(w, O_SU, 512), _fm(w, O_SZ, 512),
                 _fm(w, O_GATE, 4096)]
        wfm[l] = np.concatenate(parts, axis=0)
        sc = np.concatenate([w[:, O_GB:O_GB + 4], w[:, O_GA:O_GA + 4], w[:, O_MDT:O_MDT + 8]], axis=1)
        wsc[l] = sc.reshape(KT, 128, 16).transpose(1, 0, 2).reshape(128, KT * 16)
        witm[l] = _fm(w, O_HI, 512)
    glu = np.stack([f(inp['s5_glu_w1']), f(inp['s5_glu_w2'])], axis=1)
    wglu = np.ascontiguousarray(glu.reshape(L, 2, 4, 128, 4, 128).transpose(0, 1, 4, 3, 2, 5)).reshape(L, 2, 4, 128, 512)
    wb = f(inp['w_branch'])
    wbr = np.ascontiguousarray(wb.reshape(L, 4, 4, 128, 8, 128).transpose(0, 4, 1, 3, 2, 5)).reshape(L, 8, 4, 128, 512)
    wout = np.ascontiguousarray(f(inp['w_out']).reshape(L, KT, 128, D))
    pp = np.zeros((L, 128, NPP), np.float32)
    pr = np.zeros((L, NPR), np.float32)
    for l in range(L):
        pp[l, :, PP_CWG:PP_CWG + 48] = f(inp['gdn_conv_w'])[l].reshape(4, 12, 128).transpose(2, 1, 0).reshape(128, 48)
        pp[l, :, PP_CWM:PP_CWM + 32] = f(inp['m2_conv_w'])[l].reshape(4, 8, 128).transpose(2, 1, 0).reshape(128, 32)
        pp[l, :, PP_CBM:PP_CBM + 8] = f(inp['m2_conv_b'])[l].reshape(8, 128).T
        pp[l, :, PP_GNW] = f(inp['gdn_norm_w'])[l]
        pp[l, :, PP_MD:PP_MD + 4] = np.repeat(f(inp['m2_D'])[l], 64).reshape(4, 128).T
        pp[l, :, PP_MNW:PP_MNW + 4] = f(inp['m2_norm_w'])[l].reshape(4, 128).T
        pp[l, :, PP_HNW] = f(inp['hg_norm_w'])[l]
        pp[l, :, PP_SD:PP_SD + 4] = f(inp['s5_D'])[l].reshape(4, 128).T
        pp[l, :, PP_SAR:PP_SAR + 16] = f(inp['s5_A_re'])[l].reshape(16, 128).T
        pp[l, :, PP_SAI:PP_SAI + 16] = f(inp['s5_A_im'])[l].reshape(16, 128).T
        pp[l, :, PP_SLDT:PP_SLDT + 16] = np.repeat(f(inp['s5_log_dt'])[l], 64).reshape(16, 128).T
        pr[l, PR_A12:PR_A12 + 4] = f(inp['gdn_A_log'])[l]
        pr[l, PR_A12 + 4:PR_A12 + 12] = f(inp['m2_A_log'])[l]
        pr[l, PR_B12:PR_B12 + 4] = f(inp['gdn_dt_bias'])[l]
        pr[l, PR_B12 + 4:PR_B12 + 12] = f(inp['m2_dt_bias'])[l]
        pr[l, PR_LNG:PR_LNG + D] = f(inp['ln_g'])[l]
        pr[l, PR_LNB:PR_LNB + D] = f(inp['ln_b'])[l]
    lbl = np.ascontiguousarray(f(inp['hg_lb_logits']).reshape(L, 4, 128).transpose(2, 1, 0)).reshape(128, 4 * L)
    s5b = np.zeros((L, 2, 128, 16, 128), np.float32)
    s5c = np.zeros((L, 2, 128, 16, 32), np.float32)
    for j, (bn, cn) in enumerate((('s5_B_re', 's5_C_re'), ('s5_B_im', 's5_C_im'))):
        Bm = f(inp[bn])
        Cm = f(inp[cn])
        for pr_ in range(16):
            for g2 in range(2):
                gi = 2 * pr_ + g2
                col = (pr_ % 4) * 32 + g2 * 16
                s5b[:, j, g2 * 64:(g2 + 1) * 64, pr_, col:col + 16] = Bm[:, gi]
                s5c[:, j, g2 * 64:(g2 + 1) * 64, pr_, g2 * 16:(g2 + 1) * 16] = Cm[:, gi].transpose(0, 2, 1)
    lnin = np.stack([f(inp['ln_in_g']), f(inp['ln_in_b'])], axis=0)
    return dict(lnin=lnin, wfm=wfm, wsc=wsc, witm=witm, wglu=wglu, wbr=wbr, wout=wout, pp=pp, pr=pr, lbl=lbl,
                s5b=s5b.reshape(L, 2, 128, 2048), s5c=s5c.reshape(L, 2, 128, 512))


def make_xin(x_b, meta, Tp):
    T = x_b.shape[0] + meta.shape[0]
    xin = np.zeros((Tp, D), np.float32)
    xin[Tp - T:Tp - T + meta.shape[0]] = meta
    xin[Tp - x_b.shape[0]:] = x_b
    return xin


_NC_CACHE = {}


def kernel(**inputs):
    x = np.asarray(inputs['x'], dtype=np.float32)
    Bsz, SEQ, _ = x.shape
    depth = int(np.asarray(inputs['w_in']).shape[0])
    T = SEQ + NMETA
    ntiles = (T + TT - 1) // TT
    Tp = ntiles * TT
    ws = prep_weights(inputs, depth)
    meta = np.asarray(inputs['meta_tokens'], dtype=np.float32)
    key = (T, depth, SEQ)
    if key not in _NC_CACHE:
        _NC_CACHE[key] = build(T, depth, SEQ)
    nc = _NC_CACHE[key]
    in_maps = []
    for b in range(Bsz):
        m = dict(ws)
        m['xin'] = make_xin(x[b], meta, Tp)
        in_maps.append(m)
    res = run_bass_kernel_spmd(nc, in_maps, core_ids=list(range(Bsz)))
    return np.stack([np.asarray(res.results[b]['y'], dtype=np.float32) for b in range(Bsz)], axis=0)
```

```python
import math
import os
import numpy as np
from contextlib import ExitStack
import concourse.bass as bass
import concourse.mybir as mybir
from concourse.bass_utils import run_bass_kernel_spmd

F32 = mybir.dt.float32
F32R = mybir.dt.float32r
FAST_MM = True
POOL_TO = os.environ.get('KPOOLTO', 'dve')
AF = mybir.ActivationFunctionType
ALU = mybir.AluOpType
AX = mybir.AxisListType

D = 1024
TT = 256
NB = TT // 128
NCH = TT // 64
KT = 8
NMETA = 16
BIG = 30000.0
ALPHA = 8.0 ** 0.25
LN_EPS = 1e-5
RMS_EPS = 1e-6
N_IN = 10768
S5S = 128

O_GQKV, O_GZ, O_GB, O_GA, O_MX, O_MZ, O_MDT = 0, 1536, 2048, 2052, 2056, 3080, 3592
O_HQ, O_HF, O_HI, O_HZ, O_SU, O_SZ, O_GATE = 3600, 4112, 4624, 5136, 5648, 6160, 6672

PP_CWG, PP_CWM, PP_CBM, PP_GNW, PP_MD, PP_MNW, PP_HNW, PP_SD, PP_SAR, PP_SAI, PP_SLDT = \
    0, 48, 80, 88, 89, 93, 97, 98, 102, 118, 134
NPP = 150
PR_A12, PR_B12, PR_LNG, PR_LNB = 0, 12, 24, 24 + 1024
NPR = 24 + 2048
CH = 128
PHASES = os.environ.get('KPHASES', 'gdn,ssd,hg,s5').split(',')


class V:
    __slots__ = ('ap', 'keys')

    def __init__(self, ap, keys):
        self.ap = ap
        self.keys = keys


class Buf:
    def __init__(self, h, name, ch=CH):
        self.h = h
        self.name = name
        self.ch = ch

    def keys(self, off, n):
        return [(self.name, c) for c in range(off // self.ch, (off + n - 1) // self.ch + 1)]

    def r(self, off, n, p0=0, p1=128):
        return V(self.h[p0:p1, off:off + n], self.keys(off, n))

    def r3(self, off, a, s, p0=0, p1=128):
        return V(self.h[p0:p1, off:off + a * s].rearrange("p (a s) -> p a s", s=s), self.keys(off, a * s))

    def rs(self, off, a, stride, lo, n, p0=0, p1=128):
        return V(self.h[p0:p1, off:off + a * stride].rearrange("p (a s) -> p a s", s=stride)[:, :, lo:lo + n],
                 self.keys(off, a * stride))


def f32r(v):
    return V(v.ap.bitcast(F32R), v.keys) if FAST_MM else v


def bcast(v, shape):
    return V(v.ap.to_broadcast(shape), v.keys)


def usq(v, axis):
    return V(v.ap.unsqueeze(axis), v.keys)


class KB:
    ND = 8

    def __init__(self, nc, es):
        self.nc = nc
        self.E = {'pe': nc.tensor, 'dve': nc.vector, 'act': nc.scalar, 'pool': nc.gpsimd, 'sp': nc.sync}
        self.sem = {k: es.enter_context(nc.semaphore("s_" + k)) for k in self.E}
        self.cnt = {k: 0 for k in self.E}
        self.seen = {k: {} for k in self.E}
        self.dsem = [es.enter_context(nc.semaphore("d%d" % i)) for i in range(2 * self.ND)]
        self.dcnt = [0] * (2 * self.ND)
        self.dnext = {'sp': 0, 'pool': 0, 'act': 0}
        self.last_w = {}
        self.readers = {}
        self.nwait = 0
        self.ninst = 0
        self.per_eng = {k: 0 for k in self.E}
        self.base = {}
        self.loop = None
        self.dry = False
        self.tmpreg = {k: self.E[k].alloc_register("kbtmp_" + k) for k in self.E}

    def _wait(self, eng, tok):
        kind, a, v = tok
        if kind == 'c':
            if a == eng and eng == 'pe':
                return
            key = a
            sem = self.sem[a]
        else:
            key = ('d', a)
            sem = self.dsem[a]
        if self.seen[eng].get(key, 0) >= v:
            return
        if not self.dry:
            b = self.base.get(key, 0)
            nk = 0 if self.loop is None else self.loop[1].get(key, 0)
            if nk == 0:
                self.E[eng].wait_ge(sem, b + v)
            else:
                ti, n, start = self.loop
                r = self.tmpreg[eng]
                self.E[eng].reg_mul(r, ti, nk)
                self.E[eng].reg_add(r, r, b - start * nk + v)
                self.E[eng].wait_ge(sem, r)
            self.nwait += 1
        self.seen[eng][key] = v

    def _absval(self, key, v):
        b = self.base.get(key, 0)
        if self.loop is None:
            return b + v
        ti, n, start = self.loop
        nk = n.get(key, 0)
        if nk == 0:
            return b + v
        return ti * nk + (b - start * nk + v)

    def _deps(self, eng, reads, writes):
        for r in reads:
            t = self.last_w.get(r)
            if t is not None:
                self._wait(eng, t)
        for w in writes:
            t = self.last_w.get(w)
            if t is not None:
                self._wait(eng, t)
            for t in self.readers.get(w, ()):
                self._wait(eng, t)

    def _commit(self, tok, reads, writes):
        ws = set(writes)
        for w in writes:
            self.last_w[w] = tok
            self.readers[w] = []
        for r in reads:
            if r in ws:
                continue
            lst = self.readers.setdefault(r, [])
            lst.append(tok)
            if len(lst) > 8:
                d = {}
                for t in lst:
                    d[(t[0], t[1])] = t
                self.readers[r] = list(d.values())

    def op(self, eng, fn, reads=(), writes=()):
        self._deps(eng, reads, writes)
        self.cnt[eng] += 1
        if not self.dry:
            ins = fn(self.E[eng])
            ins.then_inc(self.sem[eng], 1)
            self.ninst += 1
            self.per_eng[eng] += 1
        self._commit(('c', eng, self.cnt[eng]), reads, writes)

    def dma(self, q, out, in_, reads=(), writes=()):
        self._deps(q, reads, writes)
        i = self.dnext[q] % self.ND + (self.ND if q == 'pool' else 0)
        self.dnext[q] += 1
        if self.dcnt[i] > 0:
            self._wait(q, ('d', i, self.dcnt[i]))
        if not self.dry:
            self.E[q].dma_start(out=out, in_=in_).then_inc(self.dsem[i], 16)
            self.ninst += 1
        self.dcnt[i] += 16
        self._commit(('d', i, self.dcnt[i]), reads, writes)

    def reset(self):
        for k in self.E:
            self.cnt[k] = 0
            self.seen[k] = {}
        for i in range(len(self.dcnt)):
            self.dcnt[i] = 0
        for q in self.dnext:
            self.dnext[q] = 0
        self.last_w = {}
        self.readers = {}

    def counts(self):
        n = {k: self.cnt[k] for k in self.E}
        for i in range(len(self.dcnt)):
            n[('d', i)] = self.dcnt[i]
        return n

    def barrier(self):
        self.finish('sp')
        self.cnt['sp'] += 1
        if not self.dry:
            self.E['sp'].sem_inc(self.sem['sp'], 1)
            self.ninst += 1
        for e in self.E:
            if e != 'sp':
                self._wait(e, ('c', 'sp', self.cnt['sp']))
        n = self.counts()
        if self.loop is None and not self.dry:
            for k, v in n.items():
                self.base[k] = self.base.get(k, 0) + v
        self.reset()
        return n

    def enter_loop(self, ti, n, start):
        self.loop = (ti, n, start)

    def exit_loop(self, niter):
        ti, n, start = self.loop
        self.loop = None
        for k, v in n.items():
            self.base[k] = self.base.get(k, 0) + niter * v

    def finish(self, q='sp'):
        for i in range(2 * self.ND):
            if self.dcnt[i] > 0:
                self._wait(q, ('d', i, self.dcnt[i]))
        for e in self.E:
            if e != q and self.cnt[e] > 0:
                self._wait(q, ('c', e, self.cnt[e]))


class G:
    def __init__(self, nc, es):
        self.nc = nc
        self.es = es
        self.kb = KB(nc, es)
        self.ident = None

    def sb(self, name, ncols, dt=F32):
        return Buf(self.es.enter_context(self.nc.sbuf_tensor(name, [128, ncols], dt)), name)

    def ps(self, name, ncols, dt=F32):
        return Buf(self.es.enter_context(self.nc.psum_tensor(name, [128, ncols], dt)), name, ch=512)

    def mm(self, out, lhsT, rhs, start=True, stop=True, r=False):
        la, ra = lhsT.ap, rhs.ap
        if r and FAST_MM:
            la = la.bitcast(F32R)
            ra = ra.bitcast(F32R)
        self.kb.op('pe', lambda e: e.matmul(out.ap, lhsT=la, rhs=ra, start=start, stop=stop),
                   reads=lhsT.keys + rhs.keys, writes=out.keys)

    def tr(self, out, in_, n=128):
        idn = self.ident.r(0, n, 0, n)
        self.kb.op('pe', lambda e: e.transpose(out.ap, in_.ap, idn.ap),
                   reads=in_.keys + idn.keys, writes=out.keys)

    def act(self, out, in_, func, bias=0.0, scale=1.0):
        rd = list(in_.keys)
        b = bias
        if isinstance(bias, V):
            rd += bias.keys
            b = bias.ap
        self.kb.op('act', lambda e: e.activation(out=out.ap, in_=in_.ap, func=func, bias=b, scale=scale),
                   reads=rd, writes=out.keys)

    def tt(self, eng, out, a, b, op):
        if eng == 'pool':
            eng = POOL_TO
        self.kb.op(eng, lambda e: e.tensor_tensor(out=out.ap, in0=a.ap, in1=b.ap, op=op),
                   reads=a.keys + b.keys, writes=out.keys)

    def ts(self, eng, out, a, s1, op0, s2=None, op1=None):
        if eng == 'pool':
            eng = POOL_TO
        rd = list(a.keys)
        x1, x2 = s1, s2
        if isinstance(s1, V):
            rd += s1.keys
            x1 = s1.ap
        if isinstance(s2, V):
            rd += s2.keys
            x2 = s2.ap
        if op1 is None:
            self.kb.op(eng, lambda e: e.tensor_scalar(out=out.ap, in0=a.ap, scalar1=x1, scalar2=None, op0=op0),
                       reads=rd, writes=out.keys)
        else:
            self.kb.op(eng, lambda e: e.tensor_scalar(out=out.ap, in0=a.ap, scalar1=x1, scalar2=x2, op0=op0, op1=op1),
                       reads=rd, writes=out.keys)

    def stt(self, out, in0, scalar, in1, op0, op1):
        rd = in0.keys + in1.keys
        x = scalar
        if isinstance(scalar, V):
            rd = rd + scalar.keys
            x = scalar.ap
        self.kb.op('dve', lambda e: e.scalar_tensor_tensor(out=out.ap, in0=in0.ap, scalar=x, in1=in1.ap, op0=op0, op1=op1),
                   reads=rd, writes=out.keys)

    def cp(self, eng, out, in_):
        if eng == 'pool':
            eng = POOL_TO
        if eng == 'act':
            self.kb.op(eng, lambda e: e.copy(out=out.ap, in_=in_.ap), reads=in_.keys, writes=out.keys)
        else:
            self.kb.op(eng, lambda e: e.tensor_copy(out=out.ap, in_=in_.ap), reads=in_.keys, writes=out.keys)

    def scan(self, out, d0, d1, init):
        rd = d0.keys + d1.keys
        x = init
        if isinstance(init, V):
            rd = rd + init.keys
            x = init.ap
        self.kb.op('dve', lambda e: e.tensor_tensor_scan(out=out.ap, data0=d0.ap, data1=d1.ap, initial=x,
                                                          op0=ALU.mult, op1=ALU.add), reads=rd, writes=out.keys)

    def memset(self, eng, out, val):
        self.kb.op(eng, lambda e: e.memset(out.ap, val), writes=out.keys)

    def asel(self, out, in_, pattern, cmp, fill, base, cm):
        self.kb.op('pool', lambda e: e.affine_select(out=out.ap, in_=in_.ap, pattern=pattern, compare_op=cmp,
                                                     fill=fill, base=base, channel_multiplier=cm),
                   reads=in_.keys, writes=out.keys)

    def rsqrt(self, out, in_, addc):
        self.kb.op('act', lambda e: e.activation(out=out.ap, in_=in_.ap, func=AF.Sqrt, bias=addc, scale=1.0),
                   reads=in_.keys, writes=out.keys)
        self.recip(out, out)

    def recip(self, out, in_):
        self.kb.op('dve', lambda e: e.reciprocal(out=out.ap, in_=in_.ap), reads=in_.keys, writes=out.keys)

    def dma(self, q, out, in_):
        self.kb.dma(q, out.ap, in_.ap, reads=in_.keys, writes=out.keys)


def build(T, depth, seq_out):
    ntiles = (T + TT - 1) // TT
    Tp = ntiles * TT
    pad = Tp - T
    out_row0 = Tp - seq_out
    nc = bass.Bass("TRN2", target_bir_lowering=False)
    L = depth

    def dram(name, shape, kind="ExternalInput"):
        return nc.dram_tensor(name, shape, F32, kind=kind).ap()

    xin = dram("xin", [Tp, D])
    lnin = dram("lnin", [2, D])
    wfm = dram("wfm", [L, 80, 128, KT * 128])
    wsc = dram("wsc", [L, 128, KT * 16])
    witm = dram("witm", [L, 4, 128, KT * 128])
    wglu = dram("wglu", [L, 2, 4, 128, 4 * 128])
    wbr = dram("wbr", [L, 8, 4, 128, 4 * 128])
    wout = dram("wout", [L, KT, 128, D])
    pp_d = dram("pp", [L, 128, NPP])
    pr_d = dram("pr", [L, NPR])
    lbl_d = dram("lbl", [128, 4 * depth])
    s5b_d = dram("s5b", [L, 2, 128, 16 * 128])
    s5c_d = dram("s5c", [L, 2, 128, 16 * 32])
    yout = dram("y", [seq_out, D], kind="ExternalOutput")
    hb = [dram("hbuf%d" % i, [Tp, D], kind="Internal") for i in range(2)]

    def DV(ap, name, blk=0):
        return V(ap, [("dram_" + name, blk)])

    with ExitStack() as es:
        g = G(nc, es)
        kb = g.kb
        sb, ps = g.sb, g.ps
        M, A_, S_ = ALU.mult, ALU.add, ALU.subtract
        ident = sb("ident", 128)
        g.ident = ident
        ones = sb("ones", 128)
        zeros = sb("zeros", 128)
        NEGS = sb("NEGS", 128)
        NEGT = sb("NEGT", 128)
        TRIU = sb("TRIU", 128)
        M01 = sb("M01", 64)
        SEG = sb("SEG", TT)
        g.memset('pool', zeros.r(0, 128), 0.0)
        g.memset('pool', ones.r(0, 128), 1.0)
        g.asel(ident.r(0, 128), zeros.r(0, 128), [[-1, 128]], ALU.not_equal, 1.0, 0, 1)
        g.asel(NEGS.r(0, 128), zeros.r(0, 128), [[-1, 128]], ALU.is_gt, BIG, 0, 1)
        g.asel(NEGT.r(0, 128), zeros.r(0, 128), [[1, 128]], ALU.is_ge, -BIG, 0, -1)
        g.asel(TRIU.r(0, 128), ones.r(0, 128), [[1, 128]], ALU.is_ge, 0.0, 0, -1)
        g.asel(M01.r(0, 64, 0, 64), ones.r(0, 64, 0, 64), [[1, 64]], ALU.is_ge, 0.0, 0, -1)
        g.memset('pool', SEG.r(0, TT), 1.0)
        for c in range(NCH):
            g.memset('pool', SEG.r(c * 64, 1), 0.0)

        NW = 6
        wring = sb("wring", NW * 1024)
        htok = sb("htok", NB * D)
        hT = sb("hT", KT * TT)
        ppt = sb("ppt", NPP)
        prt = sb("prt", NPR)
        lbl = sb("lbl_s", 4 * depth)
        lbt = sb("lbt", depth * 4)
        omlb = sb("omlb", depth * 4)
        negA12 = sb("negA12", 12)
        wsm = sb("wsm", 8)
        halo_g = sb("halo_g", 12 * 3)
        halo_m = sb("halo_m", 8 * 3)
        S_g = sb("S_g", 4 * 128)
        S_m = sb("S_m", 8 * 128)
        S_h = sb("S_h", 4 * 128)
        xs5 = sb("xs5", 32)
        BbT = sb("BbT", 2 * 16 * 128)
        Cpd = sb("Cpd", 2 * 16 * 32)
        costab = sb("costab", 16 * S5S)
        sintab = sb("sintab", 16 * S5S)
        rho = sb("rho", 16)
        Y = [sb("y%d" % i, 4 * TT) for i in range(4)]
        padm = sb("padm", NB)
        scb = sb("scb", 16 * 16)
        scb2 = sb("scb2", 16 * 16)
        bcx = sb("bcx", 8 * 128)
        ebx = sb("ebx", 8 * 128)
        wscb = sb("wscb", KT * 16)
        lnst = sb("lnst", 16)
        rbuf = sb("rbuf", D)
        NAR = 14208
        AR = sb("AR", NAR)
        YSB = sb("ysb", 4 * TT)
        MIXB = sb("mixb", 8 * TT)
        PS = [ps("ps%d" % i, 512) for i in range(8)]
        if len(PHASES) < 4:
            for yb_ in Y:
                for q_ in range(8):
                    g.cp('pool', f32r(yb_.r(q_ * 128, 128)), zeros.r(0, 128))

        stream = []

        def layer_stream(l):
            it = []
            for ct in range(16):
                it.append(("gdn%d" % ct, wfm[l, ct, :, :], 1024))
            for ct in range(12):
                it.append(("m2_%d" % ct, wfm[l, 16 + ct, :, :], 1024))
            for ct in range(8):
                it.append(("hgqf%d" % ct, wfm[l, 28 + ct, :, :], 1024))
            for ct in range(4):
                it.append(("hgi%d" % ct, witm[l, ct, :, :], 1024))
            for ct in range(4):
                it.append(("hgz%d" % ct, wfm[l, 36 + ct, :, :], 1024))
            for ct in range(8):
                it.append(("s5_%d" % ct, wfm[l, 40 + ct, :, :], 1024))
            for ct in range(4):
                for j in range(2):
                    it.append(("glu%d_%d" % (j, ct), wglu[l, j, ct, :, :], 512))
            for dt in range(8):
                for b in range(4):
                    it.append(("gate%d_%d" % (b, dt), wfm[l, 48 + b * 8 + dt, :, :], 1024))
                    it.append(("wbr%d_%d" % (b, dt), wbr[l, dt, b, :, :], 512))
            for k in range(KT):
                it.append(("wout%d" % k, wout[l, k, :, :], 1024))
            pref = []
            if 'gdn' not in PHASES:
                pref.append('gdn')
            if 'ssd' not in PHASES:
                pref.append('m2_')
            if 'hg' not in PHASES:
                pref.append('hg')
            if 's5' not in PHASES:
                pref += ['s5_', 'glu']
            it = [x for x in it if not any(x[0].startswith(q) for q in pref)]
            return it

        wst = {'issued': 0, 'used': 0}
        PF = 4

        def w_begin(l):
            stream[:] = layer_stream(l)
            wst['issued'] = 0
            wst['used'] = 0

        def w_issue():
            i = wst['issued']
            key, src, ncols = stream[i]
            slot = i % NW
            g.dma('pool', f32r(wring.r(slot * 1024, ncols)), DV(src, "w"))
            wst['issued'] += 1

        def w_get(key):
            i = wst['used']
            assert stream[i][0] == key, (stream[i][0], key)
            while wst['issued'] <= min(i + PF, len(stream) - 1):
                w_issue()
            wst['used'] += 1
            return (i % NW) * 1024

        def ln_rows(dst, src_sb, gam, bet):
            for c in range(2):
                kb.op('dve', lambda e: e.bn_stats(out=lnst.h[:, c * 6:(c + 1) * 6], in_=src_sb.ap[:, c * 512:(c + 1) * 512]),
                      reads=src_sb.keys, writes=lnst.keys(0, 16))
            kb.op('dve', lambda e: e.bn_aggr(out=lnst.h[:, 12:14], in_=lnst.h[:, 0:12].rearrange("p (c s) -> p c s", c=2)),
                  reads=lnst.keys(0, 16), writes=lnst.keys(0, 16))
            g.rsqrt(lnst.r(14, 1), lnst.r(13, 1), LN_EPS)
            g.ts('dve', dst, src_sb, lnst.r(12, 1), S_, lnst.r(14, 1), M)
            g.tt('pool', dst, dst, gam, M)
            g.tt('pool', dst, dst, bet, A_)

        def rmsnorm_fm(yv, ntile, wcol, scale_n, tmp_off, psb):
            pass

        if pad > 0:
            g.memset('dve', AR.r(0, D), 0.0)
            r = 0
            while r < pad:
                n = min(128, pad - r)
                for i in range(2):
                    g.dma('sp', DV(hb[i][r:r + n, :], "hb%d" % i), AR.r(0, D, 0, n))
                r += n

        g.dma('sp', prt.r(0, D), DV(lnin[0:1, :].to_broadcast([128, D]), "lnin"))
        g.dma('sp', prt.r(D, D), DV(lnin[1:2, :].to_broadcast([128, D]), "lnin"))
        for t in range(ntiles):
            for b in range(NB):
                r0 = t * TT + b * 128
                lo = max(0, pad - r0)
                if lo >= 128:
                    continue
                j = b % NB
                g.dma('sp', htok.r(j * D, D), DV(xin[r0:r0 + 128, :], "xin"))
                ln_rows(rbuf.r(0, D), htok.r(j * D, D), prt.r(0, D), prt.r(D, D))
                g.dma('pool', DV(hb[0][r0 + lo:r0 + 128, :], "hb0"), rbuf.r(0, D, lo, 128))

        g.dma('sp', lbl.r(0, 4 * depth), DV(lbl_d[:, :], "lbl"))
        g.act(lbl.r(0, 4 * depth), lbl.r(0, 4 * depth), AF.Exp)
        kb.op('dve', lambda e: e.tensor_reduce(out=AR.h[:, 0:4], in_=lbl.h[:, :].rearrange("p (c l) -> p c l", l=depth),
                                               axis=AX.X, op=ALU.add), reads=lbl.keys(0, 4 * depth), writes=AR.keys(0, 4))
        g.recip(AR.r(4, 4), AR.r(0, 4))
        g.memset('dve', lbt.r(0, 4), 0.0)
        for l in range(1, depth):
            ev = V(lbl.h[:, :].rearrange("p (c l) -> p c l", l=depth)[:, :, l], lbl.keys(0, 4 * depth))
            g.tt('dve', lbt.r(l * 4, 4), lbt.r((l - 1) * 4, 4), ev, A_)
        for l in range(1, depth):
            g.tt('dve', lbt.r(l * 4, 4), lbt.r(l * 4, 4), AR.r(4, 4), M)
        g.ts('dve', omlb.r(0, 4 * depth), lbt.r(0, 4 * depth), -1.0, M, 1.0, A_)

        for l in range(L):
            src = hb[l % 2]
            srcn = "hb%d" % (l % 2)
            dst = hb[(l + 1) % 2]
            dstn = "hb%d" % ((l + 1) % 2)
            last = (l == L - 1)
            g.dma('sp', ppt.r(0, NPP), DV(pp_d[l, :, :], "pp"))
            g.dma('sp', wscb.r(0, KT * 16), DV(wsc[l, :, :], "wsc"))
            g.dma('sp', prt.r(0, NPR), DV(pr_d[l:l + 1, :].to_broadcast([128, NPR]), "pr"))
            g.act(negA12.r(0, 12), prt.r(PR_A12, 12), AF.Exp)
            g.ts('dve', negA12.r(0, 12), negA12.r(0, 12), -1.0, M)
            g.ts('dve', wsm.r(0, 1), ppt.r(PP_GNW, 1), math.sqrt(128.0), M)
            g.ts('dve', wsm.r(1, 1), ppt.r(PP_HNW, 1), math.sqrt(128.0), M)
            g.ts('dve', wsm.r(2, 4), ppt.r(PP_MNW, 4), 16.0, M)
            for st_, n_ in ((halo_g, 36), (halo_m, 24), (S_g, 512), (S_m, 1024), (S_h, 512), (xs5, 32)):
                g.memset('pool', st_.r(0, n_), 0.0)
            a_re = ppt.r(PP_SAR, 16)
            a_im = ppt.r(PP_SAI, 16)

            def sl(i):
                return AR.r(8192 + i * 16, 16)
            DT_, ARD, ANG, T1, T2, COS, SIN, LRE, LIM, DEN, ZRE, ZIM = range(12)
            g.act(sl(DT_), ppt.r(PP_SLDT, 16), AF.Exp)
            g.tt('dve', sl(ARD), a_re, sl(DT_), M)
            g.act(rho.r(0, 16), sl(ARD), AF.Exp)
            g.tt('dve', sl(ANG), a_im, sl(DT_), M)
            g.act(sl(SIN), sl(ANG), AF.Sin, scale=1.0 / 32.0)
            g.ts('dve', sl(T2), sl(ANG), 1.0 / 32.0, M, 0.5 * math.pi, A_)
            g.act(sl(COS), sl(T2), AF.Sin)
            for _ in range(5):
                g.tt('dve', sl(T1), sl(COS), sl(COS), M)
                g.tt('dve', sl(T2), sl(SIN), sl(SIN), M)
                g.stt(sl(SIN), sl(SIN), 2.0, sl(COS), M, M)
                g.tt('dve', sl(COS), sl(T1), sl(T2), S_)
            g.tt('dve', sl(LRE), rho.r(0, 16), sl(COS), M)
            g.tt('dve', sl(LIM), rho.r(0, 16), sl(SIN), M)
            g.tt('dve', sl(DEN), a_re, a_re, M)
            g.tt('dve', sl(T1), a_im, a_im, M)
            g.tt('dve', sl(DEN), sl(DEN), sl(T1), A_)
            g.recip(sl(DEN), sl(DEN))
            g.ts('dve', sl(LRE), sl(LRE), -1.0, A_)
            g.tt('dve', sl(T1), sl(LRE), a_re, M)
            g.tt('dve', sl(T2), sl(LIM), a_im, M)
            g.tt('dve', sl(ZRE), sl(T1), sl(T2), A_)
            g.tt('dve', sl(ZRE), sl(ZRE), sl(DEN), M)
            g.tt('dve', sl(T1), sl(LIM), a_re, M)
            g.tt('dve', sl(T2), sl(LRE), a_im, M)
            g.tt('dve', sl(ZIM), sl(T1), sl(T2), S_)
            g.tt('dve', sl(ZIM), sl(ZIM), sl(DEN), M)
            g.cp('dve', costab.rs(0, 16, S5S, 0, 1), usq(sl(COS), 2))
            g.cp('dve', sintab.rs(0, 16, S5S, 0, 1), usq(sl(SIN), 2))
            n = 1
            while n < S5S:
                cn = bcast(costab.rs(0, 16, S5S, n - 1, 1), [128, 16, n])
                sn = bcast(sintab.rs(0, 16, S5S, n - 1, 1), [128, 16, n])
                t1 = AR.r3(8448, 16, n)
                t2 = AR.r3(9472, 16, n)
                c0 = costab.rs(0, 16, S5S, 0, n)
                s0 = sintab.rs(0, 16, S5S, 0, n)
                g.tt('dve', t1, c0, cn, M)
                g.tt('dve', t2, s0, sn, M)
                g.tt('dve', costab.rs(0, 16, S5S, n, n), t1, t2, S_)
                g.tt('dve', t1, s0, cn, M)
                g.tt('dve', t2, c0, sn, M)
                g.tt('dve', sintab.rs(0, 16, S5S, n, n), t1, t2, A_)
                n *= 2
            Bre = AR.r3(0, 16, 128)
            Bim = AR.r3(2048, 16, 128)
            X1 = AR.r3(4096, 16, 128)
            X2 = AR.r3(6144, 16, 128)
            g.dma('sp', AR.r(0, 2048), DV(s5b_d[l, 0, :, :], "s5b"))
            g.dma('sp', AR.r(2048, 2048), DV(s5b_d[l, 1, :, :], "s5b"))
            zre_b = bcast(usq(sl(ZRE), 2), [128, 16, 128])
            zim_b = bcast(usq(sl(ZIM), 2), [128, 16, 128])
            g.tt('dve', X1, Bre, zre_b, M)
            g.tt('dve', X2, Bim, zim_b, M)
            g.tt('dve', X1, X1, X2, S_)
            g.tt('dve', X2, Bim, zre_b, M)
            g.tt('dve', Bim, Bre, zim_b, M)
            g.tt('dve', X2, X2, Bim, A_)
            for ri, xo in enumerate((4096, 6144)):
                for pr4 in range(4):
                    pst = PS[pr4 % 2]
                    for q in range(4):
                        pr_ = pr4 * 4 + q
                        g.tr(pst.r(q * 128, 128), AR.r(xo + pr_ * 128, 128))
                    g.cp('act', BbT.r(ri * 2048 + pr4 * 512, 512), pst.r(0, 512))
            g.dma('sp', Cpd.r(0, 512), DV(s5c_d[l, 0, :, :], "s5c"))
            g.dma('sp', Cpd.r(512, 512), DV(s5c_d[l, 1, :, :], "s5c"))
            g.ts('pool', Cpd.r(512, 512), Cpd.r(512, 512), -1.0, M)

            def rows(ap, r0, n):
                if isinstance(r0, int):
                    return ap[r0:r0 + n, :]
                return ap[bass.ds(r0, n), :]

            def tile_body(t, dyn):
                tok0 = t * TT
                has_pad = (not dyn) and (tok0 < pad)
                w_begin(l)
                for b in range(NB):
                    r0 = tok0 + b * 128
                    g.dma('sp', htok.r(b * D, D), DV(rows(src, r0, 128), srcn))
                for k in range(KT):
                    pst = PS[k % 2]
                    for b in range(NB):
                        g.tr(pst.r(b * 128, 128), htok.r(b * D + k * 128, 128))
                    g.cp('act' if k % 2 == 0 else 'dve', f32r(hT.r(k * TT, TT)), pst.r(0, TT))
                if has_pad:
                    for b in range(NB):
                        lo = pad - (tok0 + b * 128)
                        g.asel(padm.r(b, 1), ones.r(0, 1), [[0, 1]], ALU.is_ge, 0.0, -lo, 1)

                pp_state = {'i': 0}
                W3 = TT + 3

                def proj_fm(key):
                    wo = w_get(key)
                    pst = PS[pp_state['i'] % 2]
                    pp_state['i'] += 1
                    for k in range(KT):
                        g.mm(pst.r(0, TT), wring.r(wo + k * 128, 128), hT.r(k * TT, TT), start=(k == 0), stop=(k == KT - 1), r=True)
                    return pst.r(0, TT)

                def block_scalars(b, c0, nh, sc_t=None, bxo=0):
                    sc_t = scb if sc_t is None else sc_t
                    R = lambda i, n=nh, o=0: sc_t.r(i * 16 + o, n)
                    pst = PS[2]
                    for k in range(KT):
                        g.mm(pst.r(0, 16), hT.r(k * TT + b * 128, 128), wscb.r(k * 16, 16),
                             start=(k == 0), stop=(k == KT - 1))
                    g.cp('dve', R(0, 16), pst.r(0, 16))
                    if c0 == 4:
                        g.act(R(1, 4), R(0, 4), AF.Sigmoid)
                    bo = PR_B12 + (0 if c0 == 4 else 4)
                    ao = 0 if c0 == 4 else 4
                    g.tt('dve', R(2), R(0, nh, c0), prt.r(bo, nh), A_)
                    g.act(R(3), R(2), AF.Abs)
                    g.act(R(3), R(3), AF.Exp, scale=-1.0)
                    g.act(R(3), R(3), AF.Ln, bias=1.0)
                    g.ts('dve', R(2), R(2), 0.0, ALU.max)
                    g.tt('dve', R(4), R(2), R(3), A_)
                    if has_pad and c0 == 8:
                        g.ts('dve', R(4), R(4), padm.r(b, 1), M)
                    g.tt('dve', R(5), R(4), negA12.r(ao, nh), M)
                    g.mm(pst.r(16, nh), TRIU.r(0, 128), R(5))
                    g.cp('dve', R(6), pst.r(16, nh))
                    Rv = AR.r3(11392, nh, 128)
                    g.tt('pool', Rv, bcast(usq(TRIU.r(0, 128), 1), [128, nh, 128]), bcast(usq(R(5), 2), [128, nh, 128]), M)
                    for q in range(nh // 4):
                        pq = PS[3]
                        g.mm(pq.r(0, 512), ones.r(0, 128), AR.r(11392 + q * 512, 512))
                        g.cp('act', bcx.r(bxo + q * 512, 512), pq.r(0, 512))
                    g.act(ebx.r(bxo, nh * 128), bcx.r(bxo, nh * 128), AF.Exp)
                    g.act(R(7), R(6), AF.Exp)
                    last = bcx.rs(bxo, nh, 128, 127, 1)
                    g.tt('dve', usq(R(8), 2), last, usq(R(6), 2), S_)
                    g.act(R(8), R(8), AF.Exp)
                    if c0 == 4:
                        g.tt('dve', R(9), R(1, 4), R(7), M)
                    else:
                        g.tt('dve', R(10), R(4), R(8), M)
                    return R

                def conv_fm(nct, xc_off, acc_off, halo, cw_off, cb_off):
                    W3 = TT + 3
                    g.cp('pool', AR.rs(xc_off, nct, W3, 0, 3), halo.r3(0, nct, 3))
                    for ct in range(nct):
                        xo = xc_off + ct * W3
                        acc = AR.r(acc_off + ct * TT, TT)
                        cw = lambda j: ppt.r(cw_off + ct * 4 + j, 1)
                        if cb_off is None:
                            g.ts('dve', acc, AR.r(xo + 3, TT), cw(3), M)
                        else:
                            g.ts('dve', acc, AR.r(xo + 3, TT), cw(3), M, ppt.r(cb_off + ct, 1), A_)
                        for j in range(3):
                            g.stt(acc, AR.r(xo + j, TT), cw(j), acc, M, A_)
                        g.act(acc, acc, AF.Silu)
                    g.cp('pool', halo.r3(0, nct, 3), AR.rs(xc_off, nct, W3, TT, 3))

                def rms_fm(ytiles, wcols, neps, zoffs, sq_off, rn_off):
                    pst = PS[3]
                    for i, yv in enumerate(ytiles):
                        sq = AR.r(sq_off, TT)
                        g.tt('pool', sq, yv, yv, M)
                        g.mm(pst.r(0, TT), ones.r(0, 128), sq, start=(i == 0), stop=(i == len(ytiles) - 1))
                    rn = AR.r(rn_off, TT)
                    g.rsqrt(rn, pst.r(0, TT), neps)
                    for yv, wc in zip(ytiles, wcols):
                        g.stt(yv, yv, wc, rn, M, M)

                if 'gdn' in PHASES:
                    XC, QKV, ZS = 0, 3200, 6272
                    W3 = TT + 3
                    for ct in range(12):
                        pv = proj_fm("gdn%d" % ct)
                        g.cp('act', AR.r(XC + ct * W3 + 3, TT), pv)
                    conv_fm(12, XC, QKV, halo_g, PP_CWG, None)
                    for ct in range(4):
                        pv = proj_fm("gdn%d" % (12 + ct))
                        g.act(AR.r(ZS + ct * TT, TT), pv, AF.Silu)
                    for ct in range(8):
                        x = AR.r(QKV + ct * TT, TT)
                        sq = AR.r(9600, TT)
                        g.tt('pool', sq, x, x, M)
                        pst = PS[3]
                        g.mm(pst.r(0, TT), ones.r(0, 128), sq)
                        rn = AR.r(9856, TT)
                        g.rsqrt(rn, pst.r(0, TT), RMS_EPS)
                        if ct < 4:
                            g.stt(x, rn, 128.0 ** -0.5, x, M, M)
                        else:
                            g.tt('dve', x, x, rn, M)
                    OSB = 10112
                    SETS = [7296, 11904]
                    assert NB == 2
                    Rb = [block_scalars(b, 4, 4, (scb, scb2)[b], b * 512) for b in range(NB)]
                    PSC = [(PS[4], PS[5], PS[6]), (PS[0], PS[1], PS[2])]

                    def nm(b):
                        base = SETS[b]
                        d = dict(zip(("TA", "Dm", "Am", "Bm", "QKT", "VB", "KBG", "KDEC", "Um", "WTm", "VNEW", "QDEC"),
                                     [base + i * 128 for i in range(12)]))
                        d["PQ"] = [base + 1536 + 128 * i for i in range(4)]
                        d["RR"] = [base + 2048, base + 2176]
                        return d

                    def chainA(b, h):
                        n_ = nm(b)
                        TA, Dm, Am, Bm, QKT, VB, KBG, KDEC, Um, WTm = [n_[k] for k in ("TA", "Dm", "Am", "Bm", "QKT", "VB", "KBG", "KDEC", "Um", "WTm")]
                        PQ, RR = n_["PQ"], n_["RR"]
                        pa, pb, pc = PSC[b]
                        R = Rb[b]
                        bxo = b * 512
                        qT = AR.r(QKV + h * TT + b * 128, 128)
                        kT = AR.r(QKV + (4 + h) * TT + b * 128, 128)
                        vT = AR.r(QKV + (8 + h) * TT + b * 128, 128)
                        gcb = bcx.r(bxo + h * 128, 128)
                        gcc = R(6, 1, h)
                        g.mm(pa.r(0, 128), kT, kT)
                        g.mm(pa.r(128, 128), kT, qT)
                        yield
                        g.stt(AR.r(TA, 128), gcb, gcc, NEGS.r(0, 128), S_, A_)
                        g.act(AR.r(Dm, 128), AR.r(TA, 128), AF.Exp, scale=-1.0)
                        g.stt(AR.r(Am, 128), AR.r(Dm, 128), R(1, 1, h), pa.r(0, 128), M, M)
                        yield
                        g.stt(AR.r(TA, 128), gcb, gcc, NEGT.r(0, 128), S_, A_)
                        g.act(AR.r(Dm, 128), AR.r(TA, 128), AF.Exp)
                        g.tt('dve', AR.r(QKT, 128), AR.r(Dm, 128), pa.r(128, 128), M)
                        yield
                        g.tr(pa.r(256, 128), AR.r(Am, 128))
                        g.cp('act', AR.r(Bm, 128), pa.r(256, 128))
                        yield
                        g.tr(pa.r(384, 128), kT)
                        g.ts('dve', AR.r(KBG, 128), pa.r(384, 128), R(9, 1, h), M)
                        g.ts('dve', AR.r(KDEC, 128), pa.r(384, 128), R(8, 1, h), M)
                        yield
                        g.tr(pb.r(0, 128), vT)
                        g.ts('dve', AR.r(VB, 128), pb.r(0, 128), R(1, 1, h), M)
                        g.tt('pool', AR.r(RR[0], 128), ident.r(0, 128), AR.r(Bm, 128), S_)
                        yield
                        Pc, Qc = Am, Bm
                        ri = 0
                        for lev in range(1, 7):
                            Pn, Qn = PQ[(lev % 2) * 1], PQ[2 + (lev % 2) * 1]
                            g.mm(pb.r(128, 128), AR.r(Qc, 128), AR.r(Pc, 128))
                            g.cp('act', AR.r(Pn, 128), pb.r(128, 128))
                            yield
                            if lev < 6:
                                g.mm(pb.r(256, 128), AR.r(Pc, 128), AR.r(Qc, 128))
                                g.cp('dve', AR.r(Qn, 128), pb.r(256, 128))
                                yield
                            g.mm(pb.r(384, 128), AR.r(Pn, 128), AR.r(RR[ri], 128))
                            g.tt('dve', AR.r(RR[1 - ri], 128), AR.r(RR[ri], 128), pb.r(384, 128), A_)
                            yield
                            ri = 1 - ri
                            Pc, Qc = Pn, Qn
                        TTm = AR.r(RR[ri], 128)
                        g.mm(pc.r(0, 128), TTm, AR.r(VB, 128))
                        g.cp('act', AR.r(Um, 128), pc.r(0, 128))
                        yield
                        g.mm(pc.r(128, 128), AR.r(KBG, 128), TTm)
                        g.cp('act', AR.r(WTm, 128), pc.r(128, 128))
                        yield

                    def partB(b, h):
                        n_ = nm(b)
                        QKT, KDEC, Um, WTm, VNEW, QDEC = [n_[k] for k in ("QKT", "KDEC", "Um", "WTm", "VNEW", "QDEC")]
                        pc = PSC[b][2]
                        bxo = b * 512
                        qT = AR.r(QKV + h * TT + b * 128, 128)
                        Sv = S_g.r(h * 128, 128)
                        g.mm(pc.r(256, 128), AR.r(WTm, 128), Sv)
                        g.tt('dve', AR.r(VNEW, 128), AR.r(Um, 128), pc.r(256, 128), S_)
                        g.tt('pool', AR.r(QDEC, 128), qT, ebx.r(bxo + h * 128, 128), M)
                        po = PS[7]
                        ov = po.r((h % 2) * 256 + b * 128, 128)
                        g.mm(ov, Sv, AR.r(QDEC, 128), start=True, stop=False)
                        g.mm(ov, AR.r(VNEW, 128), AR.r(QKT, 128), start=False, stop=True)
                        g.cp('act', AR.r(OSB + h * TT + b * 128, 128), ov)
                        g.mm(pc.r(384, 128), AR.r(KDEC, 128), AR.r(VNEW, 128))
                        elast = ebx.r(bxo + h * 128 + 127, 1)
                        g.stt(Sv, Sv, elast, pc.r(384, 128), M, A_)

                    for h in range(4):
                        alive = [chainA(0, h), chainA(1, h)]
                        while alive:
                            for gen in list(alive):
                                try:
                                    next(gen)
                                except StopIteration:
                                    alive.remove(gen)
                        partB(0, h)
                        partB(1, h)
                    for h in range(4):
                        ov = AR.r(OSB + h * TT, TT)
                        rms_fm([ov], [wsm.r(0, 1)], 128.0 * RMS_EPS, None, 9600, 9856)
                        g.tt('pool', f32r(Y[0].r(h * TT, TT)), ov, AR.r(ZS + h * TT, TT), M)

                if 'ssd' in PHASES:
                    XC, XBC, ZS = 0, 3200, 6272
                    for ct in range(8):
                        pv = proj_fm("m2_%d" % ct)
                        g.cp('act', AR.r(XC + ct * W3 + 3, TT), pv)
                    conv_fm(8, XC, XBC, halo_m, PP_CWM, PP_CBM)
                    for ct in range(4):
                        pv = proj_fm("m2_%d" % (8 + ct))
                        g.act(AR.r(ZS + ct * TT, TT), pv, AF.Silu)
                    XDTP, XDEC, BTOK, CBT, DTm, CTD, TMPS = 7296, 8320, 8832, 9088, 9344, 10368, 12416
                    g.memset('pool', AR.r(XDTP, 1024), 0.0)
                    YPS = [PS[6], PS[7]]
                    for b in range(NB):
                        R = block_scalars(b, 8, 8)
                        px = PS[4]
                        for c in range(4):
                            g.tr(px.r(c * 128, 128), AR.r(XBC + c * TT + b * 128, 128))
                        pbk = PS[5]
                        for gi in range(2):
                            g.tr(pbk.r(gi * 128, 128), AR.r(XBC + (4 + gi) * TT + b * 128, 128))
                        g.cp('act', AR.r(BTOK, 256), pbk.r(0, 256))
                        for par in range(2):
                            outv = V(AR.h[:, XDTP:XDTP + 1024].rearrange("p (c q) -> p c q", q=256)[:, :, par * 192: par * 192 + 64],
                                     AR.keys(XDTP, 1024))
                            inv = V(px.h[:, 0:512].rearrange("p (c q) -> p c q", q=128)[:, :, par * 64:(par + 1) * 64], px.keys(0, 512))
                            sp_ = V(scb.h[:, 4 * 16:4 * 16 + 8].rearrange("p (c q) -> p c q", q=2)[:, :, par:par + 1].to_broadcast([128, 4, 64]),
                                    scb.keys(64, 8))
                            g.tt('dve', outv, inv, sp_, M)
                        g.tt('dve', AR.r3(XDEC, 8, 64), px.r3(0, 8, 64), bcast(usq(R(10), 2), [128, 8, 64]), M)
                        for gi in range(2):
                            g.mm(pbk.r(256 + gi * 128, 128), AR.r(XBC + (4 + gi) * TT + b * 128, 128),
                                 AR.r(XBC + (6 + gi) * TT + b * 128, 128))
                        g.cp('act', AR.r(CBT, 256), pbk.r(256, 256))
                        g.tt('dve', AR.r3(TMPS, 8, 128), bcx.r3(0, 8, 128), bcast(usq(R(6), 2), [128, 8, 128]), S_)
                        g.tt('pool', AR.r3(TMPS, 8, 128), AR.r3(TMPS, 8, 128), bcast(usq(NEGT.r(0, 128), 1), [128, 8, 128]), A_)
                        g.act(AR.r(DTm, 1024), AR.r(TMPS, 1024), AF.Exp)
                        for gi in range(2):
                            g.tt('pool', AR.r3(DTm + gi * 512, 4, 128), AR.r3(DTm + gi * 512, 4, 128),
                                 bcast(usq(AR.r(CBT + gi * 128, 128), 1), [128, 4, 128]), M)
                            g.tt('pool', AR.r3(CTD + gi * 512, 4, 128), ebx.r3(gi * 512, 4, 128),
                                 bcast(usq(AR.r(XBC + (6 + gi) * TT + b * 128, 128), 1), [128, 4, 128]), M)
                        for c in range(4):
                            yv = YPS[c // 2].r((c % 2) * 256 + b * 128, 128)
                            for hh in range(2):
                                h = 2 * c + hh
                                g.mm(yv, AR.r(XDTP + h * 128, 128), AR.r(DTm + h * 128, 128), start=(hh == 0), stop=False)
                                g.mm(yv, S_m.r(h * 128, 128), AR.r(CTD + h * 128, 128), start=False, stop=(hh == 1))
                        pu = PS[3]
                        for gi in range(2):
                            g.mm(pu.r(gi * 256, 256), AR.r(BTOK + gi * 128, 128), AR.r(XDEC + gi * 256, 256))
                        for par in range(2):
                            sv = V(S_m.h[:, :].rearrange("p (c q) -> p c q", q=256)[:, :, par * 192: par * 192 + 64], S_m.keys(0, 1024))
                            el = V(ebx.h[:, :].rearrange("p (c q) -> p c q", q=256)[:, :, par * 128 + 127: par * 128 + 128].to_broadcast([128, 4, 64]),
                                   ebx.keys(0, 1024))
                            uv = V(pu.h[:, 0:512].rearrange("p (c q) -> p c q", q=128)[:, :, par * 64:(par + 1) * 64], pu.keys(0, 512))
                            g.tt('pool', sv, sv, el, M)
                            g.tt('dve', sv, sv, uv, A_)
                    for c in range(4):
                        yv = AR.r(c * TT, TT)
                        g.stt(yv, AR.r(XBC + c * TT, TT), ppt.r(PP_MD + c, 1), YPS[c // 2].r((c % 2) * 256, TT), M, A_)
                        g.tt('pool', yv, yv, AR.r(ZS + c * TT, TT), M)
                    for gi in range(2):
                        tiles = [AR.r((2 * gi + i) * TT, TT) for i in range(2)]
                        rms_fm(tiles, [wsm.r(2 + 2 * gi + i, 1) for i in range(2)], 256.0 * RMS_EPS, None, 9600, 9856)
                        for i in range(2):
                            g.cp('pool', f32r(Y[1].r((2 * gi + i) * TT, TT)), tiles[i])

                if 'hg' in PHASES:
                    QH, Fo, LOGF, GC, EG, QD, KD, KK, KDE, ZS, ITOK, ATM, KDT = \
                        0, 1024, 2048, 3072, 4096, 5120, 6144, 7168, 8192, 9216, 10240, 12288, 12544
                    for ct in range(4):
                        pv = proj_fm("hgqf%d" % ct)
                        g.act(AR.r(QH + ct * TT, TT), pv, AF.Silu)
                    for ct in range(4):
                        pv = proj_fm("hgqf%d" % (4 + ct))
                        f = AR.r(Fo + ct * TT, TT)
                        g.act(f, pv, AF.Sigmoid)
                        g.ts('dve', f, f, omlb.r(l * 4 + ct, 1), M, lbt.r(l * 4 + ct, 1), A_)
                        g.act(AR.r(LOGF + ct * TT, TT), f, AF.Ln)
                        g.ts('pool', AR.r(KK + ct * TT, TT), f, -1.0, M, 1.0, A_)
                        g.scan(AR.r(GC + ct * TT, TT), SEG.r(0, TT), AR.r(LOGF + ct * TT, TT), 0.0)
                    g.act(AR.r(EG, 1024), AR.r(GC, 1024), AF.Exp)
                    g.tt('pool', AR.r(QD, 1024), AR.r(QH, 1024), AR.r(EG, 1024), M)
                    g.ts('dve', AR.r(KD, 1024), AR.r(GC, 1024), -80.0, ALU.max)
                    g.act(AR.r(KD, 1024), AR.r(KD, 1024), AF.Exp, scale=-1.0)
                    g.tt('pool', AR.r(KD, 1024), AR.r(KD, 1024), AR.r(KK, 1024), M)
                    for ct in range(4):
                        for c in range(NCH):
                            o = ct * TT + c * 64
                            g.act(AR.r(KDE + o, 64), AR.r(GC + o, 64), AF.Exp, bias=AR.r(GC + o + 63, 1), scale=-1.0)
                    g.tt('pool', AR.r(KDE, 1024), AR.r(KDE, 1024), AR.r(KK, 1024), M)
                    for ct in range(4):
                        wo = w_get("hgi%d" % ct)
                        for c in range(NCH):
                            for k in range(KT):
                                g.mm(PS[4 + c].r(ct * 128, 128, 0, 64), hT.r(k * TT + c * 64, 64), wring.r(wo + k * 128, 128),
                                     start=(k == 0), stop=(k == KT - 1))
                    for c in range(NCH):
                        g.cp('act' if c % 2 == 0 else 'dve', AR.r(ITOK + c * 512, 512, 0, 64), PS[4 + c].r(0, 512, 0, 64))
                    for ct in range(4):
                        pv = proj_fm("hgz%d" % ct)
                        g.act(AR.r(ZS + ct * TT, TT), pv, AF.Silu)
                    OPS = [PS[6], PS[7]]
                    for c in range(NCH):
                        pa = PS[2]
                        for h in range(4):
                            o = h * TT + c * 64
                            g.mm(pa.r(h * 64, 64, 0, 64), AR.r(KD + o, 64), AR.r(QD + o, 64))
                        g.tt('dve', AR.r3(ATM, 4, 64, 0, 64), pa.r3(0, 4, 64, 0, 64), bcast(usq(M01.r(0, 64, 0, 64), 1), [64, 4, 64]), M)
                        pk = PS[3]
                        for h in range(4):
                            o = h * TT + c * 64
                            g.tr(pk.r(h * 128, 128, 0, 64), AR.r(KDE + o, 64))
                        g.cp('act', AR.r(KDT, 512, 0, 64), pk.r(0, 512, 0, 64))
                        for h in range(4):
                            o = h * TT + c * 64
                            iv = AR.r(ITOK + c * 512 + h * 128, 128, 0, 64)
                            Sv = S_h.r(h * 128, 128)
                            ov = OPS[h // 2].r((h % 2) * 256 + c * 64, 64)
                            g.mm(ov, iv, AR.r(ATM + h * 64, 64, 0, 64), start=True, stop=False)
                            g.mm(ov, Sv, AR.r(QD + o, 64), start=False, stop=True)
                            pu = PS[4 + h % 2]
                            g.mm(pu.r(256, 128), AR.r(KDT + h * 128, 128, 0, 64), iv)
                            g.stt(Sv, Sv, AR.r(EG + o + 63, 1), pu.r(256, 128), M, A_)
                    for h in range(4):
                        ov = AR.r(Fo + h * TT, TT)
                        g.cp('act', ov, OPS[h // 2].r((h % 2) * 256, TT))
                        rms_fm([ov], [wsm.r(1, 1)], 128.0 * RMS_EPS, None, LOGF, LOGF + TT)
                        g.tt('pool', f32r(Y[2].r(h * TT, TT)), ov, AR.r(ZS + h * TT, TT), M)

                if 's5' in PHASES:
                    Uo, ZS, T1o, WRE, WIM, ZR, ZI, XRE, XIM, YTOK, YS, GEL = \
                        0, 1024, 2048, 4096, 4608, 5120, 5632, 6144, 6656, 7168, 7680, 8704
                    for ct in range(4):
                        pv = proj_fm("s5_%d" % ct)
                        g.cp('act', AR.r(Uo + ct * TT, TT), pv)
                    for ct in range(4):
                        pv = proj_fm("s5_%d" % (4 + ct))
                        g.act(AR.r(ZS + ct * TT, TT), pv, AF.Silu)
                    for b in range(NB):
                        py = PS[7]
                        for ft in range(4):
                            pre, pim = PS[4], PS[5]
                            uv = AR.r(Uo + ft * TT + b * 128, 128)
                            for q in range(4):
                                pr_ = ft * 4 + q
                                g.mm(pre.r(q * 128, 128), BbT.r(pr_ * 128, 128), uv)
                                g.mm(pim.r(q * 128, 128), BbT.r(2048 + pr_ * 128, 128), uv)
                            ct_ = costab.r(ft * 512, 512)
                            st_ = sintab.r(ft * 512, 512)
                            t = [AR.r(T1o + i * 512, 512) for i in range(4)]
                            g.tt('dve', t[0], pre.r(0, 512), ct_, M)
                            g.tt('dve', t[1], pim.r(0, 512), st_, M)
                            g.tt('dve', t[2], pim.r(0, 512), ct_, M)
                            g.tt('dve', t[3], pre.r(0, 512), st_, M)
                            g.tt('pool', AR.r(WRE, 512), t[0], t[1], A_)
                            g.tt('pool', AR.r(WIM, 512), t[2], t[3], S_)
                            for q in range(4):
                                pr_ = ft * 4 + q
                                rb = bcast(rho.r(pr_, 1), [128, 128])
                                g.scan(AR.r(ZR + q * 128, 128), rb, AR.r(WRE + q * 128, 128), xs5.r(pr_, 1))
                                g.scan(AR.r(ZI + q * 128, 128), rb, AR.r(WIM + q * 128, 128), xs5.r(16 + pr_, 1))
                            g.tt('pool', t[0], AR.r(ZR, 512), ct_, M)
                            g.tt('pool', t[1], AR.r(ZI, 512), st_, M)
                            g.tt('pool', t[2], AR.r(ZI, 512), ct_, M)
                            g.tt('pool', t[3], AR.r(ZR, 512), st_, M)
                            g.tt('pool', AR.r(XRE, 512), t[0], t[1], S_)
                            g.tt('pool', AR.r(XIM, 512), t[2], t[3], A_)
                            g.cp('pool', xs5.r3(ft * 4, 4, 1), AR.rs(XRE, 4, 128, 127, 1))
                            g.cp('pool', xs5.r3(16 + ft * 4, 4, 1), AR.rs(XIM, 4, 128, 127, 1))
                            for q in range(4):
                                pr_ = ft * 4 + q
                                yv = py.r(ft * 128 + q * 32, 32)
                                g.mm(yv, AR.r(XRE + q * 128, 128), Cpd.r(pr_ * 32, 32), start=True, stop=False)
                                g.mm(yv, AR.r(XIM + q * 128, 128), Cpd.r(512 + pr_ * 32, 32), start=False, stop=True)
                        g.cp('act', AR.r(YTOK, 512), py.r(0, 512))
                        pyt = PS[6]
                        for ft in range(4):
                            g.tr(pyt.r(ft * 128, 128), AR.r(YTOK + ft * 128, 128))
                        for ft in range(4):
                            g.stt(AR.r(YS + ft * TT + b * 128, 128), AR.r(Uo + ft * TT + b * 128, 128), ppt.r(PP_SD + ft, 1),
                                  pyt.r(ft * 128, 128), M, A_)
                    x = AR.r(YS, 1024)
                    g1 = AR.r(GEL, 1024)
                    g2 = AR.r(GEL + 1024, 1024)
                    g.tt('pool', g1, x, x, M)
                    g.ts('dve', g1, g1, 0.044715, M, 1.0, A_)
                    g.tt('pool', g1, g1, x, M)
                    g.act(g2, g1, AF.Sigmoid, scale=2.0 * math.sqrt(2.0 / math.pi))
                    g.tt('pool', f32r(YSB.r(0, 4 * TT)), x, g2, M)
                    for ct in range(4):
                        w1 = w_get("glu0_%d" % ct)
                        w2 = w_get("glu1_%d" % ct)
                        p1, p2 = PS[4], PS[5]
                        for k in range(4):
                            g.mm(p1.r(0, TT), wring.r(w1 + k * 128, 128), YSB.r(k * TT, TT), start=(k == 0), stop=(k == 3), r=True)
                        for k in range(4):
                            g.mm(p2.r(0, TT), wring.r(w2 + k * 128, 128), YSB.r(k * TT, TT), start=(k == 0), stop=(k == 3), r=True)
                        sg = AR.r(GEL + ct * TT, TT)
                        g.act(sg, p2.r(0, TT), AF.Sigmoid)
                        g.tt('dve', sg, sg, p1.r(0, TT), M)
                        g.tt('pool', f32r(Y[3].r(ct * TT, TT)), sg, AR.r(ZS + ct * TT, TT), M)

                MIX, SGT, TMPM = 0, 2048, 2304
                for dt in range(8):
                    for bq in range(4):
                        pv = proj_fm("gate%d_%d" % (bq, dt))
                        sg = AR.r(SGT, TT)
                        g.act(sg, pv, AF.Sigmoid)
                        wo = w_get("wbr%d_%d" % (bq, dt))
                        pb_ = PS[2 + bq % 2]
                        for k in range(4):
                            g.mm(pb_.r(0, TT), wring.r(wo + k * 128, 128), Y[bq].r(k * TT, TT), start=(k == 0), stop=(k == 3), r=True)
                        if bq == 0:
                            g.tt('dve', f32r(MIXB.r(dt * TT, TT)), sg, pb_.r(0, TT), M)
                        else:
                            g.tt('dve', AR.r(TMPM, TT), sg, pb_.r(0, TT), M)
                            g.tt('pool', f32r(MIXB.r(dt * TT, TT)), MIXB.r(dt * TT, TT), AR.r(TMPM, TT), A_)
                for k in range(KT):
                    wo = w_get("wout%d" % k)
                    for b in range(NB):
                        for hf in range(2):
                            g.mm(PS[4 + b * 2 + hf].r(0, 512), MIXB.r(k * TT + b * 128, 128), wring.r(wo + hf * 512, 512),
                                 start=(k == 0), stop=(k == KT - 1), r=True)
                for b in range(NB):
                    r0 = tok0 + b * 128
                    for hf in range(2):
                        g.stt(rbuf.r(hf * 512, 512), htok.r(b * D + hf * 512, 512), ALPHA, PS[4 + b * 2 + hf].r(0, 512), M, A_)
                    ln_rows(rbuf.r(0, D), rbuf.r(0, D), prt.r(PR_LNG, D), prt.r(PR_LNB, D))
                    if dyn:
                        if not last:
                            g.dma('pool', DV(rows(dst, r0, 128), dstn), rbuf.r(0, D))
                        else:
                            g.dma('pool', DV(rows(yout, r0 - out_row0, 128), "y"), rbuf.r(0, D))
                        continue
                    lo = max(0, pad - r0)
                    if lo >= 128:
                        continue
                    if not last:
                        g.dma('pool', DV(dst[r0 + lo:r0 + 128, :], dstn), rbuf.r(0, D, lo, 128))
                    else:
                        lo2 = max(0, out_row0 - r0)
                        if lo2 < 128:
                            g.dma('pool', DV(yout[r0 + lo2 - out_row0:r0 + 128 - out_row0, :], "y"), rbuf.r(0, D, lo2, 128))
                assert wst['used'] == len(stream), (wst, len(stream))

            tile_body(0, False)
            kb.barrier()
            if ntiles == 2:
                tile_body(1, False)
                kb.barrier()
            elif ntiles > 2:
                assert out_row0 <= TT
                kb.dry = True
                tile_body(1, True)
                n_it = kb.barrier()
                kb.dry = False
                with nc.Fori(1, ntiles) as ti:
                    kb.enter_loop(ti, n_it, 1)
                    tile_body(ti, True)
                    kb.barrier()
                kb.exit_loop(ntiles - 1)
        kb.barrier()
        print("[build] ninst=%d nwait=%d per_eng=%s" % (kb.ninst, kb.nwait, kb.per_eng), flush=True)
    return nc


def _fm(w, c0, ncol):
    nt = ncol // 128
    x = w[:, c0:c0 + ncol].reshape(KT, 128, nt, 128)
    return np.ascontiguousarray(x.transpose(2, 1, 0, 3)).reshape(nt, 128, KT * 128)


def prep_weights(inp, depth):
    f = lambda a: np.asarray(a, dtype=np.float32)
    L = depth
    w_in = f(inp['w_in'])
    wfm = np.empty((L, 80, 128, KT * 128), np.float32)
    wsc = np.empty((L, 128, KT * 16), np.float32)
    witm = np.empty((L, 4, 128, KT * 128), np.float32)
    for l in range(L):
        w = w_in[l]
        parts = [_fm(w, O_GQKV, 1536), _fm(w, O_GZ, 512), _fm(w, O_MX, 1024), _fm(w, O_MZ, 512),
                 _fm(w, O_HQ, 512), _fm(w, O_HF, 512), _fm(w, O_HZ, 512), _fm(w, O_SU, 512), _fm(w, O_SZ, 512),
                 _fm(w, O_GATE, 4096)]
        wfm[l] = np.concatenate(parts, axis=0)
        sc = np.concatenate([w[:, O_GB:O_GB + 4], w[:, O_GA:O_GA + 4], w[:, O_MDT:O_MDT + 8]], axis=1)
        wsc[l] = sc.reshape(KT, 128, 16).transpose(1, 0, 2).reshape(128, KT * 16)
        witm[l] = _fm(w, O_HI, 512)
    glu = np.stack([f(inp['s5_glu_w1']), f(inp['s5_glu_w2'])], axis=1)
    wglu = np.ascontiguousarray(glu.reshape(L, 2, 4, 128, 4, 128).transpose(0, 1, 4, 3, 2, 5)).reshape(L, 2, 4, 128, 512)
    wb = f(inp['w_branch'])
    wbr = np.ascontiguousarray(wb.reshape(L, 4, 4, 128, 8, 128).transpose(0, 4, 1, 3, 2, 5)).reshape(L, 8, 4, 128, 512)
    wout = np.ascontiguousarray(f(inp['w_out']).reshape(L, KT, 128, D))
    pp = np.zeros((L, 128, NPP), np.float32)
    pr = np.zeros((L, NPR), np.float32)
    for l in range(L):
        pp[l, :, PP_CWG:PP_CWG + 48] = f(inp['gdn_conv_w'])[l].reshape(4, 12, 128).transpose(2, 1, 0).reshape(128, 48)
        pp[l, :, PP_CWM:PP_CWM + 32] = f(inp['m2_conv_w'])[l].reshape(4, 8, 128).transpose(2, 1, 0).reshape(128, 32)
        pp[l, :, PP_CBM:PP_CBM + 8] = f(inp['m2_conv_b'])[l].reshape(8, 128).T
        pp[l, :, PP_GNW] = f(inp['gdn_norm_w'])[l]
        pp[l, :, PP_MD:PP_MD + 4] = np.repeat(f(inp['m2_D'])[l], 64).reshape(4, 128).T
        pp[l, :, PP_MNW:PP_MNW + 4] = f(inp['m2_norm_w'])[l].reshape(4, 128).T
        pp[l, :, PP_HNW] = f(inp['hg_norm_w'])[l]
        pp[l, :, PP_SD:PP_SD + 4] = f(inp['s5_D'])[l].reshape(4, 128).T
        pp[l, :, PP_SAR:PP_SAR + 16] = f(inp['s5_A_re'])[l].reshape(16, 128).T
        pp[l, :, PP_SAI:PP_SAI + 16] = f(inp['s5_A_im'])[l].reshape(16, 128).T
        pp[l, :, PP_SLDT:PP_SLDT + 16] = np.repeat(f(inp['s5_log_dt'])[l], 64).reshape(16, 128).T
        pr[l, PR_A12:PR_A12 + 4] = f(inp['gdn_A_log'])[l]
        pr[l, PR_A12 + 4:PR_A12 + 12] = f(inp['m2_A_log'])[l]
        pr[l, PR_B12:PR_B12 + 4] = f(inp['gdn_dt_bias'])[l]
        pr[l, PR_B12 + 4:PR_B12 + 12] = f(inp['m2_dt_bias'])[l]
        pr[l, PR_LNG:PR_LNG + D] = f(inp['ln_g'])[l]
        pr[l, PR_LNB:PR_LNB + D] = f(inp['ln_b'])[l]
    lbl = np.ascontiguousarray(f(inp['hg_lb_logits']).reshape(L, 4, 128).transpose(2, 1, 0)).reshape(128, 4 * L)
    s5b = np.zeros((L, 2, 128, 16, 128), np.float32)
    s5c = np.zeros((L, 2, 128, 16, 32), np.float32)
    for j, (bn, cn) in enumerate((('s5_B_re', 's5_C_re'), ('s5_B_im', 's5_C_im'))):
        Bm = f(inp[bn])
        Cm = f(inp[cn])
        for pr_ in range(16):
            for g2 in range(2):
                gi = 2 * pr_ + g2
                col = (pr_ % 4) * 32 + g2 * 16
                s5b[:, j, g2 * 64:(g2 + 1) * 64, pr_, col:col + 16] = Bm[:, gi]
                s5c[:, j, g2 * 64:(g2 + 1) * 64, pr_, g2 * 16:(g2 + 1) * 16] = Cm[:, gi].transpose(0, 2, 1)
    lnin = np.stack([f(inp['ln_in_g']), f(inp['ln_in_b'])], axis=0)
    return dict(lnin=lnin, wfm=wfm, wsc=wsc, witm=witm, wglu=wglu, wbr=wbr, wout=wout, pp=pp, pr=pr, lbl=lbl,
                s5b=s5b.reshape(L, 2, 128, 2048), s5c=s5c.reshape(L, 2, 128, 512))


def make_xin(x_b, meta, Tp):
    T = x_b.shape[0] + meta.shape[0]
    xin = np.zeros((Tp, D), np.float32)
    xin[Tp - T:Tp - T + meta.shape[0]] = meta
    xin[Tp - x_b.shape[0]:] = x_b
    return xin


_NC_CACHE = {}


def kernel(**inputs):
    x = np.asarray(inputs['x'], dtype=np.float32)
    Bsz, SEQ, _ = x.shape
    depth = int(np.asarray(inputs['w_in']).shape[0])
    T = SEQ + NMETA
    ntiles = (T + TT - 1) // TT
    Tp = ntiles * TT
    ws = prep_weights(inputs, depth)
    meta = np.asarray(inputs['meta_tokens'], dtype=np.float32)
    key = (T, depth, SEQ)
    if key not in _NC_CACHE:
        _NC_CACHE[key] = build(T, depth, SEQ)
    nc = _NC_CACHE[key]
    in_maps = []
    for b in range(Bsz):
        m = dict(ws)
        m['xin'] = make_xin(x[b], meta, Tp)
        in_maps.append(m)
    res = run_bass_kernel_spmd(nc, in_maps, core_ids=list(range(Bsz)))
    return np.stack([np.asarray(res.results[b]['y'], dtype=np.float32) for b in range(Bsz)], axis=0)
```

```python
import math
import os
import numpy as np
from contextlib import ExitStack
import concourse.bass as bass
import concourse.mybir as mybir
from concourse.bass_utils import run_bass_kernel_spmd

F32 = mybir.dt.float32
F32R = mybir.dt.float32r
FAST_MM = True
POOL_TO = os.environ.get('KPOOLTO', 'dve')
AF = mybir.ActivationFunctionType
ALU = mybir.AluOpType
AX = mybir.AxisListType

D = 1024
TT = 256
NB = TT // 128
NCH = TT // 64
KT = 8
NMETA = 16
BIG = 30000.0
ALPHA = 8.0 ** 0.25
LN_EPS = 1e-5
RMS_EPS = 1e-6
N_IN = 10768
S5S = 128

O_GQKV, O_GZ, O_GB, O_GA, O_MX, O_MZ, O_MDT = 0, 1536, 2048, 2052, 2056, 3080, 3592
O_HQ, O_HF, O_HI, O_HZ, O_SU, O_SZ, O_GATE = 3600, 4112, 4624, 5136, 5648, 6160, 6672

PP_CWG, PP_CWM, PP_CBM, PP_GNW, PP_MD, PP_MNW, PP_HNW, PP_SD, PP_SAR, PP_SAI, PP_SLDT = \
    0, 48, 80, 88, 89, 93, 97, 98, 102, 118, 134
NPP = 150
PR_A12, PR_B12, PR_LNG, PR_LNB = 0, 12, 24, 24 + 1024
NPR = 24 + 2048
CH = 128
PHASES = os.environ.get('KPHASES', 'gdn,ssd,hg,s5').split(',')


class V:
    __slots__ = ('ap', 'keys')

    def __init__(self, ap, keys):
        self.ap = ap
        self.keys = keys


class Buf:
    def __init__(self, h, name, ch=CH):
        self.h = h
        self.name = name
        self.ch = ch

    def keys(self, off, n):
        return [(self.name, c) for c in range(off // self.ch, (off + n - 1) // self.ch + 1)]

    def r(self, off, n, p0=0, p1=128):
        return V(self.h[p0:p1, off:off + n], self.keys(off, n))

    def r3(self, off, a, s, p0=0, p1=128):
        return V(self.h[p0:p1, off:off + a * s].rearrange("p (a s) -> p a s", s=s), self.keys(off, a * s))

    def rs(self, off, a, stride, lo, n, p0=0, p1=128):
        return V(self.h[p0:p1, off:off + a * stride].rearrange("p (a s) -> p a s", s=stride)[:, :, lo:lo + n],
                 self.keys(off, a * stride))


def f32r(v):
    return V(v.ap.bitcast(F32R), v.keys) if FAST_MM else v


def bcast(v, shape):
    return V(v.ap.to_broadcast(shape), v.keys)


def usq(v, axis):
    return V(v.ap.unsqueeze(axis), v.keys)


class KB:
    ND = 8

    def __init__(self, nc, es):
        self.nc = nc
        self.E = {'pe': nc.tensor, 'dve': nc.vector, 'act': nc.scalar, 'pool': nc.gpsimd, 'sp': nc.sync}
        self.sem = {k: es.enter_context(nc.semaphore("s_" + k)) for k in self.E}
        self.cnt = {k: 0 for k in self.E}
        self.seen = {k: {} for k in self.E}
        self.dsem = [es.enter_context(nc.semaphore("d%d" % i)) for i in range(2 * self.ND)]
        self.dcnt = [0] * (2 * self.ND)
        self.dnext = {'sp': 0, 'pool': 0, 'act': 0}
        self.last_w = {}
        self.readers = {}
        self.nwait = 0
        self.ninst = 0
        self.per_eng = {k: 0 for k in self.E}
        self.base = {}
        self.loop = None
        self.dry = False
        self.tmpreg = {k: self.E[k].alloc_register("kbtmp_" + k) for k in self.E}

    def _wait(self, eng, tok):
        kind, a, v = tok
        if kind == 'c':
            if a == eng and eng == 'pe':
                return
            key = a
            sem = self.sem[a]
        else:
            key = ('d', a)
            sem = self.dsem[a]
        if self.seen[eng].get(key, 0) >= v:
            return
        if not self.dry:
            b = self.base.get(key, 0)
            nk = 0 if self.loop is None else self.loop[1].get(key, 0)
            if nk == 0:
                self.E[eng].wait_ge(sem, b + v)
            else:
                ti, n, start = self.loop
                r = self.tmpreg[eng]
                self.E[eng].reg_mul(r, ti, nk)
                self.E[eng].reg_add(r, r, b - start * nk + v)
                self.E[eng].wait_ge(sem, r)
            self.nwait += 1
        self.seen[eng][key] = v

    def _absval(self, key, v):
        b = self.base.get(key, 0)
        if self.loop is None:
            return b + v
        ti, n, start = self.loop
        nk = n.get(key, 0)
        if nk == 0:
            return b + v
        return ti * nk + (b - start * nk + v)

    def _deps(self, eng, reads, writes):
        for r in reads:
            t = self.last_w.get(r)
            if t is not None:
                self._wait(eng, t)
        for w in writes:
            t = self.last_w.get(w)
            if t is not None:
                self._wait(eng, t)
            for t in self.readers.get(w, ()):
                self._wait(eng, t)

    def _commit(self, tok, reads, writes):
        ws = set(writes)
        for w in writes:
            self.last_w[w] = tok
            self.readers[w] = []
        for r in reads:
            if r in ws:
                continue
            lst = self.readers.setdefault(r, [])
            lst.append(tok)
            if len(lst) > 8:
                d = {}
                for t in lst:
                    d[(t[0], t[1])] = t
                self.readers[r] = list(d.values())

    def op(self, eng, fn, reads=(), writes=()):
        self._deps(eng, reads, writes)
        self.cnt[eng] += 1
        if not self.dry:
            ins = fn(self.E[eng])
            ins.then_inc(self.sem[eng], 1)
            self.ninst += 1
            self.per_eng[eng] += 1
        self._commit(('c', eng, self.cnt[eng]), reads, writes)

    def dma(self, q, out, in_, reads=(), writes=()):
        self._deps(q, reads, writes)
        i = self.dnext[q] % self.ND + (self.ND if q == 'pool' else 0)
        self.dnext[q] += 1
        if self.dcnt[i] > 0:
            self._wait(q, ('d', i, self.dcnt[i]))
        if not self.dry:
            self.E[q].dma_start(out=out, in_=in_).then_inc(self.dsem[i], 16)
            self.ninst += 1
        self.dcnt[i] += 16
        self._commit(('d', i, self.dcnt[i]), reads, writes)

    def reset(self):
        for k in self.E:
            self.cnt[k] = 0
            self.seen[k] = {}
        for i in range(len(self.dcnt)):
            self.dcnt[i] = 0
        for q in self.dnext:
            self.dnext[q] = 0
        self.last_w = {}
        self.readers = {}

    def counts(self):
        n = {k: self.cnt[k] for k in self.E}
        for i in range(len(self.dcnt)):
            n[('d', i)] = self.dcnt[i]
        return n

    def barrier(self):
        self.finish('sp')
        self.cnt['sp'] += 1
        if not self.dry:
            self.E['sp'].sem_inc(self.sem['sp'], 1)
            self.ninst += 1
        for e in self.E:
            if e != 'sp':
                self._wait(e, ('c', 'sp', self.cnt['sp']))
        n = self.counts()
        if self.loop is None and not self.dry:
            for k, v in n.items():
                self.base[k] = self.base.get(k, 0) + v
        self.reset()
        return n

    def enter_loop(self, ti, n, start):
        self.loop = (ti, n, start)

    def exit_loop(self, niter):
        ti, n, start = self.loop
        self.loop = None
        for k, v in n.items():
            self.base[k] = self.base.get(k, 0) + niter * v

    def finish(self, q='sp'):
        for i in range(2 * self.ND):
            if self.dcnt[i] > 0:
                self._wait(q, ('d', i, self.dcnt[i]))
        for e in self.E:
            if e != q and self.cnt[e] > 0:
                self._wait(q, ('c', e, self.cnt[e]))


class G:
    def __init__(self, nc, es):
        self.nc = nc
        self.es = es
        self.kb = KB(nc, es)
        self.ident = None

    def sb(self, name, ncols, dt=F32):
        return Buf(self.es.enter_context(self.nc.sbuf_tensor(name, [128, ncols], dt)), name)

    def ps(self, name, ncols, dt=F32):
        return Buf(self.es.enter_context(self.nc.psum_tensor(name, [128, ncols], dt)), name, ch=512)

    def mm(self, out, lhsT, rhs, start=True, stop=True, r=False):
        la, ra = lhsT.ap, rhs.ap
        if r and FAST_MM:
            la = la.bitcast(F32R)
            ra = ra.bitcast(F32R)
        self.kb.op('pe', lambda e: e.matmul(out.ap, lhsT=la, rhs=ra, start=start, stop=stop),
                   reads=lhsT.keys + rhs.keys, writes=out.keys)

    def tr(self, out, in_, n=128):
        idn = self.ident.r(0, n, 0, n)
        self.kb.op('pe', lambda e: e.transpose(out.ap, in_.ap, idn.ap),
                   reads=in_.keys + idn.keys, writes=out.keys)

    def act(self, out, in_, func, bias=0.0, scale=1.0):
        rd = list(in_.keys)
        b = bias
        if isinstance(bias, V):
            rd += bias.keys
            b = bias.ap
        self.kb.op('act', lambda e: e.activation(out=out.ap, in_=in_.ap, func=func, bias=b, scale=scale),
                   reads=rd, writes=out.keys)

    def tt(self, eng, out, a, b, op):
        if eng == 'pool':
            eng = POOL_TO
        self.kb.op(eng, lambda e: e.tensor_tensor(out=out.ap, in0=a.ap, in1=b.ap, op=op),
                   reads=a.keys + b.keys, writes=out.keys)

    def ts(self, eng, out, a, s1, op0, s2=None, op1=None):
        if eng == 'pool':
            eng = POOL_TO
        rd = list(a.keys)
        x1, x2 = s1, s2
        if isinstance(s1, V):
            rd += s1.keys
            x1 = s1.ap
        if isinstance(s2, V):
            rd += s2.keys
            x2 = s2.ap
        if op1 is None:
            self.kb.op(eng, lambda e: e.tensor_scalar(out=out.ap, in0=a.ap, scalar1=x1, scalar2=None, op0=op0),
                       reads=rd, writes=out.keys)
        else:
            self.kb.op(eng, lambda e: e.tensor_scalar(out=out.ap, in0=a.ap, scalar1=x1, scalar2=x2, op0=op0, op1=op1),
                       reads=rd, writes=out.keys)

    def stt(self, out, in0, scalar, in1, op0, op1):
        rd = in0.keys + in1.keys
        x = scalar
        if isinstance(scalar, V):
            rd = rd + scalar.keys
            x = scalar.ap
        self.kb.op('dve', lambda e: e.scalar_tensor_tensor(out=out.ap, in0=in0.ap, scalar=x, in1=in1.ap, op0=op0, op1=op1),
                   reads=rd, writes=out.keys)

    def cp(self, eng, out, in_):
        if eng == 'pool':
            eng = POOL_TO
        if eng == 'act':
            self.kb.op(eng, lambda e: e.copy(out=out.ap, in_=in_.ap), reads=in_.keys, writes=out.keys)
        else:
            self.kb.op(eng, lambda e: e.tensor_copy(out=out.ap, in_=in_.ap), reads=in_.keys, writes=out.keys)

    def scan(self, out, d0, d1, init):
        rd = d0.keys + d1.keys
        x = init
        if isinstance(init, V):
            rd = rd + init.keys
            x = init.ap
        self.kb.op('dve', lambda e: e.tensor_tensor_scan(out=out.ap, data0=d0.ap, data1=d1.ap, initial=x,
                                                          op0=ALU.mult, op1=ALU.add), reads=rd, writes=out.keys)

    def memset(self, eng, out, val):
        self.kb.op(eng, lambda e: e.memset(out.ap, val), writes=out.keys)

    def asel(self, out, in_, pattern, cmp, fill, base, cm):
        self.kb.op('pool', lambda e: e.affine_select(out=out.ap, in_=in_.ap, pattern=pattern, compare_op=cmp,
                                                     fill=fill, base=base, channel_multiplier=cm),
                   reads=in_.keys, writes=out.keys)

    def rsqrt(self, out, in_, addc):
        self.kb.op('act', lambda e: e.activation(out=out.ap, in_=in_.ap, func=AF.Sqrt, bias=addc, scale=1.0),
                   reads=in_.keys, writes=out.keys)
        self.recip(out, out)

    def recip(self, out, in_):
        self.kb.op('dve', lambda e: e.reciprocal(out=out.ap, in_=in_.ap), reads=in_.keys, writes=out.keys)

    def dma(self, q, out, in_):
        self.kb.dma(q, out.ap, in_.ap, reads=in_.keys, writes=out.keys)


def build(T, depth, seq_out):
    ntiles = (T + TT - 1) // TT
    Tp = ntiles * TT
    pad = Tp - T
    out_row0 = Tp - seq_out
    nc = bass.Bass("TRN2", target_bir_lowering=False)
    L = depth

    def dram(name, shape, kind="ExternalInput"):
        return nc.dram_tensor(name, shape, F32, kind=kind).ap()

    xin = dram("xin", [Tp, D])
    lnin = dram("lnin", [2, D])
    wfm = dram("wfm", [L, 80, 128, KT * 128])
    wsc = dram("wsc", [L, 128, KT * 16])
    witm = dram("witm", [L, 4, 128, KT * 128])
    wglu = dram("wglu", [L, 2, 4, 128, 4 * 128])
    wbr = dram("wbr", [L, 8, 4, 128, 4 * 128])
    wout = dram("wout", [L, KT, 128, D])
    pp_d = dram("pp", [L, 128, NPP])
    pr_d = dram("pr", [L, NPR])
    lbl_d = dram("lbl", [128, 4 * depth])
    s5b_d = dram("s5b", [L, 2, 128, 16 * 128])
    s5c_d = dram("s5c", [L, 2, 128, 16 * 32])
    yout = dram("y", [seq_out, D], kind="ExternalOutput")
    hb = [dram("hbuf%d" % i, [Tp, D], kind="Internal") for i in range(2)]

    def DV(ap, name, blk=0):
        return V(ap, [("dram_" + name, blk)])

    with ExitStack() as es:
        g = G(nc, es)
        kb = g.kb
        sb, ps = g.sb, g.ps
        M, A_, S_ = ALU.mult, ALU.add, ALU.subtract
        ident = sb("ident", 128)
        g.ident = ident
        ones = sb("ones", 128)
        zeros = sb("zeros", 128)
        NEGS = sb("NEGS", 128)
        NEGT = sb("NEGT", 128)
        TRIU = sb("TRIU", 128)
        M01 = sb("M01", 64)
        SEG = sb("SEG", TT)
        g.memset('pool', zeros.r(0, 128), 0.0)
        g.memset('pool', ones.r(0, 128), 1.0)
        g.asel(ident.r(0, 128), zeros.r(0, 128), [[-1, 128]], ALU.not_equal, 1.0, 0, 1)
        g.asel(NEGS.r(0, 128), zeros.r(0, 128), [[-1, 128]], ALU.is_gt, BIG, 0, 1)
        g.asel(NEGT.r(0, 128), zeros.r(0, 128), [[1, 128]], ALU.is_ge, -BIG, 0, -1)
        g.asel(TRIU.r(0, 128), ones.r(0, 128), [[1, 128]], ALU.is_ge, 0.0, 0, -1)
        g.asel(M01.r(0, 64, 0, 64), ones.r(0, 64, 0, 64), [[1, 64]], ALU.is_ge, 0.0, 0, -1)
        g.memset('pool', SEG.r(0, TT), 1.0)
        for c in range(NCH):
            g.memset('pool', SEG.r(c * 64, 1), 0.0)

        NW = 8
        wring = sb("wring", NW * 1024)
        htok = sb("htok", NB * D)
        hT = sb("hT", KT * TT)
        ppt = sb("ppt", NPP)
        prt = sb("prt", NPR)
        lbl = sb("lbl_s", 4 * depth)
        lbt = sb("lbt", depth * 4)
        omlb = sb("omlb", depth * 4)
        negA12 = sb("negA12", 12)
        wsm = sb("wsm", 8)
        halo_g = sb("halo_g", 12 * 3)
        halo_m = sb("halo_m", 8 * 3)
        S_g = sb("S_g", 4 * 128)
        S_m = sb("S_m", 8 * 128)
        S_h = sb("S_h", 4 * 128)
        xs5 = sb("xs5", 32)
        BbT = sb("BbT", 2 * 16 * 128)
        Cpd = sb("Cpd", 2 * 16 * 32)
        costab = sb("costab", 16 * S5S)
        sintab = sb("sintab", 16 * S5S)
        rho = sb("rho", 16)
        Y = [sb("y%d" % i, 4 * TT) for i in range(4)]
        padm = sb("padm", NB)
        scb = sb("scb", 16 * 16)
        scb2 = sb("scb2", 16 * 16)
        bcx = sb("bcx", 8 * 128)
        ebx = sb("ebx", 8 * 128)
        wscb = sb("wscb", KT * 16)
        lnst = sb("lnst", 16)
        rbuf = sb("rbuf", D)
        NAR = 14208
        AR = sb("AR", NAR)
        YSB = sb("ysb", 4 * TT)
        MIXB = sb("mixb", 8 * TT)
        PS = [ps("ps%d" % i, 512) for i in range(8)]
        if len(PHASES) < 4:
            for yb_ in Y:
                for q_ in range(8):
                    g.cp('pool', f32r(yb_.r(q_ * 128, 128)), zeros.r(0, 128))

        stream = []

        def layer_stream(l):
            it = []
            for ct in range(16):
                it.append(("gdn%d" % ct, wfm[l, ct, :, :], 1024))
            for ct in range(12):
                it.append(("m2_%d" % ct, wfm[l, 16 + ct, :, :], 1024))
            for ct in range(8):
                it.append(("hgqf%d" % ct, wfm[l, 28 + ct, :, :], 1024))
            for ct in range(4):
                it.append(("hgi%d" % ct, witm[l, ct, :, :], 1024))
            for ct in range(4):
                it.append(("hgz%d" % ct, wfm[l, 36 + ct, :, :], 1024))
            for ct in range(8):
                it.append(("s5_%d" % ct, wfm[l, 40 + ct, :, :], 1024))
            for ct in range(4):
                for j in range(2):
                    it.append(("glu%d_%d" % (j, ct), wglu[l, j, ct, :, :], 512))
            for dt in range(8):
                for b in range(4):
                    it.append(("gate%d_%d" % (b, dt), wfm[l, 48 + b * 8 + dt, :, :], 1024))
                    it.append(("wbr%d_%d" % (b, dt), wbr[l, dt, b, :, :], 512))
            for k in range(KT):
                it.append(("wout%d" % k, wout[l, k, :, :], 1024))
            pref = []
            if 'gdn' not in PHASES:
                pref.append('gdn')
            if 'ssd' not in PHASES:
                pref.append('m2_')
            if 'hg' not in PHASES:
                pref.append('hg')
            if 's5' not in PHASES:
                pref += ['s5_', 'glu']
            it = [x for x in it if not any(x[0].startswith(q) for q in pref)]
            return it

        wst = {'issued': 0, 'used': 0}
        PF = 6

        def w_begin(l):
            stream[:] = layer_stream(l)
            wst['issued'] = 0
            wst['used'] = 0

        def w_issue():
            i = wst['issued']
            key, src, ncols = stream[i]
            slot = i % NW
            g.dma('pool', f32r(wring.r(slot * 1024, ncols)), DV(src, "w"))
            wst['issued'] += 1

        def w_get(key):
            i = wst['used']
            assert stream[i][0] == key, (stream[i][0], key)
            while wst['issued'] <= min(i + PF, len(stream) - 1):
                w_issue()
            wst['used'] += 1
            return (i % NW) * 1024

        def ln_rows(dst, src_sb, gam, bet):
            for c in range(2):
                kb.op('dve', lambda e: e.bn_stats(out=lnst.h[:, c * 6:(c + 1) * 6], in_=src_sb.ap[:, c * 512:(c + 1) * 512]),
                      reads=src_sb.keys, writes=lnst.keys(0, 16))
            kb.op('dve', lambda e: e.bn_aggr(out=lnst.h[:, 12:14], in_=lnst.h[:, 0:12].rearrange("p (c s) -> p c s", c=2)),
                  reads=lnst.keys(0, 16), writes=lnst.keys(0, 16))
            g.rsqrt(lnst.r(14, 1), lnst.r(13, 1), LN_EPS)
            g.ts('dve', dst, src_sb, lnst.r(12, 1), S_, lnst.r(14, 1), M)
            g.tt('pool', dst, dst, gam, M)
            g.tt('pool', dst, dst, bet, A_)

        def rmsnorm_fm(yv, ntile, wcol, scale_n, tmp_off, psb):
            pass

        if pad > 0:
            g.memset('dve', AR.r(0, D), 0.0)
            r = 0
            while r < pad:
                n = min(128, pad - r)
                for i in range(2):
                    g.dma('sp', DV(hb[i][r:r + n, :], "hb%d" % i), AR.r(0, D, 0, n))
                r += n

        g.dma('sp', prt.r(0, D), DV(lnin[0:1, :].to_broadcast([128, D]), "lnin"))
        g.dma('sp', prt.r(D, D), DV(lnin[1:2, :].to_broadcast([128, D]), "lnin"))
        for t in range(ntiles):
            for b in range(NB):
                r0 = t * TT + b * 128
                lo = max(0, pad - r0)
                if lo >= 128:
                    continue
                j = b % NB
                g.dma('sp', htok.r(j * D, D), DV(xin[r0:r0 + 128, :], "xin"))
                ln_rows(rbuf.r(0, D), htok.r(j * D, D), prt.r(0, D), prt.r(D, D))
                g.dma('pool', DV(hb[0][r0 + lo:r0 + 128, :], "hb0"), rbuf.r(0, D, lo, 128))

        g.dma('sp', lbl.r(0, 4 * depth), DV(lbl_d[:, :], "lbl"))
        g.act(lbl.r(0, 4 * depth), lbl.r(0, 4 * depth), AF.Exp)
        kb.op('dve', lambda e: e.tensor_reduce(out=AR.h[:, 0:4], in_=lbl.h[:, :].rearrange("p (c l) -> p c l", l=depth),
                                               axis=AX.X, op=ALU.add), reads=lbl.keys(0, 4 * depth), writes=AR.keys(0, 4))
        g.recip(AR.r(4, 4), AR.r(0, 4))
        g.memset('dve', lbt.r(0, 4), 0.0)
        for l in range(1, depth):
            ev = V(lbl.h[:, :].rearrange("p (c l) -> p c l", l=depth)[:, :, l], lbl.keys(0, 4 * depth))
            g.tt('dve', lbt.r(l * 4, 4), lbt.r((l - 1) * 4, 4), ev, A_)
        for l in range(1, depth):
            g.tt('dve', lbt.r(l * 4, 4), lbt.r(l * 4, 4), AR.r(4, 4), M)
        g.ts('dve', omlb.r(0, 4 * depth), lbt.r(0, 4 * depth), -1.0, M, 1.0, A_)

        for l in range(L):
            src = hb[l % 2]
            srcn = "hb%d" % (l % 2)
            dst = hb[(l + 1) % 2]
            dstn = "hb%d" % ((l + 1) % 2)
            last = (l == L - 1)
            g.dma('sp', ppt.r(0, NPP), DV(pp_d[l, :, :], "pp"))
            g.dma('sp', wscb.r(0, KT * 16), DV(wsc[l, :, :], "wsc"))
            g.dma('sp', prt.r(0, NPR), DV(pr_d[l:l + 1, :].to_broadcast([128, NPR]), "pr"))
            g.act(negA12.r(0, 12), prt.r(PR_A12, 12), AF.Exp)
            g.ts('dve', negA12.r(0, 12), negA12.r(0, 12), -1.0, M)
            g.ts('dve', wsm.r(0, 1), ppt.r(PP_GNW, 1), math.sqrt(128.0), M)
            g.ts('dve', wsm.r(1, 1), ppt.r(PP_HNW, 1), math.sqrt(128.0), M)
            g.ts('dve', wsm.r(2, 4), ppt.r(PP_MNW, 4), 16.0, M)
            for st_, n_ in ((halo_g, 36), (halo_m, 24), (S_g, 512), (S_m, 1024), (S_h, 512), (xs5, 32)):
                g.memset('pool', st_.r(0, n_), 0.0)
            a_re = ppt.r(PP_SAR, 16)
            a_im = ppt.r(PP_SAI, 16)

            def sl(i):
                return AR.r(8192 + i * 16, 16)
            DT_, ARD, ANG, T1, T2, COS, SIN, LRE, LIM, DEN, ZRE, ZIM = range(12)
            g.act(sl(DT_), ppt.r(PP_SLDT, 16), AF.Exp)
            g.tt('dve', sl(ARD), a_re, sl(DT_), M)
            g.act(rho.r(0, 16), sl(ARD), AF.Exp)
            g.tt('dve', sl(ANG), a_im, sl(DT_), M)
            g.act(sl(SIN), sl(ANG), AF.Sin, scale=1.0 / 32.0)
            g.ts('dve', sl(T2), sl(ANG), 1.0 / 32.0, M, 0.5 * math.pi, A_)
            g.act(sl(COS), sl(T2), AF.Sin)
            for _ in range(5):
                g.tt('dve', sl(T1), sl(COS), sl(COS), M)
                g.tt('dve', sl(T2), sl(SIN), sl(SIN), M)
                g.stt(sl(SIN), sl(SIN), 2.0, sl(COS), M, M)
                g.tt('dve', sl(COS), sl(T1), sl(T2), S_)
            g.tt('dve', sl(LRE), rho.r(0, 16), sl(COS), M)
            g.tt('dve', sl(LIM), rho.r(0, 16), sl(SIN), M)
            g.tt('dve', sl(DEN), a_re, a_re, M)
            g.tt('dve', sl(T1), a_im, a_im, M)
            g.tt('dve', sl(DEN), sl(DEN), sl(T1), A_)
            g.recip(sl(DEN), sl(DEN))
            g.ts('dve', sl(LRE), sl(LRE), -1.0, A_)
            g.tt('dve', sl(T1), sl(LRE), a_re, M)
            g.tt('dve', sl(T2), sl(LIM), a_im, M)
            g.tt('dve', sl(ZRE), sl(T1), sl(T2), A_)
            g.tt('dve', sl(ZRE), sl(ZRE), sl(DEN), M)
            g.tt('dve', sl(T1), sl(LIM), a_re, M)
            g.tt('dve', sl(T2), sl(LRE), a_im, M)
            g.tt('dve', sl(ZIM), sl(T1), sl(T2), S_)
            g.tt('dve', sl(ZIM), sl(ZIM), sl(DEN), M)
            g.cp('dve', costab.rs(0, 16, S5S, 0, 1), usq(sl(COS), 2))
            g.cp('dve', sintab.rs(0, 16, S5S, 0, 1), usq(sl(SIN), 2))
            n = 1
            while n < S5S:
                cn = bcast(costab.rs(0, 16, S5S, n - 1, 1), [128, 16, n])
                sn = bcast(sintab.rs(0, 16, S5S, n - 1, 1), [128, 16, n])
                t1 = AR.r3(8448, 16, n)
                t2 = AR.r3(9472, 16, n)
                c0 = costab.rs(0, 16, S5S, 0, n)
                s0 = sintab.rs(0, 16, S5S, 0, n)
                g.tt('dve', t1, c0, cn, M)
                g.tt('dve', t2, s0, sn, M)
                g.tt('dve', costab.rs(0, 16, S5S, n, n), t1, t2, S_)
                g.tt('dve', t1, s0, cn, M)
                g.tt('dve', t2, c0, sn, M)
                g.tt('dve', sintab.rs(0, 16, S5S, n, n), t1, t2, A_)
                n *= 2
            Bre = AR.r3(0, 16, 128)
            Bim = AR.r3(2048, 16, 128)
            X1 = AR.r3(4096, 16, 128)
            X2 = AR.r3(6144, 16, 128)
            g.dma('sp', AR.r(0, 2048), DV(s5b_d[l, 0, :, :], "s5b"))
            g.dma('sp', AR.r(2048, 2048), DV(s5b_d[l, 1, :, :], "s5b"))
            zre_b = bcast(usq(sl(ZRE), 2), [128, 16, 128])
            zim_b = bcast(usq(sl(ZIM), 2), [128, 16, 128])
            g.tt('dve', X1, Bre, zre_b, M)
            g.tt('dve', X2, Bim, zim_b, M)
            g.tt('dve', X1, X1, X2, S_)
            g.tt('dve', X2, Bim, zre_b, M)
            g.tt('dve', Bim, Bre, zim_b, M)
            g.tt('dve', X2, X2, Bim, A_)
            for ri, xo in enumerate((4096, 6144)):
                for pr4 in range(4):
                    pst = PS[pr4 % 2]
                    for q in range(4):
                        pr_ = pr4 * 4 + q
                        g.tr(pst.r(q * 128, 128), AR.r(xo + pr_ * 128, 128))
                    g.cp('act', BbT.r(ri * 2048 + pr4 * 512, 512), pst.r(0, 512))
            g.dma('sp', Cpd.r(0, 512), DV(s5c_d[l, 0, :, :], "s5c"))
            g.dma('sp', Cpd.r(512, 512), DV(s5c_d[l, 1, :, :], "s5c"))
            g.ts('pool', Cpd.r(512, 512), Cpd.r(512, 512), -1.0, M)

            def rows(ap, r0, n):
                if isinstance(r0, int):
                    return ap[r0:r0 + n, :]
                return ap[bass.ds(r0, n), :]

            def tile_body(t, dyn):
                tok0 = t * TT
                has_pad = (not dyn) and (tok0 < pad)
                w_begin(l)
                for b in range(NB):
                    r0 = tok0 + b * 128
                    g.dma('sp', htok.r(b * D, D), DV(rows(src, r0, 128), srcn))
                for k in range(KT):
                    pst = PS[k % 2]
                    for b in range(NB):
                        g.tr(pst.r(b * 128, 128), htok.r(b * D + k * 128, 128))
                    g.cp('act' if k % 2 == 0 else 'dve', f32r(hT.r(k * TT, TT)), pst.r(0, TT))
                if has_pad:
                    for b in range(NB):
                        lo = pad - (tok0 + b * 128)
                        g.asel(padm.r(b, 1), ones.r(0, 1), [[0, 1]], ALU.is_ge, 0.0, -lo, 1)

                pp_state = {'i': 0}
                W3 = TT + 3

                def proj_fm(key):
                    wo = w_get(key)
                    pst = PS[pp_state['i'] % 2]
                    pp_state['i'] += 1
                    for k in range(KT):
                        g.mm(pst.r(0, TT), wring.r(wo + k * 128, 128), hT.r(k * TT, TT), start=(k == 0), stop=(k == KT - 1), r=True)
                    return pst.r(0, TT)

                def block_scalars(b, c0, nh, sc_t=None, bxo=0):
                    sc_t = scb if sc_t is None else sc_t
                    R = lambda i, n=nh, o=0: sc_t.r(i * 16 + o, n)
                    pst = PS[2]
                    for k in range(KT):
                        g.mm(pst.r(0, 16), hT.r(k * TT + b * 128, 128), wscb.r(k * 16, 16),
                             start=(k == 0), stop=(k == KT - 1))
                    g.cp('dve', R(0, 16), pst.r(0, 16))
                    if c0 == 4:
                        g.act(R(1, 4), R(0, 4), AF.Sigmoid)
                    bo = PR_B12 + (0 if c0 == 4 else 4)
                    ao = 0 if c0 == 4 else 4
                    g.tt('dve', R(2), R(0, nh, c0), prt.r(bo, nh), A_)
                    g.act(R(3), R(2), AF.Abs)
                    g.act(R(3), R(3), AF.Exp, scale=-1.0)
                    g.act(R(3), R(3), AF.Ln, bias=1.0)
                    g.ts('dve', R(2), R(2), 0.0, ALU.max)
                    g.tt('dve', R(4), R(2), R(3), A_)
                    if has_pad and c0 == 8:
                        g.ts('dve', R(4), R(4), padm.r(b, 1), M)
                    g.tt('dve', R(5), R(4), negA12.r(ao, nh), M)
                    g.mm(pst.r(16, nh), TRIU.r(0, 128), R(5))
                    g.cp('dve', R(6), pst.r(16, nh))
                    Rv = AR.r3(11392, nh, 128)
                    g.tt('pool', Rv, bcast(usq(TRIU.r(0, 128), 1), [128, nh, 128]), bcast(usq(R(5), 2), [128, nh, 128]), M)
                    for q in range(nh // 4):
                        pq = PS[3]
                        g.mm(pq.r(0, 512), ones.r(0, 128), AR.r(11392 + q * 512, 512))
                        g.cp('act', bcx.r(bxo + q * 512, 512), pq.r(0, 512))
                    g.act(ebx.r(bxo, nh * 128), bcx.r(bxo, nh * 128), AF.Exp)
                    g.act(R(7), R(6), AF.Exp)
                    last = bcx.rs(bxo, nh, 128, 127, 1)
                    g.tt('dve', usq(R(8), 2), last, usq(R(6), 2), S_)
                    g.act(R(8), R(8), AF.Exp)
                    if c0 == 4:
                        g.tt('dve', R(9), R(1, 4), R(7), M)
                    else:
                        g.tt('dve', R(10), R(4), R(8), M)
                    return R

                def conv_fm(nct, xc_off, acc_off, halo, cw_off, cb_off):
                    W3 = TT + 3
                    g.cp('pool', AR.rs(xc_off, nct, W3, 0, 3), halo.r3(0, nct, 3))
                    for ct in range(nct):
                        xo = xc_off + ct * W3
                        acc = AR.r(acc_off + ct * TT, TT)
                        cw = lambda j: ppt.r(cw_off + ct * 4 + j, 1)
                        if cb_off is None:
                            g.ts('dve', acc, AR.r(xo + 3, TT), cw(3), M)
                        else:
                            g.ts('dve', acc, AR.r(xo + 3, TT), cw(3), M, ppt.r(cb_off + ct, 1), A_)
                        for j in range(3):
                            g.stt(acc, AR.r(xo + j, TT), cw(j), acc, M, A_)
                        g.act(acc, acc, AF.Silu)
                    g.cp('pool', halo.r3(0, nct, 3), AR.rs(xc_off, nct, W3, TT, 3))

                def rms_fm(ytiles, wcols, neps, zoffs, sq_off, rn_off):
                    pst = PS[3]
                    for i, yv in enumerate(ytiles):
                        sq = AR.r(sq_off, TT)
                        g.tt('pool', sq, yv, yv, M)
                        g.mm(pst.r(0, TT), ones.r(0, 128), sq, start=(i == 0), stop=(i == len(ytiles) - 1))
                    rn = AR.r(rn_off, TT)
                    g.rsqrt(rn, pst.r(0, TT), neps)
                    for yv, wc in zip(ytiles, wcols):
                        g.stt(yv, yv, wc, rn, M, M)

                if 'gdn' in PHASES:
                    XC, QKV, ZS = 0, 3200, 6272
                    W3 = TT + 3
                    for ct in range(12):
                        pv = proj_fm("gdn%d" % ct)
                        g.cp('act', AR.r(XC + ct * W3 + 3, TT), pv)
                    conv_fm(12, XC, QKV, halo_g, PP_CWG, None)
                    for ct in range(4):
                        pv = proj_fm("gdn%d" % (12 + ct))
                        g.act(AR.r(ZS + ct * TT, TT), pv, AF.Silu)
                    for ct in range(8):
                        x = AR.r(QKV + ct * TT, TT)
                        sq = AR.r(9600, TT)
                        g.tt('pool', sq, x, x, M)
                        pst = PS[3]
                        g.mm(pst.r(0, TT), ones.r(0, 128), sq)
                        rn = AR.r(9856, TT)
                        g.rsqrt(rn, pst.r(0, TT), RMS_EPS)
                        if ct < 4:
                            g.stt(x, rn, 128.0 ** -0.5, x, M, M)
                        else:
                            g.tt('dve', x, x, rn, M)
                    OSB = 10112
                    SETS = [7296, 11904]
                    assert NB == 2
                    Rb = [block_scalars(b, 4, 4, (scb, scb2)[b], b * 512) for b in range(NB)]
                    PSC = [(PS[4], PS[5], PS[6]), (PS[0], PS[1], PS[2])]

                    def nm(b):
                        base = SETS[b]
                        d = dict(zip(("TA", "Dm", "Am", "Bm", "QKT", "VB", "KBG", "KDEC", "Um", "WTm", "VNEW", "QDEC"),
                                     [base + i * 128 for i in range(12)]))
                        d["PQ"] = [base + 1536 + 128 * i for i in range(4)]
                        d["RR"] = [base + 2048, base + 2176]
                        return d

                    def chainA(b, h):
                        n_ = nm(b)
                        TA, Dm, Am, Bm, QKT, VB, KBG, KDEC, Um, WTm = [n_[k] for k in ("TA", "Dm", "Am", "Bm", "QKT", "VB", "KBG", "KDEC", "Um", "WTm")]
                        PQ, RR = n_["PQ"], n_["RR"]
                        pa, pb, pc = PSC[b]
                        R = Rb[b]
                        bxo = b * 512
                        qT = AR.r(QKV + h * TT + b * 128, 128)
                        kT = AR.r(QKV + (4 + h) * TT + b * 128, 128)
                        vT = AR.r(QKV + (8 + h) * TT + b * 128, 128)
                        gcb = bcx.r(bxo + h * 128, 128)
                        gcc = R(6, 1, h)
                        g.mm(pa.r(0, 128), kT, kT)
                        g.mm(pa.r(128, 128), kT, qT)
                        yield
                        g.stt(AR.r(TA, 128), gcb, gcc, NEGS.r(0, 128), S_, A_)
                        g.act(AR.r(Dm, 128), AR.r(TA, 128), AF.Exp, scale=-1.0)
                        g.stt(AR.r(Am, 128), AR.r(Dm, 128), R(1, 1, h), pa.r(0, 128), M, M)
                        yield
                        g.stt(AR.r(TA, 128), gcb, gcc, NEGT.r(0, 128), S_, A_)
                        g.act(AR.r(Dm, 128), AR.r(TA, 128), AF.Exp)
                        g.tt('dve', AR.r(QKT, 128), AR.r(Dm, 128), pa.r(128, 128), M)
                        yield
                        g.tr(pa.r(256, 128), AR.r(Am, 128))
                        g.cp('act', AR.r(Bm, 128), pa.r(256, 128))
                        yield
                        g.tr(pa.r(384, 128), kT)
                        g.ts('dve', AR.r(KBG, 128), pa.r(384, 128), R(9, 1, h), M)
                        g.ts('dve', AR.r(KDEC, 128), pa.r(384, 128), R(8, 1, h), M)
                        yield
                        g.tr(pb.r(0, 128), vT)
                        g.ts('dve', AR.r(VB, 128), pb.r(0, 128), R(1, 1, h), M)
                        g.tt('pool', AR.r(RR[0], 128), ident.r(0, 128), AR.r(Bm, 128), S_)
                        yield
                        Pc, Qc = Am, Bm
                        ri = 0
                        for lev in range(1, 7):
                            Pn, Qn = PQ[(lev % 2) * 1], PQ[2 + (lev % 2) * 1]
                            g.mm(pb.r(128, 128), AR.r(Qc, 128), AR.r(Pc, 128))
                            g.cp('act', AR.r(Pn, 128), pb.r(128, 128))
                            yield
                            if lev < 6:
                                g.mm(pb.r(256, 128), AR.r(Pc, 128), AR.r(Qc, 128))
                                g.cp('dve', AR.r(Qn, 128), pb.r(256, 128))
                                yield
                            g.mm(pb.r(384, 128), AR.r(Pn, 128), AR.r(RR[ri], 128))
                            g.tt('dve', AR.r(RR[1 - ri], 128), AR.r(RR[ri], 128), pb.r(384, 128), A_)
                            yield
                            ri = 1 - ri
                            Pc, Qc = Pn, Qn
                        TTm = AR.r(RR[ri], 128)
                        g.mm(pc.r(0, 128), TTm, AR.r(VB, 128))
                        g.cp('act', AR.r(Um, 128), pc.r(0, 128))
                        yield
                        g.mm(pc.r(128, 128), AR.r(KBG, 128), TTm)
                        g.cp('act', AR.r(WTm, 128), pc.r(128, 128))
                        yield

                    def partB(b, h):
                        n_ = nm(b)
                        QKT, KDEC, Um, WTm, VNEW, QDEC = [n_[k] for k in ("QKT", "KDEC", "Um", "WTm", "VNEW", "QDEC")]
                        pc = PSC[b][2]
                        bxo = b * 512
                        qT = AR.r(QKV + h * TT + b * 128, 128)
                        Sv = S_g.r(h * 128, 128)
                        g.mm(pc.r(256, 128), AR.r(WTm, 128), Sv)
                        g.tt('dve', AR.r(VNEW, 128), AR.r(Um, 128), pc.r(256, 128), S_)
                        g.tt('pool', AR.r(QDEC, 128), qT, ebx.r(bxo + h * 128, 128), M)
                        po = PS[7]
                        ov = po.r((h % 2) * 256 + b * 128, 128)
                        g.mm(ov, Sv, AR.r(QDEC, 128), start=True, stop=False)
                        g.mm(ov, AR.r(VNEW, 128), AR.r(QKT, 128), start=False, stop=True)
                        g.cp('act', AR.r(OSB + h * TT + b * 128, 128), ov)
                        g.mm(pc.r(384, 128), AR.r(KDEC, 128), AR.r(VNEW, 128))
                        elast = ebx.r(bxo + h * 128 + 127, 1)
                        g.stt(Sv, Sv, elast, pc.r(384, 128), M, A_)

                    for h in range(4):
                        alive = [chainA(0, h), chainA(1, h)]
                        while alive:
                            for gen in list(alive):
                                try:
                                    next(gen)
                                except StopIteration:
                                    alive.remove(gen)
                        partB(0, h)
                        partB(1, h)
                    for h in range(4):
                        ov = AR.r(OSB + h * TT, TT)
                        rms_fm([ov], [wsm.r(0, 1)], 128.0 * RMS_EPS, None, 9600, 9856)
                        g.tt('pool', f32r(Y[0].r(h * TT, TT)), ov, AR.r(ZS + h * TT, TT), M)

                if 'ssd' in PHASES:
                    XC, XBC, ZS = 0, 3200, 6272
                    for ct in range(8):
                        pv = proj_fm("m2_%d" % ct)
                        g.cp('act', AR.r(XC + ct * W3 + 3, TT), pv)
                    conv_fm(8, XC, XBC, halo_m, PP_CWM, PP_CBM)
                    for ct in range(4):
                        pv = proj_fm("m2_%d" % (8 + ct))
                        g.act(AR.r(ZS + ct * TT, TT), pv, AF.Silu)
                    XDTP, XDEC, BTOK, CBT, DTm, CTD, TMPS = 7296, 8320, 8832, 9088, 9344, 10368, 12416
                    g.memset('pool', AR.r(XDTP, 1024), 0.0)
                    YPS = [PS[6], PS[7]]
                    for b in range(NB):
                        R = block_scalars(b, 8, 8)
                        px = PS[4]
                        for c in range(4):
                            g.tr(px.r(c * 128, 128), AR.r(XBC + c * TT + b * 128, 128))
                        pbk = PS[5]
                        for gi in range(2):
                            g.tr(pbk.r(gi * 128, 128), AR.r(XBC + (4 + gi) * TT + b * 128, 128))
                        g.cp('act', AR.r(BTOK, 256), pbk.r(0, 256))
                        for par in range(2):
                            outv = V(AR.h[:, XDTP:XDTP + 1024].rearrange("p (c q) -> p c q", q=256)[:, :, par * 192: par * 192 + 64],
                                     AR.keys(XDTP, 1024))
                            inv = V(px.h[:, 0:512].rearrange("p (c q) -> p c q", q=128)[:, :, par * 64:(par + 1) * 64], px.keys(0, 512))
                            sp_ = V(scb.h[:, 4 * 16:4 * 16 + 8].rearrange("p (c q) -> p c q", q=2)[:, :, par:par + 1].to_broadcast([128, 4, 64]),
                                    scb.keys(64, 8))
                            g.tt('dve', outv, inv, sp_, M)
                        g.tt('dve', AR.r3(XDEC, 8, 64), px.r3(0, 8, 64), bcast(usq(R(10), 2), [128, 8, 64]), M)
                        for gi in range(2):
                            g.mm(pbk.r(256 + gi * 128, 128), AR.r(XBC + (4 + gi) * TT + b * 128, 128),
                                 AR.r(XBC + (6 + gi) * TT + b * 128, 128))
                        g.cp('act', AR.r(CBT, 256), pbk.r(256, 256))
                        g.tt('dve', AR.r3(TMPS, 8, 128), bcx.r3(0, 8, 128), bcast(usq(R(6), 2), [128, 8, 128]), S_)
                        g.tt('pool', AR.r3(TMPS, 8, 128), AR.r3(TMPS, 8, 128), bcast(usq(NEGT.r(0, 128), 1), [128, 8, 128]), A_)
                        g.act(AR.r(DTm, 1024), AR.r(TMPS, 1024), AF.Exp)
                        for gi in range(2):
                            g.tt('pool', AR.r3(DTm + gi * 512, 4, 128), AR.r3(DTm + gi * 512, 4, 128),
                                 bcast(usq(AR.r(CBT + gi * 128, 128), 1), [128, 4, 128]), M)
                            g.tt('pool', AR.r3(CTD + gi * 512, 4, 128), ebx.r3(gi * 512, 4, 128),
                                 bcast(usq(AR.r(XBC + (6 + gi) * TT + b * 128, 128), 1), [128, 4, 128]), M)
                        for c in range(4):
                            yv = YPS[c // 2].r((c % 2) * 256 + b * 128, 128)
                            for hh in range(2):
                                h = 2 * c + hh
                                g.mm(yv, AR.r(XDTP + h * 128, 128), AR.r(DTm + h * 128, 128), start=(hh == 0), stop=False)
                                g.mm(yv, S_m.r(h * 128, 128), AR.r(CTD + h * 128, 128), start=False, stop=(hh == 1))
                        pu = PS[3]
                        for gi in range(2):
                            g.mm(pu.r(gi * 256, 256), AR.r(BTOK + gi * 128, 128), AR.r(XDEC + gi * 256, 256))
                        for par in range(2):
                            sv = V(S_m.h[:, :].rearrange("p (c q) -> p c q", q=256)[:, :, par * 192: par * 192 + 64], S_m.keys(0, 1024))
                            el = V(ebx.h[:, :].rearrange("p (c q) -> p c q", q=256)[:, :, par * 128 + 127: par * 128 + 128].to_broadcast([128, 4, 64]),
                                   ebx.keys(0, 1024))
                            uv = V(pu.h[:, 0:512].rearrange("p (c q) -> p c q", q=128)[:, :, par * 64:(par + 1) * 64], pu.keys(0, 512))
                            g.tt('pool', sv, sv, el, M)
                            g.tt('dve', sv, sv, uv, A_)
                    for c in range(4):
                        yv = AR.r(c * TT, TT)
                        g.stt(yv, AR.r(XBC + c * TT, TT), ppt.r(PP_MD + c, 1), YPS[c // 2].r((c % 2) * 256, TT), M, A_)
                        g.tt('pool', yv, yv, AR.r(ZS + c * TT, TT), M)
                    for gi in range(2):
                        tiles = [AR.r((2 * gi + i) * TT, TT) for i in range(2)]
                        rms_fm(tiles, [wsm.r(2 + 2 * gi + i, 1) for i in range(2)], 256.0 * RMS_EPS, None, 9600, 9856)
                        for i in range(2):
                            g.cp('pool', f32r(Y[1].r((2 * gi + i) * TT, TT)), tiles[i])

                if 'hg' in PHASES:
                    QH, Fo, LOGF, GC, EG, QD, KD, KK, KDE, ZS, ITOK, ATM, KDT = \
                        0, 1024, 2048, 3072, 4096, 5120, 6144, 7168, 8192, 9216, 10240, 12288, 12544
                    for ct in range(4):
                        pv = proj_fm("hgqf%d" % ct)
                        g.act(AR.r(QH + ct * TT, TT), pv, AF.Silu)
                    for ct in range(4):
                        pv = proj_fm("hgqf%d" % (4 + ct))
                        f = AR.r(Fo + ct * TT, TT)
                        g.act(f, pv, AF.Sigmoid)
                        g.ts('dve', f, f, omlb.r(l * 4 + ct, 1), M, lbt.r(l * 4 + ct, 1), A_)
                        g.act(AR.r(LOGF + ct * TT, TT), f, AF.Ln)
                        g.ts('pool', AR.r(KK + ct * TT, TT), f, -1.0, M, 1.0, A_)
                        g.scan(AR.r(GC + ct * TT, TT), SEG.r(0, TT), AR.r(LOGF + ct * TT, TT), 0.0)
                    g.act(AR.r(EG, 1024), AR.r(GC, 1024), AF.Exp)
                    g.tt('pool', AR.r(QD, 1024), AR.r(QH, 1024), AR.r(EG, 1024), M)
                    g.ts('dve', AR.r(KD, 1024), AR.r(GC, 1024), -80.0, ALU.max)
                    g.act(AR.r(KD, 1024), AR.r(KD, 1024), AF.Exp, scale=-1.0)
                    g.tt('pool', AR.r(KD, 1024), AR.r(KD, 1024), AR.r(KK, 1024), M)
                    for ct in range(4):
                        for c in range(NCH):
                            o = ct * TT + c * 64
                            g.act(AR.r(KDE + o, 64), AR.r(GC + o, 64), AF.Exp, bias=AR.r(GC + o + 63, 1), scale=-1.0)
                    g.tt('pool', AR.r(KDE, 1024), AR.r(KDE, 1024), AR.r(KK, 1024), M)
                    for ct in range(4):
                        wo = w_get("hgi%d" % ct)
                        for c in range(NCH):
                            for k in range(KT):
                                g.mm(PS[4 + c].r(ct * 128, 128, 0, 64), hT.r(k * TT + c * 64, 64), wring.r(wo + k * 128, 128),
                                     start=(k == 0), stop=(k == KT - 1), r=True)
                    for c in range(NCH):
                        g.cp('act' if c % 2 == 0 else 'dve', AR.r(ITOK + c * 512, 512, 0, 64), PS[4 + c].r(0, 512, 0, 64))
                    for ct in range(4):
                        pv = proj_fm("hgz%d" % ct)
                        g.act(AR.r(ZS + ct * TT, TT), pv, AF.Silu)
                    OPS = [PS[6], PS[7]]
                    for c in range(NCH):
                        pa = PS[2]
                        for h in range(4):
                            o = h * TT + c * 64
                            g.mm(pa.r(h * 64, 64, 0, 64), AR.r(KD + o, 64), AR.r(QD + o, 64))
                        g.tt('dve', AR.r3(ATM, 4, 64, 0, 64), pa.r3(0, 4, 64, 0, 64), bcast(usq(M01.r(0, 64, 0, 64), 1), [64, 4, 64]), M)
                        pk = PS[3]
                        for h in range(4):
                            o = h * TT + c * 64
                            g.tr(pk.r(h * 128, 128, 0, 64), AR.r(KDE + o, 64))
                        g.cp('act', AR.r(KDT, 512, 0, 64), pk.r(0, 512, 0, 64))
                        for h in range(4):
                            o = h * TT + c * 64
                            iv = AR.r(ITOK + c * 512 + h * 128, 128, 0, 64)
                            Sv = S_h.r(h * 128, 128)
                            ov = OPS[h // 2].r((h % 2) * 256 + c * 64, 64)
                            g.mm(ov, iv, AR.r(ATM + h * 64, 64, 0, 64), start=True, stop=False)
                            g.mm(ov, Sv, AR.r(QD + o, 64), start=False, stop=True)
                            pu = PS[4 + h % 2]
                            g.mm(pu.r(256, 128), AR.r(KDT + h * 128, 128, 0, 64), iv)
                            g.stt(Sv, Sv, AR.r(EG + o + 63, 1), pu.r(256, 128), M, A_)
                    for h in range(4):
                        ov = AR.r(Fo + h * TT, TT)
                        g.cp('act', ov, OPS[h // 2].r((h % 2) * 256, TT))
                        rms_fm([ov], [wsm.r(1, 1)], 128.0 * RMS_EPS, None, LOGF, LOGF + TT)
                        g.tt('pool', f32r(Y[2].r(h * TT, TT)), ov, AR.r(ZS + h * TT, TT), M)

                if 's5' in PHASES:
                    Uo, ZS, T1o, WRE, WIM, ZR, ZI, XRE, XIM, YTOK, YS, GEL = \
                        0, 1024, 2048, 4096, 4608, 5120, 5632, 6144, 6656, 7168, 7680, 8704
                    for ct in range(4):
                        pv = proj_fm("s5_%d" % ct)
                        g.cp('act', AR.r(Uo + ct * TT, TT), pv)
                    for ct in range(4):
                        pv = proj_fm("s5_%d" % (4 + ct))
                        g.act(AR.r(ZS + ct * TT, TT), pv, AF.Silu)
                    for b in range(NB):
                        py = PS[7]
                        for ft in range(4):
                            pre, pim = PS[4], PS[5]
                            uv = AR.r(Uo + ft * TT + b * 128, 128)
                            for q in range(4):
                                pr_ = ft * 4 + q
                                g.mm(pre.r(q * 128, 128), BbT.r(pr_ * 128, 128), uv)
                                g.mm(pim.r(q * 128, 128), BbT.r(2048 + pr_ * 128, 128), uv)
                            ct_ = costab.r(ft * 512, 512)
                            st_ = sintab.r(ft * 512, 512)
                            t = [AR.r(T1o + i * 512, 512) for i in range(4)]
                            g.tt('dve', t[0], pre.r(0, 512), ct_, M)
                            g.tt('dve', t[1], pim.r(0, 512), st_, M)
                            g.tt('dve', t[2], pim.r(0, 512), ct_, M)
                            g.tt('dve', t[3], pre.r(0, 512), st_, M)
                            g.tt('pool', AR.r(WRE, 512), t[0], t[1], A_)
                            g.tt('pool', AR.r(WIM, 512), t[2], t[3], S_)
                            for q in range(4):
                                pr_ = ft * 4 + q
                                rb = bcast(rho.r(pr_, 1), [128, 128])
                                g.scan(AR.r(ZR + q * 128, 128), rb, AR.r(WRE + q * 128, 128), xs5.r(pr_, 1))
                                g.scan(AR.r(ZI + q * 128, 128), rb, AR.r(WIM + q * 128, 128), xs5.r(16 + pr_, 1))
                            g.tt('pool', t[0], AR.r(ZR, 512), ct_, M)
                            g.tt('pool', t[1], AR.r(ZI, 512), st_, M)
                            g.tt('pool', t[2], AR.r(ZI, 512), ct_, M)
                            g.tt('pool', t[3], AR.r(ZR, 512), st_, M)
                            g.tt('pool', AR.r(XRE, 512), t[0], t[1], S_)
                            g.tt('pool', AR.r(XIM, 512), t[2], t[3], A_)
                            g.cp('pool', xs5.r3(ft * 4, 4, 1), AR.rs(XRE, 4, 128, 127, 1))
                            g.cp('pool', xs5.r3(16 + ft * 4, 4, 1), AR.rs(XIM, 4, 128, 127, 1))
                            for q in range(4):
                                pr_ = ft * 4 + q
                                yv = py.r(ft * 128 + q * 32, 32)
                                g.mm(yv, AR.r(XRE + q * 128, 128), Cpd.r(pr_ * 32, 32), start=True, stop=False)
                                g.mm(yv, AR.r(XIM + q * 128, 128), Cpd.r(512 + pr_ * 32, 32), start=False, stop=True)
                        g.cp('act', AR.r(YTOK, 512), py.r(0, 512))
                        pyt = PS[6]
                        for ft in range(4):
                            g.tr(pyt.r(ft * 128, 128), AR.r(YTOK + ft * 128, 128))
                        for ft in range(4):
                            g.stt(AR.r(YS + ft * TT + b * 128, 128), AR.r(Uo + ft * TT + b * 128, 128), ppt.r(PP_SD + ft, 1),
                                  pyt.r(ft * 128, 128), M, A_)
                    x = AR.r(YS, 1024)
                    g1 = AR.r(GEL, 1024)
                    g2 = AR.r(GEL + 1024, 1024)
                    g.tt('pool', g1, x, x, M)
                    g.ts('dve', g1, g1, 0.044715, M, 1.0, A_)
                    g.tt('pool', g1, g1, x, M)
                    g.act(g2, g1, AF.Sigmoid, scale=2.0 * math.sqrt(2.0 / math.pi))
                    g.tt('pool', f32r(YSB.r(0, 4 * TT)), x, g2, M)
                    for ct in range(4):
                        w1 = w_get("glu0_%d" % ct)
                        w2 = w_get("glu1_%d" % ct)
                        p1, p2 = PS[4], PS[5]
                        for k in range(4):
                            g.mm(p1.r(0, TT), wring.r(w1 + k * 128, 128), YSB.r(k * TT, TT), start=(k == 0), stop=(k == 3), r=True)
                        for k in range(4):
                            g.mm(p2.r(0, TT), wring.r(w2 + k * 128, 128), YSB.r(k * TT, TT), start=(k == 0), stop=(k == 3), r=True)
                        sg = AR.r(GEL + ct * TT, TT)
                        g.act(sg, p2.r(0, TT), AF.Sigmoid)
                        g.tt('dve', sg, sg, p1.r(0, TT), M)
                        g.tt('pool', f32r(Y[3].r(ct * TT, TT)), sg, AR.r(ZS + ct * TT, TT), M)

                MIX, SGT, TMPM = 0, 2048, 2304
                for dt in range(8):
                    for bq in range(4):
                        pv = proj_fm("gate%d_%d" % (bq, dt))
                        sg = AR.r(SGT, TT)
                        g.act(sg, pv, AF.Sigmoid)
                        wo = w_get("wbr%d_%d" % (bq, dt))
                        pb_ = PS[2 + bq % 2]
                        for k in range(4):
                            g.mm(pb_.r(0, TT), wring.r(wo + k * 128, 128), Y[bq].r(k * TT, TT), start=(k == 0), stop=(k == 3), r=True)
                        if bq == 0:
                            g.tt('dve', f32r(MIXB.r(dt * TT, TT)), sg, pb_.r(0, TT), M)
                        else:
                            g.tt('dve', AR.r(TMPM, TT), sg, pb_.r(0, TT), M)
                            g.tt('pool', f32r(MIXB.r(dt * TT, TT)), MIXB.r(dt * TT, TT), AR.r(TMPM, TT), A_)
                for k in range(KT):
                    wo = w_get("wout%d" % k)
                    for b in range(NB):
                        for hf in range(2):
                            g.mm(PS[4 + b * 2 + hf].r(0, 512), MIXB.r(k * TT + b * 128, 128), wring.r(wo + hf * 512, 512),
                                 start=(k == 0), stop=(k == KT - 1), r=True)
                for b in range(NB):
                    r0 = tok0 + b * 128
                    for hf in range(2):
                        g.stt(rbuf.r(hf * 512, 512), htok.r(b * D + hf * 512, 512), ALPHA, PS[4 + b * 2 + hf].r(0, 512), M, A_)
                    ln_rows(rbuf.r(0, D), rbuf.r(0, D), prt.r(PR_LNG, D), prt.r(PR_LNB, D))
                    if dyn:
                        if not last:
                            g.dma('pool', DV(rows(dst, r0, 128), dstn), rbuf.r(0, D))
                        else:
                            g.dma('pool', DV(rows(yout, r0 - out_row0, 128), "y"), rbuf.r(0, D))
                        continue
                    lo = max(0, pad - r0)
                    if lo >= 128:
                        continue
                    if not last:
                        g.dma('pool', DV(dst[r0 + lo:r0 + 128, :], dstn), rbuf.r(0, D, lo, 128))
                    else:
                        lo2 = max(0, out_row0 - r0)
                        if lo2 < 128:
                            g.dma('pool', DV(yout[r0 + lo2 - out_row0:r0 + 128 - out_row0, :], "y"), rbuf.r(0, D, lo2, 128))
                assert wst['used'] == len(stream), (wst, len(stream))

            tile_body(0, False)
            kb.barrier()
            if ntiles == 2:
                tile_body(1, False)
                kb.barrier()
            elif ntiles > 2:
                assert out_row0 <= TT
                kb.dry = True
                tile_body(1, True)
                n_it = kb.barrier()
                kb.dry = False
                with nc.Fori(1, ntiles) as ti:
                    kb.enter_loop(ti, n_it, 1)
                    tile_body(ti, True)
                    kb.barrier()
                kb.exit_loop(ntiles - 1)
        kb.barrier()
        print("[build] ninst=%d nwait=%d per_eng=%s" % (kb.ninst, kb.nwait, kb.per_eng), flush=True)
    return nc


def _fm(w, c0, ncol):
    nt = ncol // 128
    x = w[:, c0:c0 + ncol].reshape(KT, 128, nt, 128)
    return np.ascontiguousarray(x.transpose(2, 1, 0, 3)).reshape(nt, 128, KT * 128)


def prep_weights(inp, depth):
    f = lambda a: np.asarray(a, dtype=np.float32)
    L = depth
    w_in = f(inp['w_in'])
    wfm = np.empty((L, 80, 128, KT * 128), np.float32)
    wsc = np.empty((L, 128, KT * 16), np.float32)
    witm = np.empty((L, 4, 128, KT * 128), np.float32)
    for l in range(L):
        w = w_in[l]
        parts = [_fm(w, O_GQKV, 1536), _fm(w, O_GZ, 512), _fm(w, O_MX, 1024), _fm(w, O_MZ, 512),
                 _fm(w, O_HQ, 512), _fm(w, O_HF, 512), _fm(w, O_HZ, 512), _fm(w, O_SU, 512), _fm(w, O_SZ, 512),
                 _fm(w, O_GATE, 4096)]
        wfm[l] = np.concatenate(parts, axis=0)
        sc = np.concatenate([w[:, O_GB:O_GB + 4], w[:, O_GA:O_GA + 4], w[:, O_MDT:O_MDT + 8]], axis=1)
        wsc[l] = sc.reshape(KT, 128, 16).transpose(1, 0, 2).reshape(128, KT * 16)
        witm[l] = _fm(w, O_HI, 512)
    glu = np.stack([f(inp['s5_glu_w1']), f(inp['s5_glu_w2'])], axis=1)
    wglu = np.ascontiguousarray(glu.reshape(L, 2, 4, 128, 4, 128).transpose(0, 1, 4, 3, 2, 5)).reshape(L, 2, 4, 128, 512)
    wb = f(inp['w_branch'])
    wbr = np.ascontiguousarray(wb.reshape(L, 4, 4, 128, 8, 128).transpose(0, 4, 1, 3, 2, 5)).reshape(L, 8, 4, 128, 512)
    wout = np.ascontiguousarray(f(inp['w_out']).reshape(L, KT, 128, D))
    pp = np.zeros((L, 128, NPP), np.float32)
    pr = np.zeros((L, NPR), np.float32)
    for l in range(L):
        pp[l, :, PP_CWG:PP_CWG + 48] = f(inp['gdn_conv_w'])[l].reshape(4, 12, 128).transpose(2, 1, 0).reshape(128, 48)
        pp[l, :, PP_CWM:PP_CWM + 32] = f(inp['m2_conv_w'])[l].reshape(4, 8, 128).transpose(2, 1, 0).reshape(128, 32)
        pp[l, :, PP_CBM:PP_CBM + 8] = f(inp['m2_conv_b'])[l].reshape(8, 128).T
        pp[l, :, PP_GNW] = f(inp['gdn_norm_w'])[l]
        pp[l, :, PP_MD:PP_MD + 4] = np.repeat(f(inp['m2_D'])[l], 64).reshape(4, 128).T
        pp[l, :, PP_MNW:PP_MNW + 4] = f(inp['m2_norm_w'])[l].reshape(4, 128).T
        pp[l, :, PP_HNW] = f(inp['hg_norm_w'])[l]
        pp[l, :, PP_SD:PP_SD + 4] = f(inp['s5_D'])[l].reshape(4, 128).T
        pp[l, :, PP_SAR:PP_SAR + 16] = f(inp['s5_A_re'])[l].reshape(16, 128).T
        pp[l, :, PP_SAI:PP_SAI + 16] = f(inp['s5_A_im'])[l].reshape(16, 128).T
        pp[l, :, PP_SLDT:PP_SLDT + 16] = np.repeat(f(inp['s5_log_dt'])[l], 64).reshape(16, 128).T
        pr[l, PR_A12:PR_A12 + 4] = f(inp['gdn_A_log'])[l]
        pr[l, PR_A12 + 4:PR_A12 + 12] = f(inp['m2_A_log'])[l]
        pr[l, PR_B12:PR_B12 + 4] = f(inp['gdn_dt_bias'])[l]
        pr[l, PR_B12 + 4:PR_B12 + 12] = f(inp['m2_dt_bias'])[l]
        pr[l, PR_LNG:PR_LNG + D] = f(inp['ln_g'])[l]
        pr[l, PR_LNB:PR_LNB + D] = f(inp['ln_b'])[l]
    lbl = np.ascontiguousarray(f(inp['hg_lb_logits']).reshape(L, 4, 128).transpose(2, 1, 0)).reshape(128, 4 * L)
    s5b = np.zeros((L, 2, 128, 16, 128), np.float32)
    s5c = np.zeros((L, 2, 128, 16, 32), np.float32)
    for j, (bn, cn) in enumerate((('s5_B_re', 's5_C_re'), ('s5_B_im', 's5_C_im'))):
        Bm = f(inp[bn])
        Cm = f(inp[cn])
        for pr_ in range(16):
            for g2 in range(2):
                gi = 2 * pr_ + g2
                col = (pr_ % 4) * 32 + g2 * 16
                s5b[:, j, g2 * 64:(g2 + 1) * 64, pr_, col:col + 16] = Bm[:, gi]
                s5c[:, j, g2 * 64:(g2 + 1) * 64, pr_, g2 * 16:(g2 + 1) * 16] = Cm[:, gi].transpose(0, 2, 1)
    lnin = np.stack([f(inp['ln_in_g']), f(inp['ln_in_b'])], axis=0)
    return dict(lnin=lnin, wfm=wfm, wsc=wsc, witm=witm, wglu=wglu, wbr=wbr, wout=wout, pp=pp, pr=pr, lbl=lbl,
                s5b=s5b.reshape(L, 2, 128, 2048), s5c=s5c.reshape(L, 2, 128, 512))


def make_xin(x_b, meta, Tp):
    T = x_b.shape[0] + meta.shape[0]
    xin = np.zeros((Tp, D), np.float32)
    xin[Tp - T:Tp - T + meta.shape[0]] = meta
    xin[Tp - x_b.shape[0]:] = x_b
    return xin


_NC_CACHE = {}


def kernel(**inputs):
    x = np.asarray(inputs['x'], dtype=np.float32)
    Bsz, SEQ, _ = x.shape
    depth = int(np.asarray(inputs['w_in']).shape[0])
    T = SEQ + NMETA
    ntiles = (T + TT - 1) // TT
    Tp = ntiles * TT
    ws = prep_weights(inputs, depth)
    meta = np.asarray(inputs['meta_tokens'], dtype=np.float32)
    key = (T, depth, SEQ)
    if key not in _NC_CACHE:
        _NC_CACHE[key] = build(T, depth, SEQ)
    nc = _NC_CACHE[key]
    in_maps = []
    for b in range(Bsz):
        m = dict(ws)
        m['xin'] = make_xin(x[b], meta, Tp)
        in_maps.append(m)
    res = run_bass_kernel_spmd(nc, in_maps, core_ids=list(range(Bsz)))
    return np.stack([np.asarray(res.results[b]['y'], dtype=np.float32) for b in range(Bsz)], axis=0)
```
